# Optimizing a Trainium2 kernel written in Bass

```python
import jax, jax.numpy as jnp
from jax import lax
import numpy as np

D_MODEL = 2048
BATCH = 2
SEQ = 16384
DEPTH = 4

N_MIXERS = 2
N_MLA_LAYERS = (DEPTH + 1) // 2
N_GDN_LAYERS = DEPTH // 2
NORM_EPS = 1e-6
PLE_DIM = 256
MLA_HEADS = 16
MLA_Q_LORA = 512
MLA_KV_LORA = 512
MLA_NOPE = 128
MLA_ROPE = 64
MLA_QK = MLA_NOPE + MLA_ROPE
MLA_V = 128
ROPE_THETA = 10000.0
ATTN_BLOCK = 128
MLA_IN_DIM = MLA_Q_LORA + MLA_KV_LORA + MLA_ROPE
GDN_QK_HEADS = 16
GDN_V_HEADS = 32
GDN_DK = 128
GDN_DV = 128
GDN_CONV = 4
GDN_CHUNK = 64
GDN_KEY_DIM = GDN_QK_HEADS * GDN_DK
GDN_VAL_DIM = GDN_V_HEADS * GDN_DV
GDN_CONV_DIM = 2 * GDN_KEY_DIM + GDN_VAL_DIM
GDN_IN_DIM = GDN_CONV_DIM + GDN_VAL_DIM + 2 * GDN_V_HEADS
D_FF = -((-8 * D_MODEL) // (3 * 256)) * 256

kernel_name = 'hybrid_mla_gdn_swiglu_ple_trunk'


def rms_norm(x, w):
    xf = x.astype(jnp.float32)
    y = xf * lax.rsqrt(jnp.mean(xf * xf, axis=-1, keepdims=True) + NORM_EPS)
    return (y * w.astype(jnp.float32)).astype(x.dtype)


def l2_norm(x):
    xf = x.astype(jnp.float32)
    return xf * lax.rsqrt(jnp.sum(xf * xf, axis=-1, keepdims=True) + NORM_EPS)


def rope_tables(positions, dtype):
    inv_freq = ROPE_THETA ** (-jnp.arange(0, MLA_ROPE, 2, dtype=jnp.float32) / MLA_ROPE)
    ang = positions.astype(jnp.float32)[..., None] * inv_freq
    return jnp.cos(ang)[:, :, None, :].astype(dtype), jnp.sin(ang)[:, :, None, :].astype(dtype)


def apply_rope(x, cos, sin):
    x1, x2 = jnp.split(x, 2, axis=-1)
    return jnp.concatenate([x1 * cos - x2 * sin, x2 * cos + x1 * sin], axis=-1)


def causal_block_attention(q, k, v, scale):
    B, S, H, Dqk = q.shape
    Dv = v.shape[-1]
    nb = S // ATTN_BLOCK
    q_blocks = q.reshape(B, nb, ATTN_BLOCK, H, Dqk).transpose(1, 0, 2, 3, 4)
    k_idx = jnp.arange(S)

    def one_block(args):
        qb, blk = args
        s = jnp.einsum('bqhd,bkhd->bhqk', qb, k).astype(jnp.float32) * scale
        q_idx = blk * ATTN_BLOCK + jnp.arange(ATTN_BLOCK)
        s = jnp.where(k_idx[None, :] <= q_idx[:, None], s, -jnp.inf)
        pr = jax.nn.softmax(s, axis=-1).astype(v.dtype)
        return jnp.einsum('bhqk,bkhd->bqhd', pr, v)

    o = lax.map(one_block, (q_blocks, jnp.arange(nb)))
    return o.transpose(1, 0, 2, 3, 4).reshape(B, S, H, Dv)


def mla_mixer(hn, cos, sin, w_in, q_lat_norm, kv_lat_norm, w_uq, w_ukv, q_norm, k_norm, w_o):
    B, S, _ = hn.shape
    c = hn @ w_in
    q_lat, kv_lat, k_pe = jnp.split(c, [MLA_Q_LORA, MLA_Q_LORA + MLA_KV_LORA], axis=-1)
    q = (rms_norm(q_lat, q_lat_norm) @ w_uq).reshape(B, S, MLA_HEADS, MLA_QK)
    kv = (rms_norm(kv_lat, kv_lat_norm) @ w_ukv).reshape(B, S, MLA_HEADS, MLA_NOPE + MLA_V)
    k_nope, v = jnp.split(kv, [MLA_NOPE], axis=-1)
    k_pe = jnp.broadcast_to(k_pe[:, :, None, :], (B, S, MLA_HEADS, MLA_ROPE))
    k = jnp.concatenate([k_nope, k_pe], axis=-1)
    q = rms_norm(q, q_norm)
    k = rms_norm(k, k_norm)
    q = jnp.concatenate([q[..., :MLA_NOPE], apply_rope(q[..., MLA_NOPE:], cos, sin)], axis=-1)
    k = jnp.concatenate([k[..., :MLA_NOPE], apply_rope(k[..., MLA_NOPE:], cos, sin)], axis=-1)
    o = causal_block_attention(q, k, v, MLA_QK ** -0.5)
    return o.reshape(B, S, MLA_HEADS * MLA_V) @ w_o


def causal_depthwise_conv(x, w):
    S = x.shape[1]
    xp = jnp.pad(x, ((0, 0), (GDN_CONV - 1, 0), (0, 0)))
    y = xp[:, 0:S] * w[0]
    for j in range(1, GDN_CONV):
        y = y + xp[:, j:j + S] * w[j]
    return y


def chunk_gated_delta_rule(q, k, v, g, beta):
    B, S, H, Dk = q.shape
    Dv = v.shape[-1]
    C = GDN_CHUNK
    N = S // C
    f32 = jnp.float32

    def to_chunks(t):
        return t.astype(f32).reshape(B, N, C, H, -1).transpose(0, 3, 1, 2, 4)

    q, k, v = to_chunks(q), to_chunks(k), to_chunks(v)
    g = g.astype(f32).reshape(B, N, C, H).transpose(0, 3, 1, 2)
    beta = beta.astype(f32).reshape(B, N, C, H).transpose(0, 3, 1, 2)
    g = jnp.cumsum(g, axis=-1)
    k_beta = k * beta[..., None]
    v_beta = v * beta[..., None]
    tril = jnp.tril(jnp.ones((C, C), dtype=bool))
    strict = jnp.tril(jnp.ones((C, C), dtype=bool), -1)
    diff = g[..., :, None] - g[..., None, :]
    decay = jnp.where(tril, jnp.exp(jnp.where(tril, diff, 0.0)), 0.0)
    L = jnp.where(strict, jnp.einsum('bhncd,bhnsd->bhncs', k_beta, k) * decay, 0.0)
    A = L + jnp.eye(C, dtype=f32)
    rhs = jnp.concatenate([v_beta, k_beta * jnp.exp(g)[..., None]], axis=-1)
    sol = lax.linalg.triangular_solve(A, rhs, left_side=True, lower=True, unit_diagonal=True)
    u, w = sol[..., :Dv], sol[..., Dv:]
    intra = jnp.where(tril, jnp.einsum('bhncd,bhnsd->bhncs', q, k) * decay, 0.0)

    def step(state, inp):
        q_c, k_c, u_c, w_c, g_c, a_c = inp
        v_new = u_c - jnp.einsum('bhcd,bhde->bhce', w_c, state)
        o = (jnp.einsum('bhcd,bhde->bhce', q_c * jnp.exp(g_c)[..., None], state)
             + jnp.einsum('bhcs,bhse->bhce', a_c, v_new))
        g_last = g_c[..., -1]
        k_dec = k_c * jnp.exp(g_last[..., None] - g_c)[..., None]
        state = state * jnp.exp(g_last)[..., None, None] + jnp.einsum('bhcd,bhce->bhde', k_dec, v_new)
        return state, o

    xs = tuple(jnp.moveaxis(t, 2, 0) for t in (q, k, u, w, g, intra))
    state0 = jnp.zeros((B, H, Dk, Dv), dtype=f32)
    _, o = lax.scan(step, state0, xs)
    return o.transpose(1, 0, 3, 2, 4).reshape(B, S, H, Dv)


def gdn_mixer(hn, w_in, conv_w, a_log, dt_bias, out_norm, w_out):
    B, S, _ = hn.shape
    proj = hn @ w_in
    qkv, z, b, a = jnp.split(
        proj, [GDN_CONV_DIM, GDN_CONV_DIM + GDN_VAL_DIM, GDN_CONV_DIM + GDN_VAL_DIM + GDN_V_HEADS], axis=-1)
    qkv = jax.nn.silu(causal_depthwise_conv(qkv, conv_w))
    q, k, v = jnp.split(qkv, [GDN_KEY_DIM, 2 * GDN_KEY_DIM], axis=-1)
    rep = GDN_V_HEADS // GDN_QK_HEADS
    q = jnp.repeat(l2_norm(q.reshape(B, S, GDN_QK_HEADS, GDN_DK)), rep, axis=2) * (GDN_DK ** -0.5)
    k = jnp.repeat(l2_norm(k.reshape(B, S, GDN_QK_HEADS, GDN_DK)), rep, axis=2)
    v = v.reshape(B, S, GDN_V_HEADS, GDN_DV)
    beta = jax.nn.sigmoid(b.astype(jnp.float32))
    g = -jnp.exp(a_log.astype(jnp.float32)) * jax.nn.softplus(a.astype(jnp.float32) + dt_bias.astype(jnp.float32))
    o = chunk_gated_delta_rule(q, k, v, g, beta).astype(hn.dtype)
    o = rms_norm(o, out_norm) * jax.nn.silu(z.reshape(B, S, GDN_V_HEADS, GDN_DV))
    return o.reshape(B, S, GDN_VAL_DIM) @ w_out


def swiglu_ffn(hn, w_gate_up, w_down):
    gate, up = jnp.split(hn @ w_gate_up, 2, axis=-1)
    return (jax.nn.silu(gate) * up) @ w_down


def per_layer_embedding(h, p_i, w_proj, emb_norm, gate_norm, w_gate):
    e = rms_norm(p_i @ w_proj, emb_norm)
    gate = jax.nn.sigmoid(rms_norm(h, gate_norm) @ w_gate)
    return e * gate


def setup_inputs(seed: int = 0) -> dict:
    key = jax.random.key(seed)
    ks = jax.random.split(key, 32)
    counter = [0]

    def nxt():
        kk = ks[counter[0]]
        counter[0] += 1
        return kk

    def dense(shape, fan_in):
        return jax.random.normal(nxt(), shape, jnp.float32) * (fan_in ** -0.5)

    def gain(shape):
        return 1.0 + 0.02 * jax.random.normal(nxt(), shape, jnp.float32)

    x = jax.random.normal(nxt(), (BATCH, SEQ, D_MODEL), jnp.float32)
    p = jax.random.normal(nxt(), (DEPTH, BATCH, SEQ, PLE_DIM), jnp.float32)
    offsets = jax.random.randint(nxt(), (BATCH, 1), 0, 4096, dtype=jnp.int32)
    positions = offsets + jnp.arange(SEQ, dtype=jnp.int32)[None, :]

    mixer_norm = gain((DEPTH, D_MODEL))
    mla_w_in = dense((N_MLA_LAYERS, D_MODEL, MLA_IN_DIM), D_MODEL)
    mla_q_lat_norm = gain((N_MLA_LAYERS, MLA_Q_LORA))
    mla_kv_lat_norm = gain((N_MLA_LAYERS, MLA_KV_LORA))
    mla_w_uq = dense((N_MLA_LAYERS, MLA_Q_LORA, MLA_HEADS * MLA_QK), MLA_Q_LORA)
    mla_w_ukv = dense((N_MLA_LAYERS, MLA_KV_LORA, MLA_HEADS * (MLA_NOPE + MLA_V)), MLA_KV_LORA)
    mla_q_norm = gain((N_MLA_LAYERS, MLA_QK))
    mla_k_norm = gain((N_MLA_LAYERS, MLA_QK))
    mla_w_o = dense((N_MLA_LAYERS, MLA_HEADS * MLA_V, D_MODEL), MLA_HEADS * MLA_V)

    gdn_w_in = dense((N_GDN_LAYERS, D_MODEL, GDN_IN_DIM), D_MODEL)
    gdn_conv_w = dense((N_GDN_LAYERS, GDN_CONV, GDN_CONV_DIM), GDN_CONV)
    gdn_a_log = jnp.log(jax.random.uniform(nxt(), (N_GDN_LAYERS, GDN_V_HEADS), jnp.float32, 1.0, 16.0))
    dt = jnp.exp(jax.random.uniform(nxt(), (N_GDN_LAYERS, GDN_V_HEADS), jnp.float32,
                                    float(np.log(1e-3)), float(np.log(1e-1))))
    gdn_dt_bias = dt + jnp.log(-jnp.expm1(-dt))
    gdn_out_norm = gain((N_GDN_LAYERS, GDN_DV))
    gdn_w_out = dense((N_GDN_LAYERS, GDN_VAL_DIM, D_MODEL), GDN_VAL_DIM)

    ffn_norm = gain((DEPTH, D_MODEL))
    ffn_w_gate_up = dense((DEPTH, D_MODEL, 2 * D_FF), D_MODEL)
    ffn_w_down = dense((DEPTH, D_FF, D_MODEL), D_FF)

    ple_w_proj = dense((DEPTH, PLE_DIM, D_MODEL), PLE_DIM)
    ple_norm = gain((DEPTH, D_MODEL))
    ple_gate_norm = gain((DEPTH, D_MODEL))
    ple_w_gate = dense((DEPTH, D_MODEL, D_MODEL), D_MODEL)

    return {'x': x, 'p': p, 'positions': positions, 'mixer_norm': mixer_norm,
            'mla_w_in': mla_w_in, 'mla_q_lat_norm': mla_q_lat_norm, 'mla_kv_lat_norm': mla_kv_lat_norm,
            'mla_w_uq': mla_w_uq, 'mla_w_ukv': mla_w_ukv, 'mla_q_norm': mla_q_norm, 'mla_k_norm': mla_k_norm,
            'mla_w_o': mla_w_o,
            'gdn_w_in': gdn_w_in, 'gdn_conv_w': gdn_conv_w, 'gdn_a_log': gdn_a_log, 'gdn_dt_bias': gdn_dt_bias,
            'gdn_out_norm': gdn_out_norm, 'gdn_w_out': gdn_w_out,
            'ffn_norm': ffn_norm, 'ffn_w_gate_up': ffn_w_gate_up, 'ffn_w_down': ffn_w_down,
            'ple_w_proj': ple_w_proj, 'ple_norm': ple_norm, 'ple_gate_norm': ple_gate_norm, 'ple_w_gate': ple_w_gate}


def reference(x, p, positions, mixer_norm,
              mla_w_in, mla_q_lat_norm, mla_kv_lat_norm, mla_w_uq, mla_w_ukv, mla_q_norm, mla_k_norm, mla_w_o,
              gdn_w_in, gdn_conv_w, gdn_a_log, gdn_dt_bias, gdn_out_norm, gdn_w_out,
              ffn_norm, ffn_w_gate_up, ffn_w_down,
              ple_w_proj, ple_norm, ple_gate_norm, ple_w_gate):
    cos, sin = rope_tables(positions, x.dtype)
    h = x
    for i in range(DEPTH):
        j = i // N_MIXERS
        hn = rms_norm(h, mixer_norm[i])
        if i % N_MIXERS == 0:
            h = h + mla_mixer(hn, cos, sin, mla_w_in[j], mla_q_lat_norm[j], mla_kv_lat_norm[j],
                              mla_w_uq[j], mla_w_ukv[j], mla_q_norm[j], mla_k_norm[j], mla_w_o[j])
        else:
            h = h + gdn_mixer(hn, gdn_w_in[j], gdn_conv_w[j], gdn_a_log[j], gdn_dt_bias[j],
                              gdn_out_norm[j], gdn_w_out[j])
        h = h + swiglu_ffn(rms_norm(h, ffn_norm[i]), ffn_w_gate_up[i], ffn_w_down[i])
        h = h + per_layer_embedding(h, p[i], ple_w_proj[i], ple_norm[i], ple_gate_norm[i], ple_w_gate[i])
    return h
```

```python
import math
from contextlib import ExitStack

import numpy as np
import concourse.bass as bass
import concourse.mybir as mybir
from concourse.bass_utils import run_bass_kernel_spmd

F32 = mybir.dt.float32
I32 = mybir.dt.int32
AF = mybir.ActivationFunctionType
ALU = mybir.AluOpType

D_MODEL = 2048
DEPTH = 4
NORM_EPS = 1e-6
PLE_DIM = 256
MLA_HEADS = 16
MLA_Q_LORA = 512
MLA_KV_LORA = 512
MLA_NOPE = 128
MLA_ROPE = 64
MLA_QK = 192
MLA_V = 128
ROPE_THETA = 10000.0
GDN_QK_HEADS = 16
GDN_V_HEADS = 32
GDN_DK = 128
GDN_DV = 128
GDN_CONV = 4
GDN_KEY_DIM = 2048
GDN_VAL_DIM = 4096
GDN_CONV_DIM = 8192
D_FF = 5632
NCORES = 8
TWO_PI_HI = 6.28125
TWO_PI_LO = 2.0 * math.pi - 6.28125
PI_SAFE = 3.1415925

SAME_ENGINE_SYNC = True
N_DMA_SEMS = 24


class Prog:
    ENGS = ("pe", "act", "dve", "pool", "sp")

    def __init__(self):
        self.nc = bass.Bass("TRN2", target_bir_lowering=False)
        self.stack = ExitStack()
        self.ops = {e: [] for e in self.ENGS}
        self.last_w = {}
        self.readers = {}
        self.dma_uses = [0] * N_DMA_SEMS
        self.dma_rr = 0
        self.out_dmas = []
        self.n_ops = 0

    def dram_in(self, name, shape, dtype=F32):
        return self.nc.dram_tensor(name, list(shape), dtype, kind="ExternalInput").ap()

    def dram_out(self, name, shape, dtype=F32):
        return self.nc.dram_tensor(name, list(shape), dtype, kind="ExternalOutput").ap()

    def sbuf(self, name, shape, dtype=F32):
        return self.stack.enter_context(self.nc.sbuf_tensor(name, list(shape), dtype))

    def psum(self, name, shape, dtype=F32):
        return self.stack.enter_context(self.nc.psum_tensor(name, list(shape), dtype))

    def _deps(self, eng, r, w):
        deps = set()
        for k in r:
            if k in self.last_w:
                deps.add(self.last_w[k])
        for k in w:
            if k in self.last_w:
                deps.add(self.last_w[k])
            for d in self.readers.get(k, ()):
                deps.add(d)
        out = []
        for d in deps:
            if d[0] == "eng" and d[1] == eng:
                if eng in ("pe", "sp") or not SAME_ENGINE_SYNC:
                    continue
            out.append(d)
        return out

    def _mark(self, tok, r, w):
        for k in w:
            self.last_w[k] = tok
            self.readers[k] = []
        for k in r:
            self.readers.setdefault(k, []).append(tok)

    def op(self, eng, fn, r=(), w=()):
        deps = self._deps(eng, r, w)
        idx = len(self.ops[eng])
        for d in deps:
            if d[0] == "eng":
                self.ops[d[1]][d[2]]["inc"] = True
        self.ops[eng].append({"kind": "c", "fn": fn, "deps": deps, "inc": False})
        self._mark(("eng", eng, idx), r, w)
        self.n_ops += 1

    def dma(self, out, in_, r=(), w=(), eng="sp", is_output=False):
        deps = self._deps(eng, r, w)
        for d in deps:
            if d[0] == "eng":
                self.ops[d[1]][d[2]]["inc"] = True
        k = self.dma_rr
        self.dma_rr = (self.dma_rr + 1) % N_DMA_SEMS
        use = self.dma_uses[k]
        self.dma_uses[k] += 1
        if use > 0:
            deps.append(("dma", k, 16 * use))
        tok = ("dma", k, 16 * (use + 1))
        self.ops[eng].append({"kind": "d", "out": out, "in": in_, "deps": deps, "sem": k, "inc": False})
        self._mark(tok, r, w)
        if is_output:
            self.out_dmas.append(tok)
        self.n_ops += 1

    def mm(self, out, lhsT, rhs, start, stop, r=(), w=()):
        self.op("pe", lambda e: e.matmul(out, lhsT, rhs, start=start, stop=stop), r=r, w=w)

    def emit(self):
        nc = self.nc
        st = self.stack
        esem = {e: st.enter_context(nc.semaphore("s_" + e)) for e in self.ENGS}
        dsem = [st.enter_context(nc.semaphore("d%d" % i)) for i in range(N_DMA_SEMS)]
        vals = {}
        for e in self.ENGS:
            c = 0
            for i, o in enumerate(self.ops[e]):
                if o["kind"] == "c" and o["inc"]:
                    c += 1
                    vals[(e, i)] = c
        final = list(self.out_dmas)
        ops = self.ops

        def run(ename, eng):
            waited = {}

            def do_wait(d):
                if d[0] == "eng":
                    key, v, sem = ("e", d[1]), vals[(d[1], d[2])], esem[d[1]]
                else:
                    key, v, sem = ("d", d[1]), d[2], dsem[d[1]]
                if waited.get(key, 0) >= v:
                    return
                waited[key] = v
                eng.wait_ge(sem, v)

            for i, o in enumerate(ops[ename]):
                for d in o["deps"]:
                    do_wait(d)
                if o["kind"] == "c":
                    ins = o["fn"](eng)
                    if o["inc"]:
                        ins.then_inc(esem[ename], 1)
                else:
                    eng.dma_start(out=o["out"], in_=o["in"]).then_inc(dsem[o["sem"]], 16)
            if ename == "sp":
                for d in final:
                    do_wait(d)

        with nc.Block() as block:
            @block.sync
            def _(sync):
                run("sp", sync)

            @block.tensor
            def _(pe):
                run("pe", pe)

            @block.scalar
            def _(act):
                run("act", act)

            @block.vector
            def _(dve):
                run("dve", dve)

            @block.gpsimd
            def _(pool):
                run("pool", pool)
        st.close()
        return nc


class RR:
    def __init__(self, items):
        self.items = list(items)
        self.i = 0

    def __call__(self):
        v = self.items[self.i % len(self.items)]
        self.i += 1
        return v


def tile_w(W, mc=128):
    K, M = W.shape
    return np.ascontiguousarray(W.reshape(K // 128, 128, M // mc, mc).transpose(2, 1, 0, 3))


def col_vec(v):
    return np.ascontiguousarray(v.reshape(-1, 128).T)


def rsqrt_from(P, out, out_key, src, src_key, D, eps, post_scale=1.0):
    s2 = post_scale * post_scale
    P.op("act", lambda e: e.activation(out, src, AF.Sqrt, bias=eps / s2, scale=1.0 / (D * s2)), r=[src_key], w=[out_key])
    P.op("dve", lambda e: e.reciprocal(out, out), r=[out_key], w=[out_key])


class NormHelper:
    def __init__(self, P, T):
        self.P = P
        self.T = T
        self.ones = P.sbuf("ones", [128, 128])
        P.op("pool", lambda e: e.memset(self.ones[:], 1.0), w=["ones"])
        self.sq = [P.sbuf("nsq%d" % i, [128, T]) for i in range(2)]
        self.ps = P.psum("nps", [128, T])
        self.cnt = 0

    def rstd(self, chunks, D, out, out_key, eps=NORM_EPS, post_scale=1.0):
        P = self.P
        n = len(chunks)
        for i, (ap, key, kp) in enumerate(chunks):
            s = self.cnt % 2
            self.cnt += 1
            sq = self.sq[s]
            sk = "nsq%d" % s
            eng = "act" if i % 2 == 0 else "pool"
            if eng == "act":
                P.op("act", lambda e, sq=sq, ap=ap, kp=kp: e.activation(sq[0:kp, :], ap, AF.Square), r=[key], w=[sk])
            else:
                P.op("pool", lambda e, sq=sq, ap=ap, kp=kp: e.tensor_tensor(sq[0:kp, :], ap, ap, ALU.mult), r=[key], w=[sk])
            P.mm(self.ps[:], self.ones[0:kp, :], sq[0:kp, :], start=(i == 0), stop=(i == n - 1), r=[sk, "ones"], w=["nps"])
        rsqrt_from(P, out, out_key, self.ps[:], "nps", D, eps, post_scale)


def build_post(NT, DM, gdn):
    T = 512 if NT >= 512 else NT
    NTILES = NT // T
    KC = D_MODEL // 128
    MC = DM // 128
    FC = D_FF // 128
    FH = 11
    NR = FC // FH
    P = Prog()
    hT = P.dram_in("hT", [D_MODEL, NT])
    oT = P.dram_in("oT", [DM, NT])
    w_o = P.dram_in("w_o", [KC, 128, MC, 128])
    w_gu = P.dram_in("w_gu", [2 * FC, 128, KC, 128])
    w_dn = P.dram_in("w_dn", [KC, 128, FC, 128])
    w_pp = P.dram_in("w_pp", [KC, 128, 2, 128])
    w_pg = P.dram_in("w_pg", [KC, 128, KC, 128])
    vecs = P.dram_in("vecs", [128, 3 * KC])
    pT = P.dram_in("pT", [PLE_DIM, NT])
    if gdn:
        zT = P.dram_in("zT", [DM, NT])
        onorm = P.dram_in("onorm", [128, 1])
    hout = P.dram_out("hT_out", [D_MODEL, NT])

    h = P.sbuf("h", [128, KC, T])
    hn = P.sbuf("hn", [128, KC, T])
    act = P.sbuf("actb", [128, FH, T])
    wg_b = [P.sbuf("wg%d" % i, [128, KC, 128]) for i in range(2)]
    wu_b = [P.sbuf("wu%d" % i, [128, KC, 128]) for i in range(2)]
    wd_b = [P.sbuf("wd%d" % i, [128, FH, 128]) for i in range(2)]
    wpp_b = [P.sbuf("wpp%d" % i, [128, 2, 128]) for i in range(2)]
    vec = P.sbuf("vec", [128, 3 * KC])
    rs = P.sbuf("rs", [128, T])
    rs2 = P.sbuf("rs2", [128, T])
    sg = [P.sbuf("sg%d" % i, [128, T]) for i in range(2)]
    ptile = P.sbuf("ptile", [128, 2, T])
    tmp = [P.sbuf("tmp%d" % i, [128, T]) for i in range(2)]
    if gdn:
        zb = [P.sbuf("zb%d" % i, [128, T]) for i in range(2)]
        on_sb = P.sbuf("on_sb", [128, 1])
    pA = [P.psum("pA%d" % i, [128, T]) for i in range(2)]
    pB = [P.psum("pB%d" % i, [128, T]) for i in range(2)]
    NH = NormHelper(P, T)

    P.dma(vec[:], vecs[:, :], w=["vec"])
    if gdn:
        P.dma(on_sb[:], onorm[:, :], w=["on_sb"])
    cnt = {"wo": 0, "wg": 0, "wd": 0, "ob": 0, "pa": 0, "pb": 0, "sg": 0, "wpp": 0, "tmp": 0, "zb": 0}

    def nxt(k, n):
        v = cnt[k] % n
        cnt[k] += 1
        return v

    for t in range(NTILES):
        ts = slice(t * T, (t + 1) * T)
        P.dma(h[:], hT[:, ts].rearrange("(c p) n -> p c n", p=128), w=["h"])
        for half in range(MC // KC):
            P.dma(hn[:], oT[half * D_MODEL:(half + 1) * D_MODEL, ts].rearrange("(c p) n -> p c n", p=128), w=["hn"])
            if gdn:
                for c in range(KC):
                    zs = nxt("zb", 2)
                    gc_ = half * KC + c
                    P.dma(zb[zs][:], zT[gc_ * 128:(gc_ + 1) * 128, ts], w=["zb%d" % zs])
                    NH.rstd([(hn[:, c, :], "hn", 128)], 128, rs[:], "rs")
                    P.op("act", lambda e, zs=zs: e.activation(zb[zs][:], zb[zs][:], AF.Silu), r=["zb%d" % zs], w=["zb%d" % zs])
                    P.op("dve", lambda e, c=c: e.scalar_tensor_tensor(hn[:, c, :], hn[:, c, :], on_sb[:, 0:1], rs[:], ALU.mult, ALU.mult),
                         r=["hn", "on_sb", "rs"], w=["hn"])
                    P.op("pool", lambda e, c=c, zs=zs: e.tensor_tensor(hn[:, c, :], hn[:, c, :], zb[zs][:], ALU.mult),
                         r=["hn", "zb%d" % zs], w=["hn"])
            for d in range(KC):
                s = nxt("wg", 2)
                P.dma(wg_b[s][:], w_o[d][:, half * KC:(half + 1) * KC, :], w=["wg%d" % s])
                pa = nxt("pa", 2)
                for c in range(KC):
                    P.mm(pA[pa][:], wg_b[s][:, c, :], hn[:, c, :], start=(c == 0), stop=(c == KC - 1),
                         r=["wg%d" % s, "hn"], w=["pA%d" % pa])
                P.op("dve", lambda e, d=d, pa=pa: e.tensor_tensor(h[:, d, :], h[:, d, :], pA[pa][:], ALU.add),
                     r=["h", "pA%d" % pa], w=["h"])
        NH.rstd([(h[:, c, :], "h", 128) for c in range(KC)], D_MODEL, rs[:], "rs")
        for c in range(KC):
            P.op("dve", lambda e, c=c: e.scalar_tensor_tensor(hn[:, c, :], h[:, c, :], vec[:, c:c + 1], rs[:], ALU.mult, ALU.mult),
                 r=["h", "vec", "rs"], w=["hn"])
        for rd in range(NR):
            for fi in range(FH):
                f = rd * FH + fi
                s = nxt("wg", 2)
                P.dma(wg_b[s][:], w_gu[f], w=["wg%d" % s])
                P.dma(wu_b[s][:], w_gu[FC + f], w=["wu%d" % s])
                pa = nxt("pa", 2)
                pb = nxt("pb", 2)
                for c in range(KC):
                    P.mm(pA[pa][:], wg_b[s][:, c, :], hn[:, c, :], start=(c == 0), stop=(c == KC - 1),
                         r=["wg%d" % s, "hn"], w=["pA%d" % pa])
                for c in range(KC):
                    P.mm(pB[pb][:], wu_b[s][:, c, :], hn[:, c, :], start=(c == 0), stop=(c == KC - 1),
                         r=["wu%d" % s, "hn"], w=["pB%d" % pb])
                g = nxt("sg", 2)
                P.op("act", lambda e, g=g, pa=pa: e.activation(sg[g][:], pA[pa][:], AF.Silu), r=["pA%d" % pa], w=["sg%d" % g])
                P.op("dve", lambda e, g=g, pb=pb, fi=fi: e.tensor_tensor(act[:, fi, :], sg[g][:], pB[pb][:], ALU.mult),
                     r=["sg%d" % g, "pB%d" % pb], w=["actb"])
            for d in range(KC):
                s = nxt("wd", 2)
                P.dma(wd_b[s][:], w_dn[d][:, rd * FH:(rd + 1) * FH, :], w=["wd%d" % s])
                pa = nxt("pa", 2)
                for fi in range(FH):
                    P.mm(pA[pa][:], wd_b[s][:, fi, :], act[:, fi, :], start=(fi == 0), stop=(fi == FH - 1),
                         r=["wd%d" % s, "actb"], w=["pA%d" % pa])
                P.op("dve", lambda e, d=d, pa=pa: e.tensor_tensor(h[:, d, :], h[:, d, :], pA[pa][:], ALU.add),
                     r=["h", "pA%d" % pa], w=["h"])
        P.dma(ptile[:], pT[:, ts].rearrange("(c p) n -> p c n", p=128), w=["ptile"])
        for d in range(KC):
            s = nxt("wpp", 2)
            P.dma(wpp_b[s][:], w_pp[d], w=["wpp%d" % s])
            pa = nxt("pa", 2)
            for c in range(2):
                P.mm(pA[pa][:], wpp_b[s][:, c, :], ptile[:, c, :], start=(c == 0), stop=(c == 1),
                     r=["wpp%d" % s, "ptile"], w=["pA%d" % pa])
            g = nxt("sg", 2)
            P.op("act", lambda e, g=g, pa=pa: e.activation(sg[g][:], pA[pa][:], AF.Square), r=["pA%d" % pa], w=["sg%d" % g])
            P.mm(NH.ps[:], NH.ones[:], sg[g][:], start=(d == 0), stop=(d == KC - 1), r=["sg%d" % g, "ones"], w=["nps"])
        rsqrt_from(P, rs2[:], "rs2", NH.ps[:], "nps", D_MODEL, NORM_EPS, 1.0)
        NH.rstd([(h[:, c, :], "h", 128) for c in range(KC)], D_MODEL, rs[:], "rs")
        for c in range(KC):
            P.op("dve", lambda e, c=c: e.scalar_tensor_tensor(hn[:, c, :], h[:, c, :], vec[:, 2 * KC + c:2 * KC + c + 1], rs[:], ALU.mult, ALU.mult),
                 r=["h", "vec", "rs"], w=["hn"])
        for d in range(KC):
            s = nxt("wg", 2)
            P.dma(wg_b[s][:], w_pg[d], w=["wg%d" % s])
            s2 = nxt("wpp", 2)
            P.dma(wpp_b[s2][:], w_pp[d], w=["wpp%d" % s2])
            pa = nxt("pa", 2)
            pb = nxt("pb", 2)
            for c in range(KC):
                P.mm(pA[pa][:], wg_b[s][:, c, :], hn[:, c, :], start=(c == 0), stop=(c == KC - 1),
                     r=["wg%d" % s, "hn"], w=["pA%d" % pa])
            for c in range(2):
                P.mm(pB[pb][:], wpp_b[s2][:, c, :], ptile[:, c, :], start=(c == 0), stop=(c == 1),
                     r=["wpp%d" % s2, "ptile"], w=["pB%d" % pb])
            g = nxt("sg", 2)
            P.op("act", lambda e, g=g, pa=pa: e.activation(sg[g][:], pA[pa][:], AF.Sigmoid), r=["pA%d" % pa], w=["sg%d" % g])
            tq = nxt("tmp", 2)
            P.op("dve", lambda e, tq=tq, pb=pb, d=d: e.scalar_tensor_tensor(tmp[tq][:], pB[pb][:], vec[:, KC + d:KC + d + 1], rs2[:], ALU.mult, ALU.mult),
                 r=["pB%d" % pb, "vec", "rs2"], w=["tmp%d" % tq])
            P.op("pool", lambda e, tq=tq, g=g: e.tensor_tensor(tmp[tq][:], tmp[tq][:], sg[g][:], ALU.mult),
                 r=["tmp%d" % tq, "sg%d" % g], w=["tmp%d" % tq])
            P.op("dve", lambda e, tq=tq, d=d: e.tensor_tensor(h[:, d, :], h[:, d, :], tmp[tq][:], ALU.add),
                 r=["h", "tmp%d" % tq], w=["h"])
        P.dma(hout[:, ts].rearrange("(c p) n -> p c n", p=128), h[:], r=["h"], is_output=True)
    return P.emit()


def rope_consts():
    inv = ROPE_THETA ** (-np.arange(0, MLA_ROPE, 2, dtype=np.float32) / np.float32(MLA_ROPE))
    inv = inv.astype(np.float32)
    invf = np.concatenate([inv, inv]).reshape(64, 1).astype(np.float32)
    rot = np.zeros((64, 64), np.float32)
    for m in range(32):
        rot[m + 32, m] = -1.0
        rot[m, m + 32] = 1.0
    return invf, rot


def build_mla_pre(NT):
    T = 512 if NT >= 512 else NT
    NTILES = NT // T
    KC = 16
    H = MLA_HEADS
    P = Prog()
    hT = P.dram_in("hT", [D_MODEL, NT])
    posb = P.dram_in("posb", [64, NT], I32)
    w_in = P.dram_in("w_in", [8, 128, KC, 128])
    w_inpe = P.dram_in("w_inpe", [1, 128, KC, 64])
    w_qn = P.dram_in("w_qn", [H, 128, 4, 128])
    w_qr = P.dram_in("w_qr", [H, 128, 4, 64])
    w_kn = P.dram_in("w_kn", [H, 128, 4, 128])
    w_v = P.dram_in("w_v", [H, 128, 4, 128])
    vecs = P.dram_in("vecs", [128, KC + 4 + 4 + 4])
    invf_d = P.dram_in("invf", [64, 1])
    rot_d = P.dram_in("rot", [64, 64])
    q_nope_o = P.dram_out("q_nope", [H, 128, NT])
    q_rope_o = P.dram_out("q_rope", [H, 64, NT])
    k_nope_o = P.dram_out("k_nope", [H, 128, NT])
    k_rope_o = P.dram_out("k_rope", [H, 64, NT])
    v_o = P.dram_out("vT", [H, 128, NT])

    h = P.sbuf("h", [128, KC, T])
    clat = P.sbuf("clat", [128, 8, T])
    kpe = P.sbuf("kpe", [64, T])
    vec = P.sbuf("vec", [128, KC + 12])
    invf = P.sbuf("invf_s", [64, 1])
    rot = P.sbuf("rot_s", [64, 64])
    posi = P.sbuf("posi", [64, T], I32)
    posf = P.sbuf("posf", [64, T])
    u0 = P.sbuf("u0", [64, T])
    u1 = P.sbuf("u1", [64, T])
    sin_t = P.sbuf("sin_t", [64, T])
    cos_t = P.sbuf("cos_t", [64, T])
    rs = P.sbuf("rs", [128, T])
    wb = [P.sbuf("wb%d" % i, [128, KC, 128]) for i in range(2)]
    wpe = P.sbuf("wpe", [128, KC, 64])
    wh = {k: [P.sbuf("w%s%d" % (k, i), [128, 4, 128 if k != "qr" else 64]) for i in range(2)] for k in ("qn", "qr", "kn", "v")}
    xn = [P.sbuf("xn%d" % i, [128, T]) for i in range(2)]
    xr = [P.sbuf("xr%d" % i, [64, T]) for i in range(2)]
    on = [P.sbuf("on%d" % i, [128, T]) for i in range(3)]
    orr = [P.sbuf("or%d" % i, [64, T]) for i in range(2)]
    t64 = [P.sbuf("t64_%d" % i, [64, T]) for i in range(2)]
    pA = [P.psum("pA%d" % i, [128, T]) for i in range(2)]
    pB = [P.psum("pB%d" % i, [128, T]) for i in range(2)]
    pR = P.psum("pR", [64, T])
    NH = NormHelper(P, T)
    cnt = {}

    def nxt(k, n):
        v = cnt.get(k, 0) % n
        cnt[k] = cnt.get(k, 0) + 1
        return v

    P.dma(vec[:], vecs[:, :], w=["vec"])
    P.dma(invf[:], invf_d[:, :], w=["invf"])
    P.dma(rot[:], rot_d[:, :], w=["rot"])
    C_QN, C_KN, C_QR, C_KR = KC + 8, KC + 9, KC + 10, KC + 11

    def rope(src, src_key, dst, dst_key):
        P.mm(pR[:], rot[:], src, True, True, r=[src_key, "rot"], w=["pR"])
        tq = nxt("t64", 2)
        P.op("dve", lambda e: e.tensor_tensor(t64[tq][:], pR[:], sin_t[:], ALU.mult), r=["pR", "sin_t"], w=["t64_%d" % tq])
        P.op("pool", lambda e: e.tensor_tensor(dst, src, cos_t[:], ALU.mult), r=[src_key, "cos_t"], w=[dst_key])
        P.op("pool", lambda e: e.tensor_tensor(dst, dst, t64[tq][:], ALU.add), r=[dst_key, "t64_%d" % tq], w=[dst_key])

    for t in range(NTILES):
        ts = slice(t * T, (t + 1) * T)
        P.dma(h[:], hT[:, ts].rearrange("(c p) n -> p c n", p=128), w=["h"])
        P.dma(posi[:], posb[:, ts], w=["posi"])
        P.op("dve", lambda e: e.tensor_copy(posf[:], posi[:]), r=["posi"], w=["posf"])
        P.op("dve", lambda e: e.tensor_scalar(posf[:], posf[:], invf[:, 0:1], None, ALU.mult), r=["posf", "invf"], w=["posf"])
        P.op("dve", lambda e: e.tensor_scalar(u1[:], posf[:], 1.0 / (2.0 * math.pi), None, ALU.mult), r=["posf"], w=["u1"])
        P.op("dve", lambda e: e.tensor_copy(posi[:], u1[:]), r=["u1"], w=["posi"])
        P.op("dve", lambda e: e.tensor_copy(u1[:], posi[:]), r=["posi"], w=["u1"])
        P.op("dve", lambda e: e.scalar_tensor_tensor(u0[:], u1[:], -TWO_PI_HI, posf[:], ALU.mult, ALU.add), r=["u1", "posf"], w=["u0"])
        P.op("dve", lambda e: e.scalar_tensor_tensor(u0[:], u1[:], -TWO_PI_LO, u0[:], ALU.mult, ALU.add), r=["u1", "u0"], w=["u0"])

        def wrap(buf, key):
            P.op("dve", lambda e: e.tensor_scalar(u1[:], buf, math.pi, -2.0 * math.pi, ALU.is_ge, ALU.mult), r=[key], w=["u1"])
            P.op("dve", lambda e: e.tensor_tensor(buf, buf, u1[:], ALU.add), r=[key, "u1"], w=[key])
            P.op("dve", lambda e: e.tensor_scalar(u1[:], buf, -math.pi, 2.0 * math.pi, ALU.is_lt, ALU.mult), r=[key], w=["u1"])
            P.op("dve", lambda e: e.tensor_tensor(buf, buf, u1[:], ALU.add), r=[key, "u1"], w=[key])
            P.op("dve", lambda e: e.tensor_scalar(buf, buf, PI_SAFE, -PI_SAFE, ALU.min, ALU.max), r=[key], w=[key])

        wrap(u0[:], "u0")
        P.op("act", lambda e: e.activation(sin_t[:], u0[:], AF.Sin), r=["u0"], w=["sin_t"])
        P.op("dve", lambda e: e.tensor_scalar(u0[:], u0[:], 0.5 * math.pi, None, ALU.add), r=["u0"], w=["u0"])
        wrap(u0[:], "u0")
        P.op("act", lambda e: e.activation(cos_t[:], u0[:], AF.Sin), r=["u0"], w=["cos_t"])
        NH.rstd([(h[:, c, :], "h", 128) for c in range(KC)], D_MODEL, rs[:], "rs")
        for c in range(KC):
            P.op("dve", lambda e, c=c: e.scalar_tensor_tensor(h[:, c, :], h[:, c, :], vec[:, c:c + 1], rs[:], ALU.mult, ALU.mult),
                 r=["h", "vec", "rs"], w=["h"])
        for m in range(8):
            s = nxt("wb", 2)
            P.dma(wb[s][:], w_in[m], w=["wb%d" % s])
            pa = nxt("pa", 2)
            for c in range(KC):
                P.mm(pA[pa][:], wb[s][:, c, :], h[:, c, :], c == 0, c == KC - 1, r=["wb%d" % s, "h"], w=["pA%d" % pa])
            eng = "act" if m % 2 == 0 else "dve"
            if eng == "act":
                P.op("act", lambda e, m=m, pa=pa: e.copy(clat[:, m, :], pA[pa][:]), r=["pA%d" % pa], w=["clat"])
            else:
                P.op("dve", lambda e, m=m, pa=pa: e.tensor_copy(clat[:, m, :], pA[pa][:]), r=["pA%d" % pa], w=["clat"])
        P.dma(wpe[:], w_inpe[0], w=["wpe"])
        pb = nxt("pb", 2)
        for c in range(KC):
            P.mm(pB[pb][0:64, :], wpe[:, c, :], h[:, c, :], c == 0, c == KC - 1, r=["wpe", "h"], w=["pB%d" % pb])
        P.op("act", lambda e, pb=pb: e.copy(kpe[:], pB[pb][0:64, :]), r=["pB%d" % pb], w=["kpe"])
        for base, col in ((0, KC), (4, KC + 4)):
            NH.rstd([(clat[:, base + c, :], "clat", 128) for c in range(4)], 512, rs[:], "rs")
            for c in range(4):
                P.op("dve", lambda e, c=c, base=base, col=col: e.scalar_tensor_tensor(
                    clat[:, base + c, :], clat[:, base + c, :], vec[:, col + c:col + c + 1], rs[:], ALU.mult, ALU.mult),
                    r=["clat", "vec", "rs"], w=["clat"])
        for hd in range(H):
            s = nxt("wh", 2)
            for k, wd in (("qn", w_qn), ("qr", w_qr), ("kn", w_kn), ("v", w_v)):
                P.dma(wh[k][s][:], wd[hd], w=["w%s%d" % (k, s)])
            pa = nxt("pa", 2)
            pb = nxt("pb", 2)
            for c in range(4):
                P.mm(pA[pa][:], wh["qn"][s][:, c, :], clat[:, c, :], c == 0, c == 3, r=["wqn%d" % s, "clat"], w=["pA%d" % pa])
            for c in range(4):
                P.mm(pB[pb][0:64, :], wh["qr"][s][:, c, :], clat[:, c, :], c == 0, c == 3, r=["wqr%d" % s, "clat"], w=["pB%d" % pb])
            a = nxt("xn", 2)
            b = nxt("xr", 2)
            P.op("act", lambda e, a=a, pa=pa: e.copy(xn[a][:], pA[pa][:]), r=["pA%d" % pa], w=["xn%d" % a])
            P.op("dve", lambda e, b=b, pb=pb: e.tensor_copy(xr[b][:], pB[pb][0:64, :]), r=["pB%d" % pb], w=["xr%d" % b])
            NH.rstd([(xn[a][:], "xn%d" % a, 128), (xr[b][:], "xr%d" % b, 64)], MLA_QK, rs[:], "rs")
            o = nxt("on", 3)
            P.op("dve", lambda e, a=a, o=o: e.scalar_tensor_tensor(on[o][:], xn[a][:], vec[:, C_QN:C_QN + 1], rs[:], ALU.mult, ALU.mult),
                 r=["xn%d" % a, "vec", "rs"], w=["on%d" % o])
            P.dma(q_nope_o[hd][:, ts], on[o][:], r=["on%d" % o], is_output=True)
            P.op("dve", lambda e, b=b: e.scalar_tensor_tensor(xr[b][:], xr[b][:], vec[0:64, C_QR:C_QR + 1], rs[0:64, :], ALU.mult, ALU.mult),
                 r=["xr%d" % b, "vec", "rs"], w=["xr%d" % b])
            ro = nxt("or", 2)
            rope(xr[b][:], "xr%d" % b, orr[ro][:], "or%d" % ro)
            P.dma(q_rope_o[hd][:, ts], orr[ro][:], r=["or%d" % ro], is_output=True)
            pa = nxt("pa", 2)
            for c in range(4):
                P.mm(pA[pa][:], wh["kn"][s][:, c, :], clat[:, 4 + c, :], c == 0, c == 3, r=["wkn%d" % s, "clat"], w=["pA%d" % pa])
            a = nxt("xn", 2)
            P.op("act", lambda e, a=a, pa=pa: e.copy(xn[a][:], pA[pa][:]), r=["pA%d" % pa], w=["xn%d" % a])
            NH.rstd([(xn[a][:], "xn%d" % a, 128), (kpe[:], "kpe", 64)], MLA_QK, rs[:], "rs")
            o = nxt("on", 3)
            P.op("dve", lambda e, a=a, o=o: e.scalar_tensor_tensor(on[o][:], xn[a][:], vec[:, C_KN:C_KN + 1], rs[:], ALU.mult, ALU.mult),
                 r=["xn%d" % a, "vec", "rs"], w=["on%d" % o])
            P.dma(k_nope_o[hd][:, ts], on[o][:], r=["on%d" % o], is_output=True)
            b = nxt("xr", 2)
            P.op("dve", lambda e, b=b: e.scalar_tensor_tensor(xr[b][:], kpe[:], vec[0:64, C_KR:C_KR + 1], rs[0:64, :], ALU.mult, ALU.mult),
                 r=["kpe", "vec", "rs"], w=["xr%d" % b])
            ro = nxt("or", 2)
            rope(xr[b][:], "xr%d" % b, orr[ro][:], "or%d" % ro)
            P.dma(k_rope_o[hd][:, ts], orr[ro][:], r=["or%d" % ro], is_output=True)
            pa = nxt("pa", 2)
            for c in range(4):
                P.mm(pA[pa][:], wh["v"][s][:, c, :], clat[:, 4 + c, :], c == 0, c == 3, r=["wv%d" % s, "clat"], w=["pA%d" % pa])
            o = nxt("on", 3)
            P.op("act", lambda e, o=o, pa=pa: e.copy(on[o][:], pA[pa][:]), r=["pA%d" % pa], w=["on%d" % o])
            P.dma(v_o[hd][:, ts], on[o][:], r=["on%d" % o], is_output=True)
    return P.emit()


def mla_pre_inputs(hT, posb, lw):
    H = MLA_HEADS
    w_in = lw["w_in"]
    w_uq = lw["w_uq"].reshape(512, H, MLA_QK)
    w_ukv = lw["w_ukv"].reshape(512, H, 256)
    invf, rot = rope_consts()
    vecs = np.zeros((128, 28), np.float32)
    vecs[:, 0:16] = col_vec(lw["mixer_norm"])
    vecs[:, 16:20] = col_vec(lw["q_lat_norm"])
    vecs[:, 20:24] = col_vec(lw["kv_lat_norm"])
    vecs[:, 24] = lw["q_norm"][:128]
    vecs[:, 25] = lw["k_norm"][:128]
    vecs[0:64, 26] = lw["q_norm"][128:]
    vecs[0:64, 27] = lw["k_norm"][128:]
    return {
        "w_in": tile_w(w_in[:, :1024]), "w_inpe": tile_w(np.ascontiguousarray(w_in[:, 1024:1088]), 64),
        "w_qn": tile_w(np.ascontiguousarray(w_uq[:, :, :128]).reshape(512, H * 128)),
        "w_qr": tile_w(np.ascontiguousarray(w_uq[:, :, 128:]).reshape(512, H * 64), 64),
        "w_kn": tile_w(np.ascontiguousarray(w_ukv[:, :, :128]).reshape(512, H * 128)),
        "w_v": tile_w(np.ascontiguousarray(w_ukv[:, :, 128:]).reshape(512, H * 128)),
        "vecs": vecs, "invf": invf, "rot": rot,
    }


def attn_masks(T=512):
    m = np.zeros((T // 128, 128, T), np.float32)
    q = np.arange(T)[None, :]
    for kb in range(T // 128):
        k = (kb * 128 + np.arange(128))[:, None]
        m[kb] = (q >= k).astype(np.float32)
    return m


def build_attn(S, NP):
    T = 512
    NQ = S // T
    NB = T // 128
    scale = MLA_QK ** -0.5
    P = Prog()
    qn_d = P.dram_in("qn", [NP, 128, S])
    qr_d = P.dram_in("qr", [NP, 64, S])
    kn_d = P.dram_in("kn", [NP, 128, S])
    kr_d = P.dram_in("kr", [NP, 64, S])
    v_d = P.dram_in("v", [NP, 128, S // 128, 128])
    mk_d = P.dram_in("masks", [NB, 128, T])
    o_d = P.dram_out("oT", [NP, 128, S])
    qn = [P.sbuf("qn%d" % i, [128, T]) for i in range(2)]
    qr = [P.sbuf("qr%d" % i, [64, T]) for i in range(2)]
    kn = [P.sbuf("kn%d" % i, [128, T]) for i in range(3)]
    kr = [P.sbuf("kr%d" % i, [64, T]) for i in range(3)]
    vb = [P.sbuf("vb%d" % i, [128, NB, 128]) for i in range(3)]
    mk = P.sbuf("mk", [128, NB, T])
    ones = P.sbuf("ones", [128, 128])
    pt = [P.sbuf("pt%d" % i, [128, T]) for i in range(3)]
    rl = P.sbuf("rl", [128, T])
    ot = [P.sbuf("ot%d" % i, [128, T]) for i in range(2)]
    sp = [P.psum("sp%d" % i, [128, T]) for i in range(2)]
    op_ = [P.psum("op%d" % i, [128, T]) for i in range(2)]
    lp = [P.psum("lp%d" % i, [128, T]) for i in range(2)]
    P.op("pool", lambda e: e.memset(ones[:], 1.0), w=["ones"])
    for kb in range(NB):
        P.dma(mk[:, kb, :], mk_d[kb], w=["mk"])
    cnt = {}

    def nxt(k, n):
        v = cnt.get(k, 0) % n
        cnt[k] = cnt.get(k, 0) + 1
        return v

    for pr in range(NP):
        for j in range(NQ):
            qs = nxt("q", 2)
            tsq = slice(j * T, (j + 1) * T)
            P.dma(qn[qs][:], qn_d[pr][:, tsq], w=["qn%d" % qs])
            P.dma(qr[qs][:], qr_d[pr][:, tsq], w=["qr%d" % qs])
            acc = nxt("acc", 2)
            for c in range(j + 1):
                ks = nxt("k", 3)
                tsk = slice(c * T, (c + 1) * T)
                P.dma(kn[ks][:], kn_d[pr][:, tsk], w=["kn%d" % ks])
                P.dma(kr[ks][:], kr_d[pr][:, tsk], w=["kr%d" % ks])
                P.dma(vb[ks][:], v_d[pr][:, c * NB:(c + 1) * NB, :], w=["vb%d" % ks])
                for kb in range(NB):
                    s = nxt("sp", 2)
                    ksl = slice(kb * 128, (kb + 1) * 128)
                    P.mm(sp[s][:], kn[ks][:, ksl], qn[qs][:], True, False, r=["kn%d" % ks, "qn%d" % qs], w=["sp%d" % s])
                    P.mm(sp[s][:], kr[ks][:, ksl], qr[qs][:], False, True, r=["kr%d" % ks, "qr%d" % qs], w=["sp%d" % s])
                    p = nxt("pt", 3)
                    P.op("act", lambda e, p=p, s=s: e.activation(pt[p][:], sp[s][:], AF.Exp, scale=scale), r=["sp%d" % s], w=["pt%d" % p])
                    if c == j:
                        P.op("pool", lambda e, p=p, kb=kb: e.tensor_tensor(pt[p][:], pt[p][:], mk[:, kb, :], ALU.mult),
                             r=["pt%d" % p, "mk"], w=["pt%d" % p])
                    first = (c == 0 and kb == 0)
                    last = (c == j and kb == NB - 1)
                    P.mm(op_[acc][:], vb[ks][:, kb, :], pt[p][:], first, last, r=["vb%d" % ks, "pt%d" % p], w=["op%d" % acc])
                    P.mm(lp[acc][:], ones[:], pt[p][:], first, last, r=["ones", "pt%d" % p], w=["lp%d" % acc])
            P.op("dve", lambda e, acc=acc: e.reciprocal(rl[:], lp[acc][:]), r=["lp%d" % acc], w=["rl"])
            o = nxt("ot", 2)
            P.op("dve", lambda e, acc=acc, o=o: e.tensor_tensor(ot[o][:], op_[acc][:], rl[:], ALU.mult), r=["op%d" % acc, "rl"], w=["ot%d" % o])
            P.dma(o_d[pr][:, tsq], ot[o][:], r=["ot%d" % o], is_output=True)
    return P.emit()


class Rot:
    def __init__(self, P, name, shape, n, psum=False, dtype=F32):
        self.bufs = [(P.psum if psum else P.sbuf)("%s%d" % (name, i), shape, dtype) for i in range(n)]
        self.keys = ["%s%d" % (name, i) for i in range(n)]
        self.i = 0

    def next(self):
        k = self.i % len(self.bufs)
        self.i += 1
        return self.bufs[k], self.keys[k]


def build_gdn_pre(NT):
    T = 512 if NT >= 512 else NT
    NTILES = NT // T
    KC = 16
    NQK = 64
    P = Prog()
    hT = P.dram_in("hT", [D_MODEL, 3 + NT])
    w_in = P.dram_in("w_in", [96, 128, KC, 128])
    w_ba = P.dram_in("w_ba", [1, 128, KC, 64])
    vecs = P.dram_in("vecs", [128, KC])
    cw_d = P.dram_in("cw", [128, NQK * 4])
    hp_d = P.dram_in("hp", [64, 2])
    qT_o = P.dram_out("qT", [2048, NT])
    kT_o = P.dram_out("kT", [2048, NT])
    vT_o = P.dram_out("vT", [4096, NT])
    zT_o = P.dram_out("zT", [4096, NT])
    bg_o = P.dram_out("bg", [64, NT])

    h = P.sbuf("h", [128, KC, T])
    hh = P.sbuf("hh", [128, KC, 3])
    vec = P.sbuf("vec", [128, KC])
    cw = P.sbuf("cw_s", [128, NQK * 4])
    hp = P.sbuf("hp_s", [64, 2])
    nal = P.sbuf("nal", [64, 1])
    carry = P.sbuf("carry", [128, NQK, 3])
    rs = P.sbuf("rs", [128, T])
    rsh = P.sbuf("rsh", [128, 3])
    wb = Rot(P, "wb", [128, KC, 128], 2)
    wba = P.sbuf("wba", [128, KC, 64])
    xc = Rot(P, "xc", [128, T + 3], 2)
    yb = Rot(P, "yb", [128, T], 2)
    ob = Rot(P, "ob", [128, T], 3)
    sb64 = Rot(P, "sb64_", [64, T], 3)
    bgt = P.sbuf("bgt", [64, T])
    pA = Rot(P, "pA", [128, T], 3, psum=True)
    pH = P.psum("pH", [128, 512])
    NH = NormHelper(P, T)

    P.dma(vec[:], vecs[:, :], w=["vec"])
    P.dma(cw[:], cw_d[:, :], w=["cw_s"])
    P.dma(hp[:], hp_d[:, :], w=["hp_s"])
    P.op("act", lambda e: e.activation(nal[32:64, :], hp[32:64, 1:2], AF.Exp), r=["hp_s"], w=["nal"])
    P.op("dve", lambda e: e.tensor_scalar(nal[32:64, :], nal[32:64, :], -1.0, None, ALU.mult), r=["nal"], w=["nal"])

    P.dma(hh[:], hT[:, 0:3].rearrange("(c p) n -> p c n", p=128), w=["hh"])
    sqh = P.sbuf("sqh", [128, 3])
    for c in range(KC):
        P.op("dve", lambda e, c=c: e.tensor_tensor(sqh[:], hh[:, c, :], hh[:, c, :], ALU.mult), r=["hh"], w=["sqh"])
        P.mm(pH[:, 0:3], NH.ones[:], sqh[:], c == 0, c == KC - 1, r=["sqh", "ones"], w=["pH"])
    rsqrt_from(P, rsh[:], "rsh", pH[:, 0:3], "pH", D_MODEL, NORM_EPS)
    for c in range(KC):
        P.op("dve", lambda e, c=c: e.scalar_tensor_tensor(hh[:, c, :], hh[:, c, :], vec[:, c:c + 1], rsh[:], ALU.mult, ALU.mult),
             r=["hh", "vec", "rsh"], w=["hh"])
    for m in range(NQK):
        w, wk = wb.next()
        P.dma(w[:], w_in[m], w=[wk])
        for c in range(KC):
            P.mm(pH[:, 0:3], w[:, c, :], hh[:, c, :], c == 0, c == KC - 1, r=[wk, "hh"], w=["pH"])
        P.op("act", lambda e, m=m: e.copy(carry[:, m, :], pH[:, 0:3]), r=["pH"], w=["carry"])

    for t in range(NTILES):
        ts = slice(t * T, (t + 1) * T)
        P.dma(h[:], hT[:, 3 + t * T:3 + (t + 1) * T].rearrange("(c p) n -> p c n", p=128), w=["h"])
        NH.rstd([(h[:, c, :], "h", 128) for c in range(KC)], D_MODEL, rs[:], "rs")
        for c in range(KC):
            P.op("dve", lambda e, c=c: e.scalar_tensor_tensor(h[:, c, :], h[:, c, :], vec[:, c:c + 1], rs[:], ALU.mult, ALU.mult),
                 r=["h", "vec", "rs"], w=["h"])
        for m in range(96):
            w, wk = wb.next()
            P.dma(w[:], w_in[m], w=[wk])
            ps, pk = pA.next()
            for c in range(KC):
                P.mm(ps[:], w[:, c, :], h[:, c, :], c == 0, c == KC - 1, r=[wk, "h"], w=[pk])
            if m >= NQK:
                o, ok = ob.next()
                P.op("act", lambda e, o=o, ps=ps: e.copy(o[:], ps[:]), r=[pk], w=[ok])
                P.dma(zT_o[(m - NQK) * 128:(m - NQK + 1) * 128, ts], o[:], r=[ok], is_output=True)
                continue
            x, xk = xc.next()
            P.op("act", lambda e, x=x, ps=ps: e.copy(x[:, 3:3 + T], ps[:]), r=[pk], w=[xk])
            P.op("pool", lambda e, x=x, m=m: e.tensor_copy(x[:, 0:3], carry[:, m, :]), r=["carry"], w=[xk])
            P.op("pool", lambda e, x=x, m=m: e.tensor_copy(carry[:, m, :], x[:, T:T + 3]), r=[xk], w=["carry"])
            y, yk = yb.next()
            P.op("dve", lambda e, x=x, y=y, m=m: e.tensor_scalar(y[:], x[:, 0:T], cw[:, m * 4:m * 4 + 1], None, ALU.mult), r=[xk, "cw_s"], w=[yk])
            for j in range(1, 4):
                P.op("dve", lambda e, x=x, y=y, m=m, j=j: e.scalar_tensor_tensor(y[:], x[:, j:j + T], cw[:, m * 4 + j:m * 4 + j + 1], y[:], ALU.mult, ALU.add),
                     r=[xk, "cw_s", yk], w=[yk])
            o, ok = ob.next()
            P.op("act", lambda e, o=o, y=y: e.activation(o[:], y[:], AF.Silu), r=[yk], w=[ok])
            if m < 32:
                NH.rstd([(o[:], ok, 128)], 1.0, rs[:], "rs", eps=NORM_EPS, post_scale=(GDN_DK ** -0.5 if m < 16 else 1.0))
                P.op("dve", lambda e, o=o: e.tensor_tensor(o[:], o[:], rs[:], ALU.mult), r=[ok, "rs"], w=[ok])
                dst = qT_o if m < 16 else kT_o
                mm_ = m % 16
            else:
                dst = vT_o
                mm_ = m - 32
            P.dma(dst[mm_ * 128:(mm_ + 1) * 128, ts], o[:], r=[ok], is_output=True)
        P.dma(wba[:], w_ba[0], w=["wba"])
        ps, pk = pA.next()
        for c in range(KC):
            P.mm(ps[0:64, :], wba[:, c, :], h[:, c, :], c == 0, c == KC - 1, r=["wba", "h"], w=[pk])
        P.op("act", lambda e, ps=ps: e.activation(bgt[0:32, :], ps[0:32, :], AF.Sigmoid), r=[pk], w=["bgt"])
        x1, k1 = sb64.next()
        x2, k2 = sb64.next()
        x3, k3 = sb64.next()
        P.op("dve", lambda e, ps=ps, x1=x1: e.tensor_scalar(x1[32:64, :], ps[32:64, :], hp[32:64, 0:1], None, ALU.add), r=[pk, "hp_s"], w=[k1])
        P.op("act", lambda e, x1=x1, x2=x2: e.activation(x2[32:64, :], x1[32:64, :], AF.Abs), r=[k1], w=[k2])
        P.op("act", lambda e, x2=x2: e.activation(x2[32:64, :], x2[32:64, :], AF.Exp, scale=-1.0), r=[k2], w=[k2])
        P.op("act", lambda e, x2=x2, x3=x3: e.activation(x3[32:64, :], x2[32:64, :], AF.Ln, bias=1.0), r=[k2], w=[k3])
        P.op("dve", lambda e, x1=x1, x3=x3: e.scalar_tensor_tensor(x3[32:64, :], x1[32:64, :], 0.0, x3[32:64, :], ALU.max, ALU.add), r=[k1, k3], w=[k3])
        P.op("dve", lambda e, x3=x3: e.tensor_scalar(bgt[32:64, :], x3[32:64, :], nal[32:64, 0:1], None, ALU.mult), r=[k3, "nal"], w=["bgt"])
        P.dma(bg_o[:, ts], bgt[:], r=["bgt"], is_output=True)
    return P.emit()


def gdn_pre_inputs(lw):
    w_in = lw["w_in"]
    cw = lw["conv_w"]
    cwt = np.ascontiguousarray(cw.T.reshape(64, 128, 4).transpose(1, 0, 2).reshape(128, 256))
    hp = np.zeros((64, 2), np.float32)
    hp[32:, 0] = lw["dt_bias"]
    hp[32:, 1] = lw["a_log"]
    return {"w_in": tile_w(np.ascontiguousarray(w_in[:, :12288])), "w_ba": tile_w(np.ascontiguousarray(w_in[:, 12288:12352]), 64),
            "vecs": col_vec(lw["mixer_norm"]), "cw": cwt, "hp": hp}


def gdn_consts():
    p = np.arange(128)[:, None]
    f = np.arange(128)[None, :]
    triu = (f >= p).astype(np.float32)
    stril = (p > f).astype(np.float32)
    ident = np.eye(128, dtype=np.float32)
    return np.ascontiguousarray(np.stack([triu, stril, ident]))


def build_gdn_core(S, NHD):
    C = 128
    NCH = S // C
    GRP = 4 if NCH % 4 == 0 else 1
    NQ = (NHD + 1) // 2
    P = Prog()
    QT = P.dram_in("QT", [NQ, 128, S])
    KT = P.dram_in("KT", [NQ, 128, S])
    Ktm = P.dram_in("Ktm", [NQ, 128, NCH, 128])
    Vtm = P.dram_in("Vtm", [NHD, 128, NCH, 128])
    Gd = P.dram_in("G", [NHD, 128, NCH])
    Bd = P.dram_in("Bt", [NHD, 128, NCH])
    cst = P.dram_in("cst", [3, 128, 128])
    o_d = P.dram_out("o", [NHD, 128, NCH, 128])

    triu = P.sbuf("triu", [128, 128])
    stril = P.sbuf("stril", [128, 128])
    ident = P.sbuf("ident", [128, 128])
    ones = P.sbuf("ones", [128, 128])
    P.dma(triu[:], cst[0], w=["triu"])
    P.dma(stril[:], cst[1], w=["stril"])
    P.dma(ident[:], cst[2], w=["ident"])
    P.op("pool", lambda e: e.memset(ones[:], 1.0), w=["ones"])
    Gh = Rot(P, "Gh", [128, NCH], 2)
    Bh = Rot(P, "Bh", [128, NCH], 2)
    kt4 = Rot(P, "kt4_", [128, GRP * 128], 2)
    qt4 = Rot(P, "qt4_", [128, GRP * 128], 2)
    km4 = Rot(P, "km4_", [128, GRP, 128], 2)
    vm4 = Rot(P, "vm4_", [128, GRP, 128], 2)
    ob4 = Rot(P, "ob4_", [128, GRP, 128], 2)
    St = P.sbuf("St", [128, 128])
    sq = lambda name, n: Rot(P, name, [128, 128], n)
    gbc, decL, decT, L0r, N0r, intr, Pr, Lr, Nr = (sq("gbc", 2), sq("decL", 2), sq("decT", 2), sq("L0r", 2), sq("N0r", 2),
                                                   sq("intr", 3), sq("Pr", 3), sq("Lr", 3), sq("Nr", 3))
    t12 = sq("t12_", 3)
    Vb, Kbg, Kdec, ub, wTb, vnew, o1 = sq("Vb", 2), sq("Kbg", 2), sq("Kdec", 3), sq("ub", 3), sq("wTb", 3), sq("vnew", 2), sq("o1_", 2)
    scl = Rot(P, "scl", [128, 8], 3)
    banks = [P.psum("bk%d" % i, [128, 512]) for i in range(8)]

    class PsRot:
        def __init__(self, idxs):
            self.idxs = idxs
            self.i = 0

        def next(self):
            k = self.idxs[self.i % len(self.idxs)]
            self.i += 1
            return banks[k], "bk%d" % k

    psA = PsRot([0, 1, 2, 3, 4])
    psB = PsRot([5, 6, 7])

    def copy_to(eng, dst, dk, src, sk):
        if eng == "act":
            P.op("act", lambda e: e.copy(dst, src), r=[sk], w=[dk])
        else:
            P.op("dve", lambda e: e.tensor_copy(dst, src), r=[sk], w=[dk])

    for hd in range(NHD):
        qk = hd // 2
        G_, Gk = Gh.next()
        B_, Bk = Bh.next()
        P.dma(G_[:], Gd[hd], w=[Gk])
        P.dma(B_[:], Bd[hd], w=[Bk])
        P.op("pool", lambda e: e.memset(St[:], 0.0), w=["St"])
        grp = {}

        def load_group(gi):
            k4, k4k = kt4.next()
            q4, q4k = qt4.next()
            m4, m4k = km4.next()
            v4, v4k = vm4.next()
            sl = slice(gi * GRP * 128, (gi + 1) * GRP * 128)
            P.dma(k4[:], KT[qk][:, sl], w=[k4k])
            P.dma(q4[:], QT[qk][:, sl], w=[q4k])
            P.dma(m4[:], Ktm[qk][:, gi * GRP:(gi + 1) * GRP, :], w=[m4k])
            P.dma(v4[:], Vtm[hd][:, gi * GRP:(gi + 1) * GRP, :], w=[v4k])
            grp[gi] = (k4, k4k, q4, q4k, m4, m4k, v4, v4k)

        def pre(n):
            gi, li = n // GRP, n % GRP
            if li == 0:
                load_group(gi)
            k4, k4k, q4, q4k, m4, m4k, v4, v4k = grp[gi]
            ktc = k4[:, li * 128:(li + 1) * 128]
            qtc = q4[:, li * 128:(li + 1) * 128]
            ktm = m4[:, li, :]
            vtm = v4[:, li, :]
            gcol = G_[:, n:n + 1]
            bcol = B_[:, n:n + 1]
            gb, gbk = gbc.next()
            P.op("dve", lambda e: e.tensor_scalar(gb[:], ones[:], gcol, None, ALU.mult), r=["ones", Gk], w=[gbk])
            pss, pssk = psA.next()
            P.mm(pss[:, 0:128], triu[:], gb[:], True, True, r=["triu", gbk], w=[pssk])
            P.mm(pss[:, 128:256], ones[:], gb[:], True, True, r=["ones", gbk], w=[pssk])
            psb_, psbk = psA.next()
            psb = psb_[:, 0:128]
            P.mm(psb, gb[:], triu[:], True, True, r=[gbk, "triu"], w=[psbk])
            sc, sck = scl.next()
            P.op("act", lambda e: e.copy(sc[:, 0:1], pss[:, 0:1]), r=[pssk], w=[sck])
            P.op("act", lambda e: e.copy(sc[:, 1:2], pss[:, 128:129]), r=[pssk], w=[sck])
            P.op("act", lambda e: e.activation(sc[:, 2:4], sc[:, 0:2], AF.Exp), r=[sck], w=[sck])
            P.op("dve", lambda e: e.tensor_tensor(sc[:, 4:5], sc[:, 1:2], sc[:, 0:1], ALU.subtract), r=[sck], w=[sck])
            P.op("act", lambda e: e.activation(sc[:, 4:5], sc[:, 4:5], AF.Exp), r=[sck], w=[sck])
            P.op("dve", lambda e: e.tensor_tensor(sc[:, 5:6], sc[:, 2:3], bcol, ALU.mult), r=[sck, Bk], w=[sck])
            t1, t1k = t12.next()
            dl, dlk = decL.next()
            P.op("dve", lambda e: e.tensor_scalar(t1[:], psb, sc[:, 0:1], 0.0, ALU.subtract, ALU.max), r=[psbk, sck], w=[t1k])
            P.op("act", lambda e: e.activation(dl[:], t1[:], AF.Exp, scale=-1.0), r=[t1k], w=[dlk])
            P.op("pool", lambda e: e.tensor_tensor(dl[:], dl[:], stril[:], ALU.mult), r=[dlk, "stril"], w=[dlk])
            t2, t2k = t12.next()
            dt_, dtk = decT.next()
            P.op("dve", lambda e: e.tensor_scalar(t2[:], psb, sc[:, 0:1], 0.0, ALU.subtract, ALU.min), r=[psbk, sck], w=[t2k])
            P.op("act", lambda e: e.activation(dt_[:], t2[:], AF.Exp), r=[t2k], w=[dtk])
            P.op("pool", lambda e: e.tensor_tensor(dt_[:], dt_[:], triu[:], ALU.mult), r=[dtk, "triu"], w=[dtk])
            psc, psck = psA.next()
            psc = psc[:, 0:128]
            P.mm(psc, ktc, ktc, True, True, r=[k4k], w=[psck])
            L0, L0k = L0r.next()
            P.op("dve", lambda e: e.scalar_tensor_tensor(L0[:], psc, bcol, dl[:], ALU.mult, ALU.mult), r=[psck, Bk, dlk], w=[L0k])
            psd, psdk = psA.next()
            psd = psd[:, 0:128]
            P.mm(psd, L0[:], ident[:], True, True, r=[L0k, "ident"], w=[psdk])
            N0, N0k = N0r.next()
            copy_to("act", N0[:], N0k, psd, psdk)
            pse, psek = psA.next()
            pse = pse[:, 0:128]
            P.mm(pse, ktc, qtc, True, True, r=[k4k, q4k], w=[psek])
            it, itk = intr.next()
            P.op("dve", lambda e: e.tensor_tensor(it[:], pse, dt_[:], ALU.mult), r=[psek, dtk], w=[itk])
            Pc, Pck = Pr.next()
            P.op("pool", lambda e, Pc=Pc: e.tensor_tensor(Pc[:], ident[:], N0[:], ALU.subtract), r=["ident", N0k], w=[Pck])
            Lp, Lpk, Np, Npk = L0, L0k, N0, N0k
            for lev in range(1, 7):
                psl, pslk = psA.next()
                psl = psl[:, 0:128]
                P.mm(psl, Np[:], Lp[:], True, True, r=[Npk, Lpk], w=[pslk])
                if lev < 6:
                    psn, psnk = psA.next()
                    psn = psn[:, 0:128]
                    P.mm(psn, Lp[:], Np[:], True, True, r=[Npk, Lpk], w=[psnk])
                Ln, Lnk = Lr.next()
                copy_to("dve" if lev % 2 else "act", Ln[:], Lnk, psl, pslk)
                if lev < 6:
                    Nn, Nnk = Nr.next()
                    copy_to("act" if lev % 2 else "dve", Nn[:], Nnk, psn, psnk)
                psp, pspk = psA.next()
                psp = psp[:, 0:128]
                P.mm(psp, Ln[:], Pc[:], True, True, r=[Lnk, Pck], w=[pspk])
                Pn, Pnk = Pr.next()
                P.op("dve", lambda e, Pn=Pn, Pc=Pc, psp=psp: e.tensor_tensor(Pn[:], Pc[:], psp, ALU.add), r=[Pck, pspk], w=[Pnk])
                Pc, Pck = Pn, Pnk
                Lp, Lpk = Ln, Lnk
                if lev < 6:
                    Np, Npk = Nn, Nnk
            vb_, vbk = Vb.next()
            kb_, kbk = Kbg.next()
            kd_, kdk = Kdec.next()
            P.op("act", lambda e: e.mul(vb_[:], vtm, bcol), r=[v4k, Bk], w=[vbk])
            P.op("act", lambda e: e.mul(kb_[:], ktm, sc[:, 5:6]), r=[m4k, sck], w=[kbk])
            P.op("pool", lambda e: e.tensor_scalar(kd_[:], ktm, sc[:, 4:5], None, ALU.mult), r=[m4k, sck], w=[kdk])
            psu, psuk = psA.next()
            psu = psu[:, 0:128]
            P.mm(psu, Pc[:], vb_[:], True, True, r=[Pck, vbk], w=[psuk])
            u_, uk = ub.next()
            copy_to("act", u_[:], uk, psu, psuk)
            psw, pswk = psA.next()
            psw = psw[:, 0:128]
            P.mm(psw, kb_[:], Pc[:], True, True, r=[kbk, Pck], w=[pswk])
            w_, wk = wTb.next()
            copy_to("dve", w_[:], wk, psw, pswk)
            return dict(u=u_, uk=uk, w=w_, wk=wk, it=it, itk=itk, kd=kd_, kdk=kdk, qtc=qtc, q4k=q4k, sc=sc, sck=sck)

        obuf = [None]

        def seq(n, d):
            gi, li = n // GRP, n % GRP
            if li == 0:
                obuf[0] = ob4.next()
            ob_, obk = obuf[0]
            sc = d["sc"]
            ps1, ps1k = psB.next()
            ps1 = ps1[:, 0:128]
            P.mm(ps1, d["w"][:], St[:], True, True, r=[d["wk"], "St"], w=[ps1k])
            vn, vnk = vnew.next()
            P.op("dve", lambda e: e.tensor_tensor(vn[:], d["u"][:], ps1, ALU.subtract), r=[d["uk"], ps1k], w=[vnk])
            ps2, ps2k = psB.next()
            ps2 = ps2[:, 0:128]
            P.mm(ps2, d["qtc"], St[:], True, True, r=[d["q4k"], "St"], w=[ps2k])
            o1_, o1k = o1.next()
            P.op("act", lambda e: e.mul(o1_[:], ps2, sc[:, 2:3]), r=[ps2k, d["sck"]], w=[o1k])
            ps3, ps3k = psB.next()
            ps3 = ps3[:, 0:128]
            P.mm(ps3, d["it"][:], vn[:], True, True, r=[d["itk"], vnk], w=[ps3k])
            P.op("dve", lambda e: e.tensor_tensor(ob_[:, li, :], o1_[:], ps3, ALU.add), r=[o1k, ps3k], w=[obk])
            ps4, ps4k = psB.next()
            ps4 = ps4[:, 0:128]
            P.mm(ps4, d["kd"][:], vn[:], True, True, r=[d["kdk"], vnk], w=[ps4k])
            P.op("dve", lambda e: e.scalar_tensor_tensor(St[:], St[:], sc[:, 3:4], ps4, ALU.mult, ALU.add), r=["St", d["sck"], ps4k], w=["St"])
            if li == GRP - 1:
                P.dma(o_d[hd][:, gi * GRP:(gi + 1) * GRP, :], ob_[:], r=[obk], is_output=True)

        prev = pre(0)
        for n in range(NCH):
            nxt_ = pre(n + 1) if n + 1 < NCH else None
            seq(n, prev)
            prev = nxt_
    return P.emit()


_PROGS = {}


def _prog(key, builder, *args):
    k = (key,) + tuple(args)
    if k not in _PROGS:
        _PROGS[k] = builder(*args)
    return _PROGS[k]


def _run(nc, in_maps):
    import sys
    import time
    t0 = time.time()
    res = run_bass_kernel_spmd(nc, in_maps, core_ids=list(range(NCORES)))
    print("[kernel] launch %.1fs" % (time.time() - t0), file=sys.stderr, flush=True)
    return res.results


def _c(a):
    return np.ascontiguousarray(a, dtype=np.float32)


def kernel(x, p, positions, mixer_norm,
           mla_w_in, mla_q_lat_norm, mla_kv_lat_norm, mla_w_uq, mla_w_ukv, mla_q_norm, mla_k_norm, mla_w_o,
           gdn_w_in, gdn_conv_w, gdn_a_log, gdn_dt_bias, gdn_out_norm, gdn_w_out,
           ffn_norm, ffn_w_gate_up, ffn_w_down,
           ple_w_proj, ple_norm, ple_gate_norm, ple_w_gate):
    x = np.asarray(x)
    p = np.asarray(p)
    positions = np.asarray(positions)
    B, S, D = x.shape
    CB = NCORES // B
    NT = S // CB
    NCH = S // 128
    cb = lambda c: (c // CB, slice((c % CB) * NT, (c % CB + 1) * NT))
    hT = []
    posb = []
    for c in range(NCORES):
        b, tok = cb(c)
        hT.append(_c(x[b, tok].T))
        posb.append(np.ascontiguousarray(np.broadcast_to(positions[b, tok][None, :], (64, NT)).astype(np.int32)))
    masks = attn_masks()
    gconst = gdn_consts()
    depth = mixer_norm.shape[0]
    for i in range(depth):
        j = i // 2
        if i % 2 == 0:
            lw = {"mixer_norm": np.asarray(mixer_norm[i]), "w_in": np.asarray(mla_w_in[j]), "q_lat_norm": np.asarray(mla_q_lat_norm[j]),
                  "kv_lat_norm": np.asarray(mla_kv_lat_norm[j]), "w_uq": np.asarray(mla_w_uq[j]), "w_ukv": np.asarray(mla_w_ukv[j]),
                  "q_norm": np.asarray(mla_q_norm[j]), "k_norm": np.asarray(mla_k_norm[j])}
            com = mla_pre_inputs(None, None, lw)
            res = _run(_prog("mla_pre", build_mla_pre, NT), [dict(com, hT=hT[c], posb=posb[c]) for c in range(NCORES)])
            H = MLA_HEADS
            full = {k: np.empty((B, H, d, S), np.float32) for k, d in (("q_nope", 128), ("q_rope", 64), ("k_nope", 128), ("k_rope", 64), ("vT", 128))}
            for c in range(NCORES):
                b, tok = cb(c)
                for k in full:
                    full[k][b][:, :, tok] = res[c][k]
            del res
            flat = {k: v.reshape(B * H, v.shape[2], S) for k, v in full.items()}
            vtm = _c(flat["vT"].transpose(0, 2, 1).reshape(B * H, NCH, 128, 128).transpose(0, 2, 1, 3))
            NP = B * H // NCORES
            ims = []
            for c in range(NCORES):
                sl = slice(c * NP, (c + 1) * NP)
                ims.append({"qn": _c(flat["q_nope"][sl]), "qr": _c(flat["q_rope"][sl]), "kn": _c(flat["k_nope"][sl]),
                            "kr": _c(flat["k_rope"][sl]), "v": _c(vtm[sl]), "masks": masks})
            res = _run(_prog("attn", build_attn, S, NP), ims)
            del ims, full, flat, vtm
            ofull = np.concatenate([res[c]["oT"] for c in range(NCORES)], axis=0).reshape(B, H * 128, S)
            del res
            oT = [_c(ofull[cb(c)[0]][:, cb(c)[1]]) for c in range(NCORES)]
            w_o = tile_w(np.asarray(mla_w_o[j]))
            DM = H * 128
            extra = [{} for _ in range(NCORES)]
            gdn = False
        else:
            lw = {"mixer_norm": np.asarray(mixer_norm[i]), "w_in": np.asarray(gdn_w_in[j]), "conv_w": np.asarray(gdn_conv_w[j]),
                  "a_log": np.asarray(gdn_a_log[j]), "dt_bias": np.asarray(gdn_dt_bias[j])}
            com = gdn_pre_inputs(lw)
            ims = []
            for c in range(NCORES):
                he = np.zeros((D, 3 + NT), np.float32)
                he[:, 3:] = hT[c]
                if c % CB != 0:
                    he[:, 0:3] = hT[c - 1][:, NT - 3:]
                ims.append(dict(com, hT=he))
            res = _run(_prog("gdn_pre", build_gdn_pre, NT), ims)
            del ims
            full = {k: np.empty((B, d, S), np.float32) for k, d in (("qT", 2048), ("kT", 2048), ("vT", 4096), ("bg", 64))}
            zT = []
            for c in range(NCORES):
                b, tok = cb(c)
                for k in full:
                    full[k][b][:, tok] = res[c][k]
                zT.append(_c(res[c]["zT"]))
            del res
            NHD = B * GDN_V_HEADS // NCORES
            NQ = NHD // 2
            tmaj = lambda a: _c(a.transpose(0, 2, 1).reshape(a.shape[0], NCH, 128, a.shape[1]).transpose(0, 2, 1, 3))
            ims = []
            for c in range(NCORES):
                b = c // CB
                v0 = (c % CB) * NHD
                q0 = v0 // 2
                QT = full["qT"][b].reshape(16, 128, S)[q0:q0 + NQ]
                KT = full["kT"][b].reshape(16, 128, S)[q0:q0 + NQ]
                VT = full["vT"][b].reshape(32, 128, S)[v0:v0 + NHD]
                Bt = full["bg"][b][v0:v0 + NHD]
                G = full["bg"][b][32 + v0:32 + v0 + NHD]
                ims.append({"QT": _c(QT), "KT": _c(KT), "Ktm": tmaj(KT), "Vtm": tmaj(VT),
                            "G": _c(G.reshape(NHD, NCH, 128).transpose(0, 2, 1)), "Bt": _c(Bt.reshape(NHD, NCH, 128).transpose(0, 2, 1)),
                            "cst": gconst})
            res = _run(_prog("gdn_core", build_gdn_core, S, NHD), ims)
            del ims, full
            ofull = np.empty((B, 32, 128, S), np.float32)
            for c in range(NCORES):
                b = c // CB
                v0 = (c % CB) * NHD
                o = res[c]["o"]
                ofull[b, v0:v0 + NHD] = o.transpose(0, 3, 2, 1).reshape(NHD, 128, S)
            del res
            ofull = ofull.reshape(B, 4096, S)
            oT = [_c(ofull[cb(c)[0]][:, cb(c)[1]]) for c in range(NCORES)]
            w_o = tile_w(np.asarray(gdn_w_out[j]))
            DM = 4096
            onorm = _c(np.asarray(gdn_out_norm[j]).reshape(128, 1))
            extra = [{"zT": zT[c], "onorm": onorm} for c in range(NCORES)]
            gdn = True
        del ofull
        com = {"w_o": w_o, "w_gu": tile_w(np.asarray(ffn_w_gate_up[i])), "w_dn": tile_w(np.asarray(ffn_w_down[i])),
               "w_pp": tile_w(np.asarray(ple_w_proj[i])), "w_pg": tile_w(np.asarray(ple_w_gate[i])),
               "vecs": _c(np.concatenate([col_vec(np.asarray(ffn_norm[i])), col_vec(np.asarray(ple_norm[i])),
                                          col_vec(np.asarray(ple_gate_norm[i]))], axis=1))}
        ims = []
        for c in range(NCORES):
            b, tok = cb(c)
            ims.append(dict(com, hT=hT[c], oT=oT[c], pT=_c(p[i][b, tok].T), **extra[c]))
        res = _run(_prog("post", build_post, NT, DM, gdn), ims)
        del ims, oT, com
        hT = [res[c]["hT_out"] for c in range(NCORES)]
        del res
    out = np.empty((B, S, D), np.float32)
    for c in range(NCORES):
        b, tok = cb(c)
        out[b, tok] = hT[c].T
    return out
```

```python
import math
from contextlib import ExitStack

import numpy as np
import concourse.bass as bass
import concourse.mybir as mybir
from concourse.bass_utils import run_bass_kernel_spmd

F32 = mybir.dt.float32
I32 = mybir.dt.int32
AF = mybir.ActivationFunctionType
ALU = mybir.AluOpType

D_MODEL = 2048
DEPTH = 4
NORM_EPS = 1e-6
PLE_DIM = 256
MLA_HEADS = 16
MLA_Q_LORA = 512
MLA_KV_LORA = 512
MLA_NOPE = 128
MLA_ROPE = 64
MLA_QK = 192
MLA_V = 128
ROPE_THETA = 10000.0
GDN_QK_HEADS = 16
GDN_V_HEADS = 32
GDN_DK = 128
GDN_DV = 128
GDN_CONV = 4
GDN_KEY_DIM = 2048
GDN_VAL_DIM = 4096
GDN_CONV_DIM = 8192
D_FF = 5632
NCORES = 8
TWO_PI_HI = 6.28125
TWO_PI_LO = 2.0 * math.pi - 6.28125
PI_SAFE = 3.1415925

SAME_ENGINE_SYNC = True
N_DMA_SEMS = 24


class Prog:
    ENGS = ("pe", "act", "dve", "pool", "sp")

    def __init__(self):
        self.nc = bass.Bass("TRN2", target_bir_lowering=False)
        self.stack = ExitStack()
        self.ops = {e: [] for e in self.ENGS}
        self.last_w = {}
        self.readers = {}
        self.dma_uses = [0] * N_DMA_SEMS
        self.dma_rr = 0
        self.out_dmas = []
        self.n_ops = 0

    def dram_in(self, name, shape, dtype=F32):
        return self.nc.dram_tensor(name, list(shape), dtype, kind="ExternalInput").ap()

    def dram_out(self, name, shape, dtype=F32):
        return self.nc.dram_tensor(name, list(shape), dtype, kind="ExternalOutput").ap()

    def sbuf(self, name, shape, dtype=F32):
        return self.stack.enter_context(self.nc.sbuf_tensor(name, list(shape), dtype))

    def psum(self, name, shape, dtype=F32):
        return self.stack.enter_context(self.nc.psum_tensor(name, list(shape), dtype))

    def _deps(self, eng, r, w):
        deps = set()
        for k in r:
            if k in self.last_w:
                deps.add(self.last_w[k])
        for k in w:
            if k in self.last_w:
                deps.add(self.last_w[k])
            for d in self.readers.get(k, ()):
                deps.add(d)
        out = []
        for d in deps:
            if d[0] == "eng" and d[1] == eng:
                if eng in ("pe", "sp") or not SAME_ENGINE_SYNC:
                    continue
            out.append(d)
        return out

    def _mark(self, tok, r, w):
        for k in w:
            self.last_w[k] = tok
            self.readers[k] = []
        for k in r:
            self.readers.setdefault(k, []).append(tok)

    def op(self, eng, fn, r=(), w=()):
        deps = self._deps(eng, r, w)
        idx = len(self.ops[eng])
        for d in deps:
            if d[0] == "eng":
                self.ops[d[1]][d[2]]["inc"] = True
        self.ops[eng].append({"kind": "c", "fn": fn, "deps": deps, "inc": False})
        self._mark(("eng", eng, idx), r, w)
        self.n_ops += 1

    def dma(self, out, in_, r=(), w=(), eng="sp", is_output=False):
        deps = self._deps(eng, r, w)
        for d in deps:
            if d[0] == "eng":
                self.ops[d[1]][d[2]]["inc"] = True
        k = self.dma_rr
        self.dma_rr = (self.dma_rr + 1) % N_DMA_SEMS
        use = self.dma_uses[k]
        self.dma_uses[k] += 1
        if use > 0:
            deps.append(("dma", k, 16 * use))
        tok = ("dma", k, 16 * (use + 1))
        self.ops[eng].append({"kind": "d", "out": out, "in": in_, "deps": deps, "sem": k, "inc": False})
        self._mark(tok, r, w)
        if is_output:
            self.out_dmas.append(tok)
        self.n_ops += 1

    def mm(self, out, lhsT, rhs, start, stop, r=(), w=()):
        self.op("pe", lambda e: e.matmul(out, lhsT, rhs, start=start, stop=stop), r=r, w=w)

    def emit(self):
        nc = self.nc
        st = self.stack
        esem = {e: st.enter_context(nc.semaphore("s_" + e)) for e in self.ENGS}
        dsem = [st.enter_context(nc.semaphore("d%d" % i)) for i in range(N_DMA_SEMS)]
        vals = {}
        for e in self.ENGS:
            c = 0
            for i, o in enumerate(self.ops[e]):
                if o["kind"] == "c" and o["inc"]:
                    c += 1
                    vals[(e, i)] = c
        final = list(self.out_dmas)
        ops = self.ops

        def run(ename, eng):
            waited = {}

            def do_wait(d):
                if d[0] == "eng":
                    key, v, sem = ("e", d[1]), vals[(d[1], d[2])], esem[d[1]]
                else:
                    key, v, sem = ("d", d[1]), d[2], dsem[d[1]]
                if waited.get(key, 0) >= v:
                    return
                waited[key] = v
                eng.wait_ge(sem, v)

            for i, o in enumerate(ops[ename]):
                for d in o["deps"]:
                    do_wait(d)
                if o["kind"] == "c":
                    ins = o["fn"](eng)
                    if o["inc"]:
                        ins.then_inc(esem[ename], 1)
                else:
                    eng.dma_start(out=o["out"], in_=o["in"]).then_inc(dsem[o["sem"]], 16)
            if ename == "sp":
                for d in final:
                    do_wait(d)

        with nc.Block() as block:
            @block.sync
            def _(sync):
                run("sp", sync)

            @block.tensor
            def _(pe):
                run("pe", pe)

            @block.scalar
            def _(act):
                run("act", act)

            @block.vector
            def _(dve):
                run("dve", dve)

            @block.gpsimd
            def _(pool):
                run("pool", pool)
        st.close()
        return nc


class RR:
    def __init__(self, items):
        self.items = list(items)
        self.i = 0

    def __call__(self):
        v = self.items[self.i % len(self.items)]
        self.i += 1
        return v


def tile_w(W, mc=128):
    K, M = W.shape
    return np.ascontiguousarray(W.reshape(K // 128, 128, M // mc, mc).transpose(2, 1, 0, 3))


def col_vec(v):
    return np.ascontiguousarray(v.reshape(-1, 128).T)


def rsqrt_from(P, out, out_key, src, src_key, D, eps, post_scale=1.0):
    s2 = post_scale * post_scale
    P.op("act", lambda e: e.activation(out, src, AF.Sqrt, bias=eps / s2, scale=1.0 / (D * s2)), r=[src_key], w=[out_key])
    P.op("dve", lambda e: e.reciprocal(out, out), r=[out_key], w=[out_key])


class NormHelper:
    def __init__(self, P, T):
        self.P = P
        self.T = T
        self.ones = P.sbuf("ones", [128, 128])
        P.op("pool", lambda e: e.memset(self.ones[:], 1.0), w=["ones"])
        self.sq = [P.sbuf("nsq%d" % i, [128, T]) for i in range(2)]
        self.ps = P.psum("nps", [128, T])
        self.cnt = 0

    def rstd(self, chunks, D, out, out_key, eps=NORM_EPS, post_scale=1.0):
        P = self.P
        n = len(chunks)
        for i, (ap, key, kp) in enumerate(chunks):
            s = self.cnt % 2
            self.cnt += 1
            sq = self.sq[s]
            sk = "nsq%d" % s
            eng = "act" if i % 2 == 0 else "pool"
            if eng == "act":
                P.op("act", lambda e, sq=sq, ap=ap, kp=kp: e.activation(sq[0:kp, :], ap, AF.Square), r=[key], w=[sk])
            else:
                P.op("pool", lambda e, sq=sq, ap=ap, kp=kp: e.tensor_tensor(sq[0:kp, :], ap, ap, ALU.mult), r=[key], w=[sk])
            P.mm(self.ps[:], self.ones[0:kp, :], sq[0:kp, :], start=(i == 0), stop=(i == n - 1), r=[sk, "ones"], w=["nps"])
        rsqrt_from(P, out, out_key, self.ps[:], "nps", D, eps, post_scale)


def build_post(NT, DM, gdn):
    T = 512 if NT >= 512 else NT
    NTILES = NT // T
    KC = D_MODEL // 128
    MC = DM // 128
    FC = D_FF // 128
    FH = 11
    NR = FC // FH
    P = Prog()
    hT = P.dram_in("hT", [D_MODEL, NT])
    oT = P.dram_in("oT", [DM, NT])
    w_o = P.dram_in("w_o", [KC, 128, MC, 128])
    w_gu = P.dram_in("w_gu", [2 * FC, 128, KC, 128])
    w_dn = P.dram_in("w_dn", [KC, 128, FC, 128])
    w_pp = P.dram_in("w_pp", [KC, 128, 2, 128])
    w_pg = P.dram_in("w_pg", [KC, 128, KC, 128])
    vecs = P.dram_in("vecs", [128, 3 * KC])
    pT = P.dram_in("pT", [PLE_DIM, NT])
    if gdn:
        zT = P.dram_in("zT", [DM, NT])
        onorm = P.dram_in("onorm", [128, 1])
    hout = P.dram_out("hT_out", [D_MODEL, NT])

    h = P.sbuf("h", [128, KC, T])
    hn = P.sbuf("hn", [128, KC, T])
    act = P.sbuf("actb", [128, FH, T])
    wg_b = [P.sbuf("wg%d" % i, [128, KC, 128]) for i in range(2)]
    wu_b = [P.sbuf("wu%d" % i, [128, KC, 128]) for i in range(2)]
    wd_b = [P.sbuf("wd%d" % i, [128, FH, 128]) for i in range(2)]
    wpp_b = [P.sbuf("wpp%d" % i, [128, 2, 128]) for i in range(2)]
    vec = P.sbuf("vec", [128, 3 * KC])
    rs = P.sbuf("rs", [128, T])
    rs2 = P.sbuf("rs2", [128, T])
    sg = [P.sbuf("sg%d" % i, [128, T]) for i in range(2)]
    ptile = P.sbuf("ptile", [128, 2, T])
    tmp = [P.sbuf("tmp%d" % i, [128, T]) for i in range(2)]
    if gdn:
        zb = [P.sbuf("zb%d" % i, [128, T]) for i in range(2)]
        on_sb = P.sbuf("on_sb", [128, 1])
    pA = [P.psum("pA%d" % i, [128, T]) for i in range(2)]
    pB = [P.psum("pB%d" % i, [128, T]) for i in range(2)]
    NH = NormHelper(P, T)

    P.dma(vec[:], vecs[:, :], w=["vec"])
    if gdn:
        P.dma(on_sb[:], onorm[:, :], w=["on_sb"])
    cnt = {"wo": 0, "wg": 0, "wd": 0, "ob": 0, "pa": 0, "pb": 0, "sg": 0, "wpp": 0, "tmp": 0, "zb": 0}

    def nxt(k, n):
        v = cnt[k] % n
        cnt[k] += 1
        return v

    for t in range(NTILES):
        ts = slice(t * T, (t + 1) * T)
        P.dma(h[:], hT[:, ts].rearrange("(c p) n -> p c n", p=128), w=["h"])
        for half in range(MC // KC):
            P.dma(hn[:], oT[half * D_MODEL:(half + 1) * D_MODEL, ts].rearrange("(c p) n -> p c n", p=128), w=["hn"])
            if gdn:
                for c in range(KC):
                    zs = nxt("zb", 2)
                    gc_ = half * KC + c
                    P.dma(zb[zs][:], zT[gc_ * 128:(gc_ + 1) * 128, ts], w=["zb%d" % zs])
                    NH.rstd([(hn[:, c, :], "hn", 128)], 128, rs[:], "rs")
                    P.op("act", lambda e, zs=zs: e.activation(zb[zs][:], zb[zs][:], AF.Silu), r=["zb%d" % zs], w=["zb%d" % zs])
                    P.op("dve", lambda e, c=c: e.scalar_tensor_tensor(hn[:, c, :], hn[:, c, :], on_sb[:, 0:1], rs[:], ALU.mult, ALU.mult),
                         r=["hn", "on_sb", "rs"], w=["hn"])
                    P.op("pool", lambda e, c=c, zs=zs: e.tensor_tensor(hn[:, c, :], hn[:, c, :], zb[zs][:], ALU.mult),
                         r=["hn", "zb%d" % zs], w=["hn"])
            for d in range(KC):
                s = nxt("wg", 2)
                P.dma(wg_b[s][:], w_o[d][:, half * KC:(half + 1) * KC, :], w=["wg%d" % s])
                pa = nxt("pa", 2)
                for c in range(KC):
                    P.mm(pA[pa][:], wg_b[s][:, c, :], hn[:, c, :], start=(c == 0), stop=(c == KC - 1),
                         r=["wg%d" % s, "hn"], w=["pA%d" % pa])
                P.op("dve", lambda e, d=d, pa=pa: e.tensor_tensor(h[:, d, :], h[:, d, :], pA[pa][:], ALU.add),
                     r=["h", "pA%d" % pa], w=["h"])
        NH.rstd([(h[:, c, :], "h", 128) for c in range(KC)], D_MODEL, rs[:], "rs")
        for c in range(KC):
            P.op("dve", lambda e, c=c: e.scalar_tensor_tensor(hn[:, c, :], h[:, c, :], vec[:, c:c + 1], rs[:], ALU.mult, ALU.mult),
                 r=["h", "vec", "rs"], w=["hn"])
        for rd in range(NR):
            for fi in range(FH):
                f = rd * FH + fi
                s = nxt("wg", 2)
                P.dma(wg_b[s][:], w_gu[f], w=["wg%d" % s])
                P.dma(wu_b[s][:], w_gu[FC + f], w=["wu%d" % s])
                pa = nxt("pa", 2)
                pb = nxt("pb", 2)
                for c in range(KC):
                    P.mm(pA[pa][:], wg_b[s][:, c, :], hn[:, c, :], start=(c == 0), stop=(c == KC - 1),
                         r=["wg%d" % s, "hn"], w=["pA%d" % pa])
                for c in range(KC):
                    P.mm(pB[pb][:], wu_b[s][:, c, :], hn[:, c, :], start=(c == 0), stop=(c == KC - 1),
                         r=["wu%d" % s, "hn"], w=["pB%d" % pb])
                g = nxt("sg", 2)
                P.op("act", lambda e, g=g, pa=pa: e.activation(sg[g][:], pA[pa][:], AF.Silu), r=["pA%d" % pa], w=["sg%d" % g])
                P.op("dve", lambda e, g=g, pb=pb, fi=fi: e.tensor_tensor(act[:, fi, :], sg[g][:], pB[pb][:], ALU.mult),
                     r=["sg%d" % g, "pB%d" % pb], w=["actb"])
            for d in range(KC):
                s = nxt("wd", 2)
                P.dma(wd_b[s][:], w_dn[d][:, rd * FH:(rd + 1) * FH, :], w=["wd%d" % s])
                pa = nxt("pa", 2)
                for fi in range(FH):
                    P.mm(pA[pa][:], wd_b[s][:, fi, :], act[:, fi, :], start=(fi == 0), stop=(fi == FH - 1),
                         r=["wd%d" % s, "actb"], w=["pA%d" % pa])
                P.op("dve", lambda e, d=d, pa=pa: e.tensor_tensor(h[:, d, :], h[:, d, :], pA[pa][:], ALU.add),
                     r=["h", "pA%d" % pa], w=["h"])
        P.dma(ptile[:], pT[:, ts].rearrange("(c p) n -> p c n", p=128), w=["ptile"])
        for d in range(KC):
            s = nxt("wpp", 2)
            P.dma(wpp_b[s][:], w_pp[d], w=["wpp%d" % s])
            pa = nxt("pa", 2)
            for c in range(2):
                P.mm(pA[pa][:], wpp_b[s][:, c, :], ptile[:, c, :], start=(c == 0), stop=(c == 1),
                     r=["wpp%d" % s, "ptile"], w=["pA%d" % pa])
            g = nxt("sg", 2)
            P.op("act", lambda e, g=g, pa=pa: e.activation(sg[g][:], pA[pa][:], AF.Square), r=["pA%d" % pa], w=["sg%d" % g])
            P.mm(NH.ps[:], NH.ones[:], sg[g][:], start=(d == 0), stop=(d == KC - 1), r=["sg%d" % g, "ones"], w=["nps"])
        rsqrt_from(P, rs2[:], "rs2", NH.ps[:], "nps", D_MODEL, NORM_EPS, 1.0)
        NH.rstd([(h[:, c, :], "h", 128) for c in range(KC)], D_MODEL, rs[:], "rs")
        for c in range(KC):
            P.op("dve", lambda e, c=c: e.scalar_tensor_tensor(hn[:, c, :], h[:, c, :], vec[:, 2 * KC + c:2 * KC + c + 1], rs[:], ALU.mult, ALU.mult),
                 r=["h", "vec", "rs"], w=["hn"])
        for d in range(KC):
            s = nxt("wg", 2)
            P.dma(wg_b[s][:], w_pg[d], w=["wg%d" % s])
            s2 = nxt("wpp", 2)
            P.dma(wpp_b[s2][:], w_pp[d], w=["wpp%d" % s2])
            pa = nxt("pa", 2)
            pb = nxt("pb", 2)
            for c in range(KC):
                P.mm(pA[pa][:], wg_b[s][:, c, :], hn[:, c, :], start=(c == 0), stop=(c == KC - 1),
                     r=["wg%d" % s, "hn"], w=["pA%d" % pa])
            for c in range(2):
                P.mm(pB[pb][:], wpp_b[s2][:, c, :], ptile[:, c, :], start=(c == 0), stop=(c == 1),
                     r=["wpp%d" % s2, "ptile"], w=["pB%d" % pb])
            g = nxt("sg", 2)
            P.op("act", lambda e, g=g, pa=pa: e.activation(sg[g][:], pA[pa][:], AF.Sigmoid), r=["pA%d" % pa], w=["sg%d" % g])
            tq = nxt("tmp", 2)
            P.op("dve", lambda e, tq=tq, pb=pb, d=d: e.scalar_tensor_tensor(tmp[tq][:], pB[pb][:], vec[:, KC + d:KC + d + 1], rs2[:], ALU.mult, ALU.mult),
                 r=["pB%d" % pb, "vec", "rs2"], w=["tmp%d" % tq])
            P.op("pool", lambda e, tq=tq, g=g: e.tensor_tensor(tmp[tq][:], tmp[tq][:], sg[g][:], ALU.mult),
                 r=["tmp%d" % tq, "sg%d" % g], w=["tmp%d" % tq])
            P.op("dve", lambda e, tq=tq, d=d: e.tensor_tensor(h[:, d, :], h[:, d, :], tmp[tq][:], ALU.add),
                 r=["h", "tmp%d" % tq], w=["h"])
        P.dma(hout[:, ts].rearrange("(c p) n -> p c n", p=128), h[:], r=["h"], is_output=True)
    return P.emit()


def rope_consts():
    inv = ROPE_THETA ** (-np.arange(0, MLA_ROPE, 2, dtype=np.float32) / np.float32(MLA_ROPE))
    inv = inv.astype(np.float32)
    invf = np.concatenate([inv, inv]).reshape(64, 1).astype(np.float32)
    rot = np.zeros((64, 64), np.float32)
    for m in range(32):
        rot[m + 32, m] = -1.0
        rot[m, m + 32] = 1.0
    return invf, rot


def build_mla_pre(NT):
    T = 512 if NT >= 512 else NT
    NTILES = NT // T
    KC = 16
    H = MLA_HEADS
    P = Prog()
    hT = P.dram_in("hT", [D_MODEL, NT])
    posb = P.dram_in("posb", [64, NT], I32)
    w_in = P.dram_in("w_in", [8, 128, KC, 128])
    w_inpe = P.dram_in("w_inpe", [1, 128, KC, 64])
    w_qn = P.dram_in("w_qn", [H, 128, 4, 128])
    w_qr = P.dram_in("w_qr", [H, 128, 4, 64])
    w_kn = P.dram_in("w_kn", [H, 128, 4, 128])
    w_v = P.dram_in("w_v", [H, 128, 4, 128])
    vecs = P.dram_in("vecs", [128, KC + 4 + 4 + 4])
    invf_d = P.dram_in("invf", [64, 1])
    rot_d = P.dram_in("rot", [64, 64])
    q_nope_o = P.dram_out("q_nope", [H, 128, NT])
    q_rope_o = P.dram_out("q_rope", [H, 64, NT])
    k_nope_o = P.dram_out("k_nope", [H, 128, NT])
    k_rope_o = P.dram_out("k_rope", [H, 64, NT])
    v_o = P.dram_out("vT", [H, 128, NT])

    h = P.sbuf("h", [128, KC, T])
    clat = P.sbuf("clat", [128, 8, T])
    kpe = P.sbuf("kpe", [64, T])
    vec = P.sbuf("vec", [128, KC + 12])
    invf = P.sbuf("invf_s", [64, 1])
    rot = P.sbuf("rot_s", [64, 64])
    posi = P.sbuf("posi", [64, T], I32)
    posf = P.sbuf("posf", [64, T])
    u0 = P.sbuf("u0", [64, T])
    u1 = P.sbuf("u1", [64, T])
    sin_t = P.sbuf("sin_t", [64, T])
    cos_t = P.sbuf("cos_t", [64, T])
    rs = P.sbuf("rs", [128, T])
    wb = [P.sbuf("wb%d" % i, [128, KC, 128]) for i in range(2)]
    wpe = P.sbuf("wpe", [128, KC, 64])
    wh = {k: [P.sbuf("w%s%d" % (k, i), [128, 4, 128 if k != "qr" else 64]) for i in range(2)] for k in ("qn", "qr", "kn", "v")}
    xn = [P.sbuf("xn%d" % i, [128, T]) for i in range(2)]
    xr = [P.sbuf("xr%d" % i, [64, T]) for i in range(2)]
    on = [P.sbuf("on%d" % i, [128, T]) for i in range(3)]
    orr = [P.sbuf("or%d" % i, [64, T]) for i in range(2)]
    t64 = [P.sbuf("t64_%d" % i, [64, T]) for i in range(2)]
    pA = [P.psum("pA%d" % i, [128, T]) for i in range(2)]
    pB = [P.psum("pB%d" % i, [128, T]) for i in range(2)]
    pR = P.psum("pR", [64, T])
    NH = NormHelper(P, T)
    cnt = {}

    def nxt(k, n):
        v = cnt.get(k, 0) % n
        cnt[k] = cnt.get(k, 0) + 1
        return v

    P.dma(vec[:], vecs[:, :], w=["vec"])
    P.dma(invf[:], invf_d[:, :], w=["invf"])
    P.dma(rot[:], rot_d[:, :], w=["rot"])
    C_QN, C_KN, C_QR, C_KR = KC + 8, KC + 9, KC + 10, KC + 11

    def rope(src, src_key, dst, dst_key):
        P.mm(pR[:], rot[:], src, True, True, r=[src_key, "rot"], w=["pR"])
        tq = nxt("t64", 2)
        P.op("dve", lambda e: e.tensor_tensor(t64[tq][:], pR[:], sin_t[:], ALU.mult), r=["pR", "sin_t"], w=["t64_%d" % tq])
        P.op("pool", lambda e: e.tensor_tensor(dst, src, cos_t[:], ALU.mult), r=[src_key, "cos_t"], w=[dst_key])
        P.op("pool", lambda e: e.tensor_tensor(dst, dst, t64[tq][:], ALU.add), r=[dst_key, "t64_%d" % tq], w=[dst_key])

    for t in range(NTILES):
        ts = slice(t * T, (t + 1) * T)
        P.dma(h[:], hT[:, ts].rearrange("(c p) n -> p c n", p=128), w=["h"])
        P.dma(posi[:], posb[:, ts], w=["posi"])
        P.op("dve", lambda e: e.tensor_copy(posf[:], posi[:]), r=["posi"], w=["posf"])
        P.op("dve", lambda e: e.tensor_scalar(posf[:], posf[:], invf[:, 0:1], None, ALU.mult), r=["posf", "invf"], w=["posf"])
        P.op("dve", lambda e: e.tensor_scalar(u1[:], posf[:], 1.0 / (2.0 * math.pi), None, ALU.mult), r=["posf"], w=["u1"])
        P.op("dve", lambda e: e.tensor_copy(posi[:], u1[:]), r=["u1"], w=["posi"])
        P.op("dve", lambda e: e.tensor_copy(u1[:], posi[:]), r=["posi"], w=["u1"])
        P.op("dve", lambda e: e.scalar_tensor_tensor(u0[:], u1[:], -TWO_PI_HI, posf[:], ALU.mult, ALU.add), r=["u1", "posf"], w=["u0"])
        P.op("dve", lambda e: e.scalar_tensor_tensor(u0[:], u1[:], -TWO_PI_LO, u0[:], ALU.mult, ALU.add), r=["u1", "u0"], w=["u0"])

        def wrap(buf, key):
            P.op("dve", lambda e: e.tensor_scalar(u1[:], buf, math.pi, -2.0 * math.pi, ALU.is_ge, ALU.mult), r=[key], w=["u1"])
            P.op("dve", lambda e: e.tensor_tensor(buf, buf, u1[:], ALU.add), r=[key, "u1"], w=[key])
            P.op("dve", lambda e: e.tensor_scalar(u1[:], buf, -math.pi, 2.0 * math.pi, ALU.is_lt, ALU.mult), r=[key], w=["u1"])
            P.op("dve", lambda e: e.tensor_tensor(buf, buf, u1[:], ALU.add), r=[key, "u1"], w=[key])
            P.op("dve", lambda e: e.tensor_scalar(buf, buf, PI_SAFE, -PI_SAFE, ALU.min, ALU.max), r=[key], w=[key])

        wrap(u0[:], "u0")
        P.op("act", lambda e: e.activation(sin_t[:], u0[:], AF.Sin), r=["u0"], w=["sin_t"])
        P.op("dve", lambda e: e.tensor_scalar(u0[:], u0[:], 0.5 * math.pi, None, ALU.add), r=["u0"], w=["u0"])
        wrap(u0[:], "u0")
        P.op("act", lambda e: e.activation(cos_t[:], u0[:], AF.Sin), r=["u0"], w=["cos_t"])
        NH.rstd([(h[:, c, :], "h", 128) for c in range(KC)], D_MODEL, rs[:], "rs")
        for c in range(KC):
            P.op("dve", lambda e, c=c: e.scalar_tensor_tensor(h[:, c, :], h[:, c, :], vec[:, c:c + 1], rs[:], ALU.mult, ALU.mult),
                 r=["h", "vec", "rs"], w=["h"])
        for m in range(8):
            s = nxt("wb", 2)
            P.dma(wb[s][:], w_in[m], w=["wb%d" % s])
            pa = nxt("pa", 2)
            for c in range(KC):
                P.mm(pA[pa][:], wb[s][:, c, :], h[:, c, :], c == 0, c == KC - 1, r=["wb%d" % s, "h"], w=["pA%d" % pa])
            eng = "act" if m % 2 == 0 else "dve"
            if eng == "act":
                P.op("act", lambda e, m=m, pa=pa: e.copy(clat[:, m, :], pA[pa][:]), r=["pA%d" % pa], w=["clat"])
            else:
                P.op("dve", lambda e, m=m, pa=pa: e.tensor_copy(clat[:, m, :], pA[pa][:]), r=["pA%d" % pa], w=["clat"])
        P.dma(wpe[:], w_inpe[0], w=["wpe"])
        pb = nxt("pb", 2)
        for c in range(KC):
            P.mm(pB[pb][0:64, :], wpe[:, c, :], h[:, c, :], c == 0, c == KC - 1, r=["wpe", "h"], w=["pB%d" % pb])
        P.op("act", lambda e, pb=pb: e.copy(kpe[:], pB[pb][0:64, :]), r=["pB%d" % pb], w=["kpe"])
        for base, col in ((0, KC), (4, KC + 4)):
            NH.rstd([(clat[:, base + c, :], "clat", 128) for c in range(4)], 512, rs[:], "rs")
            for c in range(4):
                P.op("dve", lambda e, c=c, base=base, col=col: e.scalar_tensor_tensor(
                    clat[:, base + c, :], clat[:, base + c, :], vec[:, col + c:col + c + 1], rs[:], ALU.mult, ALU.mult),
                    r=["clat", "vec", "rs"], w=["clat"])
        for hd in range(H):
            s = nxt("wh", 2)
            for k, wd in (("qn", w_qn), ("qr", w_qr), ("kn", w_kn), ("v", w_v)):
                P.dma(wh[k][s][:], wd[hd], w=["w%s%d" % (k, s)])
            pa = nxt("pa", 2)
            pb = nxt("pb", 2)
            for c in range(4):
                P.mm(pA[pa][:], wh["qn"][s][:, c, :], clat[:, c, :], c == 0, c == 3, r=["wqn%d" % s, "clat"], w=["pA%d" % pa])
            for c in range(4):
                P.mm(pB[pb][0:64, :], wh["qr"][s][:, c, :], clat[:, c, :], c == 0, c == 3, r=["wqr%d" % s, "clat"], w=["pB%d" % pb])
            a = nxt("xn", 2)
            b = nxt("xr", 2)
            P.op("act", lambda e, a=a, pa=pa: e.copy(xn[a][:], pA[pa][:]), r=["pA%d" % pa], w=["xn%d" % a])
            P.op("dve", lambda e, b=b, pb=pb: e.tensor_copy(xr[b][:], pB[pb][0:64, :]), r=["pB%d" % pb], w=["xr%d" % b])
            NH.rstd([(xn[a][:], "xn%d" % a, 128), (xr[b][:], "xr%d" % b, 64)], MLA_QK, rs[:], "rs")
            o = nxt("on", 3)
            P.op("dve", lambda e, a=a, o=o: e.scalar_tensor_tensor(on[o][:], xn[a][:], vec[:, C_QN:C_QN + 1], rs[:], ALU.mult, ALU.mult),
                 r=["xn%d" % a, "vec", "rs"], w=["on%d" % o])
            P.dma(q_nope_o[hd][:, ts], on[o][:], r=["on%d" % o], is_output=True)
            P.op("dve", lambda e, b=b: e.scalar_tensor_tensor(xr[b][:], xr[b][:], vec[0:64, C_QR:C_QR + 1], rs[0:64, :], ALU.mult, ALU.mult),
                 r=["xr%d" % b, "vec", "rs"], w=["xr%d" % b])
            ro = nxt("or", 2)
            rope(xr[b][:], "xr%d" % b, orr[ro][:], "or%d" % ro)
            P.dma(q_rope_o[hd][:, ts], orr[ro][:], r=["or%d" % ro], is_output=True)
            pa = nxt("pa", 2)
            for c in range(4):
                P.mm(pA[pa][:], wh["kn"][s][:, c, :], clat[:, 4 + c, :], c == 0, c == 3, r=["wkn%d" % s, "clat"], w=["pA%d" % pa])
            a = nxt("xn", 2)
            P.op("act", lambda e, a=a, pa=pa: e.copy(xn[a][:], pA[pa][:]), r=["pA%d" % pa], w=["xn%d" % a])
            NH.rstd([(xn[a][:], "xn%d" % a, 128), (kpe[:], "kpe", 64)], MLA_QK, rs[:], "rs")
            o = nxt("on", 3)
            P.op("dve", lambda e, a=a, o=o: e.scalar_tensor_tensor(on[o][:], xn[a][:], vec[:, C_KN:C_KN + 1], rs[:], ALU.mult, ALU.mult),
                 r=["xn%d" % a, "vec", "rs"], w=["on%d" % o])
            P.dma(k_nope_o[hd][:, ts], on[o][:], r=["on%d" % o], is_output=True)
            b = nxt("xr", 2)
            P.op("dve", lambda e, b=b: e.scalar_tensor_tensor(xr[b][:], kpe[:], vec[0:64, C_KR:C_KR + 1], rs[0:64, :], ALU.mult, ALU.mult),
                 r=["kpe", "vec", "rs"], w=["xr%d" % b])
            ro = nxt("or", 2)
            rope(xr[b][:], "xr%d" % b, orr[ro][:], "or%d" % ro)
            P.dma(k_rope_o[hd][:, ts], orr[ro][:], r=["or%d" % ro], is_output=True)
            pa = nxt("pa", 2)
            for c in range(4):
                P.mm(pA[pa][:], wh["v"][s][:, c, :], clat[:, 4 + c, :], c == 0, c == 3, r=["wv%d" % s, "clat"], w=["pA%d" % pa])
            o = nxt("on", 3)
            P.op("act", lambda e, o=o, pa=pa: e.copy(on[o][:], pA[pa][:]), r=["pA%d" % pa], w=["on%d" % o])
            P.dma(v_o[hd][:, ts], on[o][:], r=["on%d" % o], is_output=True)
    return P.emit()


def mla_pre_inputs(hT, posb, lw):
    H = MLA_HEADS
    w_in = lw["w_in"]
    w_uq = lw["w_uq"].reshape(512, H, MLA_QK)
    w_ukv = lw["w_ukv"].reshape(512, H, 256)
    invf, rot = rope_consts()
    vecs = np.zeros((128, 28), np.float32)
    vecs[:, 0:16] = col_vec(lw["mixer_norm"])
    vecs[:, 16:20] = col_vec(lw["q_lat_norm"])
    vecs[:, 20:24] = col_vec(lw["kv_lat_norm"])
    vecs[:, 24] = lw["q_norm"][:128]
    vecs[:, 25] = lw["k_norm"][:128]
    vecs[0:64, 26] = lw["q_norm"][128:]
    vecs[0:64, 27] = lw["k_norm"][128:]
    return {
        "w_in": tile_w(w_in[:, :1024]), "w_inpe": tile_w(np.ascontiguousarray(w_in[:, 1024:1088]), 64),
        "w_qn": tile_w(np.ascontiguousarray(w_uq[:, :, :128]).reshape(512, H * 128)),
        "w_qr": tile_w(np.ascontiguousarray(w_uq[:, :, 128:]).reshape(512, H * 64), 64),
        "w_kn": tile_w(np.ascontiguousarray(w_ukv[:, :, :128]).reshape(512, H * 128)),
        "w_v": tile_w(np.ascontiguousarray(w_ukv[:, :, 128:]).reshape(512, H * 128)),
        "vecs": vecs, "invf": invf, "rot": rot,
    }


def attn_masks(T=512):
    m = np.zeros((T // 128, 128, T), np.float32)
    q = np.arange(T)[None, :]
    for kb in range(T // 128):
        k = (kb * 128 + np.arange(128))[:, None]
        m[kb] = (q >= k).astype(np.float32)
    return m


def build_attn(S, NP):
    T = 512
    NQ = S // T
    NB = T // 128
    scale = MLA_QK ** -0.5
    P = Prog()
    qn_d = P.dram_in("qn", [NP, 128, S])
    qr_d = P.dram_in("qr", [NP, 64, S])
    kn_d = P.dram_in("kn", [NP, 128, S])
    kr_d = P.dram_in("kr", [NP, 64, S])
    v_d = P.dram_in("v", [NP, 128, S // 128, 128])
    mk_d = P.dram_in("masks", [NB, 128, T])
    o_d = P.dram_out("oT", [NP, 128, S])
    qn = [P.sbuf("qn%d" % i, [128, T]) for i in range(2)]
    qr = [P.sbuf("qr%d" % i, [64, T]) for i in range(2)]
    kn = [P.sbuf("kn%d" % i, [128, T]) for i in range(3)]
    kr = [P.sbuf("kr%d" % i, [64, T]) for i in range(3)]
    vb = [P.sbuf("vb%d" % i, [128, NB, 128]) for i in range(3)]
    mk = P.sbuf("mk", [128, NB, T])
    ones = P.sbuf("ones", [128, 128])
    pt = [P.sbuf("pt%d" % i, [128, T]) for i in range(3)]
    rl = P.sbuf("rl", [128, T])
    pacc = [P.sbuf("pacc%d" % i, [128, T]) for i in range(2)]
    ot = [P.sbuf("ot%d" % i, [128, T]) for i in range(2)]
    sp = [P.psum("sp%d" % i, [128, T]) for i in range(2)]
    op_ = [P.psum("op%d" % i, [128, T]) for i in range(2)]
    lp = [P.psum("lp%d" % i, [128, T]) for i in range(2)]
    P.op("pool", lambda e: e.memset(ones[:], 1.0), w=["ones"])
    for kb in range(NB):
        P.dma(mk[:, kb, :], mk_d[kb], w=["mk"])
    cnt = {}

    def nxt(k, n):
        v = cnt.get(k, 0) % n
        cnt[k] = cnt.get(k, 0) + 1
        return v

    for pr in range(NP):
        for j in range(NQ):
            qs = nxt("q", 2)
            tsq = slice(j * T, (j + 1) * T)
            P.dma(qn[qs][:], qn_d[pr][:, tsq], w=["qn%d" % qs])
            P.dma(qr[qs][:], qr_d[pr][:, tsq], w=["qr%d" % qs])
            acc = nxt("acc", 2)
            for c in range(j + 1):
                ks = nxt("k", 3)
                tsk = slice(c * T, (c + 1) * T)
                P.dma(kn[ks][:], kn_d[pr][:, tsk], w=["kn%d" % ks])
                P.dma(kr[ks][:], kr_d[pr][:, tsk], w=["kr%d" % ks])
                P.dma(vb[ks][:], v_d[pr][:, c * NB:(c + 1) * NB, :], w=["vb%d" % ks])
                for kb in range(NB):
                    s = nxt("sp", 2)
                    ksl = slice(kb * 128, (kb + 1) * 128)
                    P.mm(sp[s][:], kn[ks][:, ksl], qn[qs][:], True, False, r=["kn%d" % ks, "qn%d" % qs], w=["sp%d" % s])
                    P.mm(sp[s][:], kr[ks][:, ksl], qr[qs][:], False, True, r=["kr%d" % ks, "qr%d" % qs], w=["sp%d" % s])
                    p = nxt("pt", 3)
                    P.op("act", lambda e, p=p, s=s: e.activation(pt[p][:], sp[s][:], AF.Exp, scale=scale), r=["sp%d" % s], w=["pt%d" % p])
                    if c == j:
                        P.op("pool", lambda e, p=p, kb=kb: e.tensor_tensor(pt[p][:], pt[p][:], mk[:, kb, :], ALU.mult),
                             r=["pt%d" % p, "mk"], w=["pt%d" % p])
                    first = (c == 0 and kb == 0)
                    last = (c == j and kb == NB - 1)
                    P.mm(op_[acc][:], vb[ks][:, kb, :], pt[p][:], first, last, r=["vb%d" % ks, "pt%d" % p], w=["op%d" % acc])
                    if first:
                        P.op("pool", lambda e, p=p, acc=acc: e.tensor_copy(pacc[acc][:], pt[p][:]), r=["pt%d" % p], w=["pacc%d" % acc])
                    else:
                        P.op("pool", lambda e, p=p, acc=acc: e.tensor_tensor(pacc[acc][:], pacc[acc][:], pt[p][:], ALU.add),
                             r=["pacc%d" % acc, "pt%d" % p], w=["pacc%d" % acc])
            P.mm(lp[acc][:], ones[:], pacc[acc][:], True, True, r=["ones", "pacc%d" % acc], w=["lp%d" % acc])
            P.op("dve", lambda e, acc=acc: e.reciprocal(rl[:], lp[acc][:]), r=["lp%d" % acc], w=["rl"])
            o = nxt("ot", 2)
            P.op("dve", lambda e, acc=acc, o=o: e.tensor_tensor(ot[o][:], op_[acc][:], rl[:], ALU.mult), r=["op%d" % acc, "rl"], w=["ot%d" % o])
            P.dma(o_d[pr][:, tsq], ot[o][:], r=["ot%d" % o], is_output=True)
    return P.emit()


class Rot:
    def __init__(self, P, name, shape, n, psum=False, dtype=F32):
        self.bufs = [(P.psum if psum else P.sbuf)("%s%d" % (name, i), shape, dtype) for i in range(n)]
        self.keys = ["%s%d" % (name, i) for i in range(n)]
        self.i = 0

    def next(self):
        k = self.i % len(self.bufs)
        self.i += 1
        return self.bufs[k], self.keys[k]


def build_gdn_pre(NT):
    T = 512 if NT >= 512 else NT
    NTILES = NT // T
    KC = 16
    NQK = 64
    P = Prog()
    hT = P.dram_in("hT", [D_MODEL, 3 + NT])
    w_in = P.dram_in("w_in", [96, 128, KC, 128])
    w_ba = P.dram_in("w_ba", [1, 128, KC, 64])
    vecs = P.dram_in("vecs", [128, KC])
    cw_d = P.dram_in("cw", [128, NQK * 4])
    hp_d = P.dram_in("hp", [64, 2])
    qT_o = P.dram_out("qT", [2048, NT])
    kT_o = P.dram_out("kT", [2048, NT])
    vT_o = P.dram_out("vT", [4096, NT])
    zT_o = P.dram_out("zT", [4096, NT])
    bg_o = P.dram_out("bg", [64, NT])

    h = P.sbuf("h", [128, KC, T])
    hh = P.sbuf("hh", [128, KC, 3])
    vec = P.sbuf("vec", [128, KC])
    cw = P.sbuf("cw_s", [128, NQK * 4])
    hp = P.sbuf("hp_s", [64, 2])
    nal = P.sbuf("nal", [64, 1])
    carry = P.sbuf("carry", [128, NQK, 3])
    rs = P.sbuf("rs", [128, T])
    rsh = P.sbuf("rsh", [128, 3])
    wb = Rot(P, "wb", [128, KC, 128], 2)
    wba = P.sbuf("wba", [128, KC, 64])
    xc = Rot(P, "xc", [128, T + 3], 2)
    yb = Rot(P, "yb", [128, T], 2)
    ob = Rot(P, "ob", [128, T], 3)
    sb64 = Rot(P, "sb64_", [64, T], 3)
    bgt = P.sbuf("bgt", [64, T])
    pA = Rot(P, "pA", [128, T], 3, psum=True)
    pH = P.psum("pH", [128, 512])
    NH = NormHelper(P, T)

    P.dma(vec[:], vecs[:, :], w=["vec"])
    P.dma(cw[:], cw_d[:, :], w=["cw_s"])
    P.dma(hp[:], hp_d[:, :], w=["hp_s"])
    P.op("act", lambda e: e.activation(nal[32:64, :], hp[32:64, 1:2], AF.Exp), r=["hp_s"], w=["nal"])
    P.op("dve", lambda e: e.tensor_scalar(nal[32:64, :], nal[32:64, :], -1.0, None, ALU.mult), r=["nal"], w=["nal"])

    P.dma(hh[:], hT[:, 0:3].rearrange("(c p) n -> p c n", p=128), w=["hh"])
    sqh = P.sbuf("sqh", [128, 3])
    for c in range(KC):
        P.op("dve", lambda e, c=c: e.tensor_tensor(sqh[:], hh[:, c, :], hh[:, c, :], ALU.mult), r=["hh"], w=["sqh"])
        P.mm(pH[:, 0:3], NH.ones[:], sqh[:], c == 0, c == KC - 1, r=["sqh", "ones"], w=["pH"])
    rsqrt_from(P, rsh[:], "rsh", pH[:, 0:3], "pH", D_MODEL, NORM_EPS)
    for c in range(KC):
        P.op("dve", lambda e, c=c: e.scalar_tensor_tensor(hh[:, c, :], hh[:, c, :], vec[:, c:c + 1], rsh[:], ALU.mult, ALU.mult),
             r=["hh", "vec", "rsh"], w=["hh"])
    for m in range(NQK):
        w, wk = wb.next()
        P.dma(w[:], w_in[m], w=[wk])
        for c in range(KC):
            P.mm(pH[:, 0:3], w[:, c, :], hh[:, c, :], c == 0, c == KC - 1, r=[wk, "hh"], w=["pH"])
        P.op("act", lambda e, m=m: e.copy(carry[:, m, :], pH[:, 0:3]), r=["pH"], w=["carry"])

    for t in range(NTILES):
        ts = slice(t * T, (t + 1) * T)
        P.dma(h[:], hT[:, 3 + t * T:3 + (t + 1) * T].rearrange("(c p) n -> p c n", p=128), w=["h"])
        NH.rstd([(h[:, c, :], "h", 128) for c in range(KC)], D_MODEL, rs[:], "rs")
        for c in range(KC):
            P.op("dve", lambda e, c=c: e.scalar_tensor_tensor(h[:, c, :], h[:, c, :], vec[:, c:c + 1], rs[:], ALU.mult, ALU.mult),
                 r=["h", "vec", "rs"], w=["h"])
        for m in range(96):
            w, wk = wb.next()
            P.dma(w[:], w_in[m], w=[wk])
            ps, pk = pA.next()
            for c in range(KC):
                P.mm(ps[:], w[:, c, :], h[:, c, :], c == 0, c == KC - 1, r=[wk, "h"], w=[pk])
            if m >= NQK:
                o, ok = ob.next()
                P.op("act", lambda e, o=o, ps=ps: e.copy(o[:], ps[:]), r=[pk], w=[ok])
                P.dma(zT_o[(m - NQK) * 128:(m - NQK + 1) * 128, ts], o[:], r=[ok], is_output=True)
                continue
            x, xk = xc.next()
            P.op("act", lambda e, x=x, ps=ps: e.copy(x[:, 3:3 + T], ps[:]), r=[pk], w=[xk])
            P.op("pool", lambda e, x=x, m=m: e.tensor_copy(x[:, 0:3], carry[:, m, :]), r=["carry"], w=[xk])
            P.op("pool", lambda e, x=x, m=m: e.tensor_copy(carry[:, m, :], x[:, T:T + 3]), r=[xk], w=["carry"])
            y, yk = yb.next()
            P.op("dve", lambda e, x=x, y=y, m=m: e.tensor_scalar(y[:], x[:, 0:T], cw[:, m * 4:m * 4 + 1], None, ALU.mult), r=[xk, "cw_s"], w=[yk])
            for j in range(1, 4):
                P.op("dve", lambda e, x=x, y=y, m=m, j=j: e.scalar_tensor_tensor(y[:], x[:, j:j + T], cw[:, m * 4 + j:m * 4 + j + 1], y[:], ALU.mult, ALU.add),
                     r=[xk, "cw_s", yk], w=[yk])
            o, ok = ob.next()
            P.op("act", lambda e, o=o, y=y: e.activation(o[:], y[:], AF.Silu), r=[yk], w=[ok])
            if m < 32:
                NH.rstd([(o[:], ok, 128)], 1.0, rs[:], "rs", eps=NORM_EPS, post_scale=(GDN_DK ** -0.5 if m < 16 else 1.0))
                P.op("dve", lambda e, o=o: e.tensor_tensor(o[:], o[:], rs[:], ALU.mult), r=[ok, "rs"], w=[ok])
                dst = qT_o if m < 16 else kT_o
                mm_ = m % 16
            else:
                dst = vT_o
                mm_ = m - 32
            P.dma(dst[mm_ * 128:(mm_ + 1) * 128, ts], o[:], r=[ok], is_output=True)
        P.dma(wba[:], w_ba[0], w=["wba"])
        ps, pk = pA.next()
        for c in range(KC):
            P.mm(ps[0:64, :], wba[:, c, :], h[:, c, :], c == 0, c == KC - 1, r=["wba", "h"], w=[pk])
        P.op("act", lambda e, ps=ps: e.activation(bgt[0:32, :], ps[0:32, :], AF.Sigmoid), r=[pk], w=["bgt"])
        x1, k1 = sb64.next()
        x2, k2 = sb64.next()
        x3, k3 = sb64.next()
        P.op("dve", lambda e, ps=ps, x1=x1: e.tensor_scalar(x1[32:64, :], ps[32:64, :], hp[32:64, 0:1], None, ALU.add), r=[pk, "hp_s"], w=[k1])
        P.op("act", lambda e, x1=x1, x2=x2: e.activation(x2[32:64, :], x1[32:64, :], AF.Abs), r=[k1], w=[k2])
        P.op("act", lambda e, x2=x2: e.activation(x2[32:64, :], x2[32:64, :], AF.Exp, scale=-1.0), r=[k2], w=[k2])
        P.op("act", lambda e, x2=x2, x3=x3: e.activation(x3[32:64, :], x2[32:64, :], AF.Ln, bias=1.0), r=[k2], w=[k3])
        P.op("dve", lambda e, x1=x1, x3=x3: e.scalar_tensor_tensor(x3[32:64, :], x1[32:64, :], 0.0, x3[32:64, :], ALU.max, ALU.add), r=[k1, k3], w=[k3])
        P.op("dve", lambda e, x3=x3: e.tensor_scalar(bgt[32:64, :], x3[32:64, :], nal[32:64, 0:1], None, ALU.mult), r=[k3, "nal"], w=["bgt"])
        P.dma(bg_o[:, ts], bgt[:], r=["bgt"], is_output=True)
    return P.emit()


def gdn_pre_inputs(lw):
    w_in = lw["w_in"]
    cw = lw["conv_w"]
    cwt = np.ascontiguousarray(cw.T.reshape(64, 128, 4).transpose(1, 0, 2).reshape(128, 256))
    hp = np.zeros((64, 2), np.float32)
    hp[32:, 0] = lw["dt_bias"]
    hp[32:, 1] = lw["a_log"]
    return {"w_in": tile_w(np.ascontiguousarray(w_in[:, :12288])), "w_ba": tile_w(np.ascontiguousarray(w_in[:, 12288:12352]), 64),
            "vecs": col_vec(lw["mixer_norm"]), "cw": cwt, "hp": hp}


def gdn_consts():
    p = np.arange(128)[:, None]
    f = np.arange(128)[None, :]
    triu = (f >= p).astype(np.float32)
    stril = (p > f).astype(np.float32)
    ident = np.eye(128, dtype=np.float32)
    return np.ascontiguousarray(np.stack([triu, stril, ident]))


def build_gdn_core(S, NHD):
    C = 128
    NCH = S // C
    GRP = 4 if NCH % 4 == 0 else 1
    NQ = (NHD + 1) // 2
    P = Prog()
    QT = P.dram_in("QT", [NQ, 128, S])
    KT = P.dram_in("KT", [NQ, 128, S])
    Ktm = P.dram_in("Ktm", [NQ, 128, NCH, 128])
    Vtm = P.dram_in("Vtm", [NHD, 128, NCH, 128])
    Gd = P.dram_in("G", [NHD, 128, NCH])
    Bd = P.dram_in("Bt", [NHD, 128, NCH])
    cst = P.dram_in("cst", [3, 128, 128])
    o_d = P.dram_out("o", [NHD, 128, NCH, 128])

    triu = P.sbuf("triu", [128, 128])
    stril = P.sbuf("stril", [128, 128])
    ident = P.sbuf("ident", [128, 128])
    ones = P.sbuf("ones", [128, 128])
    P.dma(triu[:], cst[0], w=["triu"])
    P.dma(stril[:], cst[1], w=["stril"])
    P.dma(ident[:], cst[2], w=["ident"])
    P.op("pool", lambda e: e.memset(ones[:], 1.0), w=["ones"])
    Gh = Rot(P, "Gh", [128, NCH], 2)
    Bh = Rot(P, "Bh", [128, NCH], 2)
    kt4 = Rot(P, "kt4_", [128, GRP * 128], 2)
    qt4 = Rot(P, "qt4_", [128, GRP * 128], 2)
    km4 = Rot(P, "km4_", [128, GRP, 128], 2)
    vm4 = Rot(P, "vm4_", [128, GRP, 128], 2)
    ob4 = Rot(P, "ob4_", [128, GRP, 128], 2)
    St = P.sbuf("St", [128, 128])
    sq = lambda name, n: Rot(P, name, [128, 128], n)
    gbc, decL, decT, L0r, N0r, intr, Pr, Lr, Nr = (sq("gbc", 2), sq("decL", 2), sq("decT", 2), sq("L0r", 2), sq("N0r", 2),
                                                   sq("intr", 3), sq("Pr", 3), sq("Lr", 3), sq("Nr", 3))
    t12 = sq("t12_", 3)
    Vb, Kbg, Kdec, ub, wTb, vnew, o1 = sq("Vb", 2), sq("Kbg", 2), sq("Kdec", 3), sq("ub", 3), sq("wTb", 3), sq("vnew", 2), sq("o1_", 2)
    scl = Rot(P, "scl", [128, 8], 3)
    banks = [P.psum("bk%d" % i, [128, 512]) for i in range(8)]

    class PsRot:
        def __init__(self, idxs):
            self.idxs = idxs
            self.i = 0

        def next(self):
            k = self.idxs[self.i % len(self.idxs)]
            self.i += 1
            return banks[k], "bk%d" % k

    psA = PsRot([0, 1, 2, 3, 4])
    psB = PsRot([5, 6, 7])

    def copy_to(eng, dst, dk, src, sk):
        if eng == "act":
            P.op("act", lambda e: e.copy(dst, src), r=[sk], w=[dk])
        else:
            P.op("dve", lambda e: e.tensor_copy(dst, src), r=[sk], w=[dk])

    for hd in range(NHD):
        qk = hd // 2
        G_, Gk = Gh.next()
        B_, Bk = Bh.next()
        P.dma(G_[:], Gd[hd], w=[Gk])
        P.dma(B_[:], Bd[hd], w=[Bk])
        P.op("pool", lambda e: e.memset(St[:], 0.0), w=["St"])
        grp = {}

        def load_group(gi):
            k4, k4k = kt4.next()
            q4, q4k = qt4.next()
            m4, m4k = km4.next()
            v4, v4k = vm4.next()
            sl = slice(gi * GRP * 128, (gi + 1) * GRP * 128)
            P.dma(k4[:], KT[qk][:, sl], w=[k4k])
            P.dma(q4[:], QT[qk][:, sl], w=[q4k])
            P.dma(m4[:], Ktm[qk][:, gi * GRP:(gi + 1) * GRP, :], w=[m4k])
            P.dma(v4[:], Vtm[hd][:, gi * GRP:(gi + 1) * GRP, :], w=[v4k])
            grp[gi] = (k4, k4k, q4, q4k, m4, m4k, v4, v4k)

        def pre(n):
            gi, li = n // GRP, n % GRP
            if li == 0:
                load_group(gi)
            k4, k4k, q4, q4k, m4, m4k, v4, v4k = grp[gi]
            ktc = k4[:, li * 128:(li + 1) * 128]
            qtc = q4[:, li * 128:(li + 1) * 128]
            ktm = m4[:, li, :]
            vtm = v4[:, li, :]
            gcol = G_[:, n:n + 1]
            bcol = B_[:, n:n + 1]
            gb, gbk = gbc.next()
            P.op("dve", lambda e: e.tensor_scalar(gb[:], ones[:], gcol, None, ALU.mult), r=["ones", Gk], w=[gbk])
            pss, pssk = psA.next()
            P.mm(pss[:, 0:128], triu[:], gb[:], True, True, r=["triu", gbk], w=[pssk])
            P.mm(pss[:, 128:256], ones[:], gb[:], True, True, r=["ones", gbk], w=[pssk])
            psb_, psbk = psA.next()
            psb = psb_[:, 0:128]
            P.mm(psb, gb[:], triu[:], True, True, r=[gbk, "triu"], w=[psbk])
            sc, sck = scl.next()
            P.op("act", lambda e: e.copy(sc[:, 0:1], pss[:, 0:1]), r=[pssk], w=[sck])
            P.op("act", lambda e: e.copy(sc[:, 1:2], pss[:, 128:129]), r=[pssk], w=[sck])
            P.op("act", lambda e: e.activation(sc[:, 2:4], sc[:, 0:2], AF.Exp), r=[sck], w=[sck])
            P.op("dve", lambda e: e.tensor_tensor(sc[:, 4:5], sc[:, 1:2], sc[:, 0:1], ALU.subtract), r=[sck], w=[sck])
            P.op("act", lambda e: e.activation(sc[:, 4:5], sc[:, 4:5], AF.Exp), r=[sck], w=[sck])
            P.op("dve", lambda e: e.tensor_tensor(sc[:, 5:6], sc[:, 2:3], bcol, ALU.mult), r=[sck, Bk], w=[sck])
            t1, t1k = t12.next()
            dl, dlk = decL.next()
            P.op("dve", lambda e: e.tensor_scalar(t1[:], psb, sc[:, 0:1], 0.0, ALU.subtract, ALU.max), r=[psbk, sck], w=[t1k])
            P.op("act", lambda e: e.activation(dl[:], t1[:], AF.Exp, scale=-1.0), r=[t1k], w=[dlk])
            P.op("pool", lambda e: e.tensor_tensor(dl[:], dl[:], stril[:], ALU.mult), r=[dlk, "stril"], w=[dlk])
            t2, t2k = t12.next()
            dt_, dtk = decT.next()
            P.op("dve", lambda e: e.tensor_scalar(t2[:], psb, sc[:, 0:1], 0.0, ALU.subtract, ALU.min), r=[psbk, sck], w=[t2k])
            P.op("act", lambda e: e.activation(dt_[:], t2[:], AF.Exp), r=[t2k], w=[dtk])
            P.op("pool", lambda e: e.tensor_tensor(dt_[:], dt_[:], triu[:], ALU.mult), r=[dtk, "triu"], w=[dtk])
            psc, psck = psA.next()
            psc = psc[:, 0:128]
            P.mm(psc, ktc, ktc, True, True, r=[k4k], w=[psck])
            L0, L0k = L0r.next()
            P.op("dve", lambda e: e.scalar_tensor_tensor(L0[:], psc, bcol, dl[:], ALU.mult, ALU.mult), r=[psck, Bk, dlk], w=[L0k])
            psd, psdk = psA.next()
            psd = psd[:, 0:128]
            P.mm(psd, L0[:], ident[:], True, True, r=[L0k, "ident"], w=[psdk])
            N0, N0k = N0r.next()
            copy_to("act", N0[:], N0k, psd, psdk)
            pse, psek = psA.next()
            pse = pse[:, 0:128]
            P.mm(pse, ktc, qtc, True, True, r=[k4k, q4k], w=[psek])
            it, itk = intr.next()
            P.op("dve", lambda e: e.tensor_tensor(it[:], pse, dt_[:], ALU.mult), r=[psek, dtk], w=[itk])
            Pc, Pck = Pr.next()
            P.op("pool", lambda e, Pc=Pc: e.tensor_tensor(Pc[:], ident[:], N0[:], ALU.subtract), r=["ident", N0k], w=[Pck])
            Lp, Lpk, Np, Npk = L0, L0k, N0, N0k
            for lev in range(1, 7):
                psl, pslk = psA.next()
                psl = psl[:, 0:128]
                P.mm(psl, Np[:], Lp[:], True, True, r=[Npk, Lpk], w=[pslk])
                if lev < 6:
                    psn, psnk = psA.next()
                    psn = psn[:, 0:128]
                    P.mm(psn, Lp[:], Np[:], True, True, r=[Npk, Lpk], w=[psnk])
                Ln, Lnk = Lr.next()
                copy_to("dve" if lev % 2 else "act", Ln[:], Lnk, psl, pslk)
                if lev < 6:
                    Nn, Nnk = Nr.next()
                    copy_to("act" if lev % 2 else "dve", Nn[:], Nnk, psn, psnk)
                psp, pspk = psA.next()
                psp = psp[:, 0:128]
                P.mm(psp, Ln[:], Pc[:], True, True, r=[Lnk, Pck], w=[pspk])
                Pn, Pnk = Pr.next()
                P.op("dve", lambda e, Pn=Pn, Pc=Pc, psp=psp: e.tensor_tensor(Pn[:], Pc[:], psp, ALU.add), r=[Pck, pspk], w=[Pnk])
                Pc, Pck = Pn, Pnk
                Lp, Lpk = Ln, Lnk
                if lev < 6:
                    Np, Npk = Nn, Nnk
            vb_, vbk = Vb.next()
            kb_, kbk = Kbg.next()
            kd_, kdk = Kdec.next()
            P.op("act", lambda e: e.mul(vb_[:], vtm, bcol), r=[v4k, Bk], w=[vbk])
            P.op("act", lambda e: e.mul(kb_[:], ktm, sc[:, 5:6]), r=[m4k, sck], w=[kbk])
            P.op("pool", lambda e: e.tensor_scalar(kd_[:], ktm, sc[:, 4:5], None, ALU.mult), r=[m4k, sck], w=[kdk])
            psu, psuk = psA.next()
            psu = psu[:, 0:128]
            P.mm(psu, Pc[:], vb_[:], True, True, r=[Pck, vbk], w=[psuk])
            u_, uk = ub.next()
            copy_to("act", u_[:], uk, psu, psuk)
            psw, pswk = psA.next()
            psw = psw[:, 0:128]
            P.mm(psw, kb_[:], Pc[:], True, True, r=[kbk, Pck], w=[pswk])
            w_, wk = wTb.next()
            copy_to("dve", w_[:], wk, psw, pswk)
            return dict(u=u_, uk=uk, w=w_, wk=wk, it=it, itk=itk, kd=kd_, kdk=kdk, qtc=qtc, q4k=q4k, sc=sc, sck=sck)

        obuf = [None]

        def seq(n, d):
            gi, li = n // GRP, n % GRP
            if li == 0:
                obuf[0] = ob4.next()
            ob_, obk = obuf[0]
            sc = d["sc"]
            ps1, ps1k = psB.next()
            ps1 = ps1[:, 0:128]
            P.mm(ps1, d["w"][:], St[:], True, True, r=[d["wk"], "St"], w=[ps1k])
            vn, vnk = vnew.next()
            P.op("dve", lambda e: e.tensor_tensor(vn[:], d["u"][:], ps1, ALU.subtract), r=[d["uk"], ps1k], w=[vnk])
            ps2, ps2k = psB.next()
            ps2 = ps2[:, 0:128]
            P.mm(ps2, d["qtc"], St[:], True, True, r=[d["q4k"], "St"], w=[ps2k])
            o1_, o1k = o1.next()
            P.op("act", lambda e: e.mul(o1_[:], ps2, sc[:, 2:3]), r=[ps2k, d["sck"]], w=[o1k])
            ps3, ps3k = psB.next()
            ps3 = ps3[:, 0:128]
            P.mm(ps3, d["it"][:], vn[:], True, True, r=[d["itk"], vnk], w=[ps3k])
            P.op("dve", lambda e: e.tensor_tensor(ob_[:, li, :], o1_[:], ps3, ALU.add), r=[o1k, ps3k], w=[obk])
            ps4, ps4k = psB.next()
            ps4 = ps4[:, 0:128]
            P.mm(ps4, d["kd"][:], vn[:], True, True, r=[d["kdk"], vnk], w=[ps4k])
            P.op("dve", lambda e: e.scalar_tensor_tensor(St[:], St[:], sc[:, 3:4], ps4, ALU.mult, ALU.add), r=["St", d["sck"], ps4k], w=["St"])
            if li == GRP - 1:
                P.dma(o_d[hd][:, gi * GRP:(gi + 1) * GRP, :], ob_[:], r=[obk], is_output=True)

        prev = pre(0)
        for n in range(NCH):
            nxt_ = pre(n + 1) if n + 1 < NCH else None
            seq(n, prev)
            prev = nxt_
    return P.emit()


_PROGS = {}


def _prog(key, builder, *args):
    k = (key,) + tuple(args)
    if k not in _PROGS:
        _PROGS[k] = builder(*args)
    return _PROGS[k]


def _run(nc, in_maps):
    import sys
    import time
    t0 = time.time()
    res = run_bass_kernel_spmd(nc, in_maps, core_ids=list(range(NCORES)))
    print("[kernel] launch %.1fs" % (time.time() - t0), file=sys.stderr, flush=True)
    return res.results


def _c(a):
    return np.ascontiguousarray(a, dtype=np.float32)


def kernel(x, p, positions, mixer_norm,
           mla_w_in, mla_q_lat_norm, mla_kv_lat_norm, mla_w_uq, mla_w_ukv, mla_q_norm, mla_k_norm, mla_w_o,
           gdn_w_in, gdn_conv_w, gdn_a_log, gdn_dt_bias, gdn_out_norm, gdn_w_out,
           ffn_norm, ffn_w_gate_up, ffn_w_down,
           ple_w_proj, ple_norm, ple_gate_norm, ple_w_gate):
    x = np.asarray(x)
    p = np.asarray(p)
    positions = np.asarray(positions)
    B, S, D = x.shape
    CB = NCORES // B
    NT = S // CB
    NCH = S // 128
    cb = lambda c: (c // CB, slice((c % CB) * NT, (c % CB + 1) * NT))
    hT = []
    posb = []
    for c in range(NCORES):
        b, tok = cb(c)
        hT.append(_c(x[b, tok].T))
        posb.append(np.ascontiguousarray(np.broadcast_to(positions[b, tok][None, :], (64, NT)).astype(np.int32)))
    masks = attn_masks()
    gconst = gdn_consts()
    depth = mixer_norm.shape[0]
    for i in range(depth):
        j = i // 2
        if i % 2 == 0:
            lw = {"mixer_norm": np.asarray(mixer_norm[i]), "w_in": np.asarray(mla_w_in[j]), "q_lat_norm": np.asarray(mla_q_lat_norm[j]),
                  "kv_lat_norm": np.asarray(mla_kv_lat_norm[j]), "w_uq": np.asarray(mla_w_uq[j]), "w_ukv": np.asarray(mla_w_ukv[j]),
                  "q_norm": np.asarray(mla_q_norm[j]), "k_norm": np.asarray(mla_k_norm[j])}
            com = mla_pre_inputs(None, None, lw)
            res = _run(_prog("mla_pre", build_mla_pre, NT), [dict(com, hT=hT[c], posb=posb[c]) for c in range(NCORES)])
            H = MLA_HEADS
            full = {k: np.empty((B, H, d, S), np.float32) for k, d in (("q_nope", 128), ("q_rope", 64), ("k_nope", 128), ("k_rope", 64), ("vT", 128))}
            for c in range(NCORES):
                b, tok = cb(c)
                for k in full:
                    full[k][b][:, :, tok] = res[c][k]
            del res
            flat = {k: v.reshape(B * H, v.shape[2], S) for k, v in full.items()}
            vtm = _c(flat["vT"].transpose(0, 2, 1).reshape(B * H, NCH, 128, 128).transpose(0, 2, 1, 3))
            NP = B * H // NCORES
            ims = []
            for c in range(NCORES):
                sl = slice(c * NP, (c + 1) * NP)
                ims.append({"qn": _c(flat["q_nope"][sl]), "qr": _c(flat["q_rope"][sl]), "kn": _c(flat["k_nope"][sl]),
                            "kr": _c(flat["k_rope"][sl]), "v": _c(vtm[sl]), "masks": masks})
            res = _run(_prog("attn", build_attn, S, NP), ims)
            del ims, full, flat, vtm
            ofull = np.concatenate([res[c]["oT"] for c in range(NCORES)], axis=0).reshape(B, H * 128, S)
            del res
            oT = [_c(ofull[cb(c)[0]][:, cb(c)[1]]) for c in range(NCORES)]
            w_o = tile_w(np.asarray(mla_w_o[j]))
            DM = H * 128
            extra = [{} for _ in range(NCORES)]
            gdn = False
        else:
            lw = {"mixer_norm": np.asarray(mixer_norm[i]), "w_in": np.asarray(gdn_w_in[j]), "conv_w": np.asarray(gdn_conv_w[j]),
                  "a_log": np.asarray(gdn_a_log[j]), "dt_bias": np.asarray(gdn_dt_bias[j])}
            com = gdn_pre_inputs(lw)
            ims = []
            for c in range(NCORES):
                he = np.zeros((D, 3 + NT), np.float32)
                he[:, 3:] = hT[c]
                if c % CB != 0:
                    he[:, 0:3] = hT[c - 1][:, NT - 3:]
                ims.append(dict(com, hT=he))
            res = _run(_prog("gdn_pre", build_gdn_pre, NT), ims)
            del ims
            full = {k: np.empty((B, d, S), np.float32) for k, d in (("qT", 2048), ("kT", 2048), ("vT", 4096), ("bg", 64))}
            zT = []
            for c in range(NCORES):
                b, tok = cb(c)
                for k in full:
                    full[k][b][:, tok] = res[c][k]
                zT.append(_c(res[c]["zT"]))
            del res
            NHD = B * GDN_V_HEADS // NCORES
            NQ = NHD // 2
            tmaj = lambda a: _c(a.transpose(0, 2, 1).reshape(a.shape[0], NCH, 128, a.shape[1]).transpose(0, 2, 1, 3))
            ims = []
            for c in range(NCORES):
                b = c // CB
                v0 = (c % CB) * NHD
                q0 = v0 // 2
                QT = full["qT"][b].reshape(16, 128, S)[q0:q0 + NQ]
                KT = full["kT"][b].reshape(16, 128, S)[q0:q0 + NQ]
                VT = full["vT"][b].reshape(32, 128, S)[v0:v0 + NHD]
                Bt = full["bg"][b][v0:v0 + NHD]
                G = full["bg"][b][32 + v0:32 + v0 + NHD]
                ims.append({"QT": _c(QT), "KT": _c(KT), "Ktm": tmaj(KT), "Vtm": tmaj(VT),
                            "G": _c(G.reshape(NHD, NCH, 128).transpose(0, 2, 1)), "Bt": _c(Bt.reshape(NHD, NCH, 128).transpose(0, 2, 1)),
                            "cst": gconst})
            res = _run(_prog("gdn_core", build_gdn_core, S, NHD), ims)
            del ims, full
            ofull = np.empty((B, 32, 128, S), np.float32)
            for c in range(NCORES):
                b = c // CB
                v0 = (c % CB) * NHD
                o = res[c]["o"]
                ofull[b, v0:v0 + NHD] = o.transpose(0, 3, 2, 1).reshape(NHD, 128, S)
            del res
            ofull = ofull.reshape(B, 4096, S)
            oT = [_c(ofull[cb(c)[0]][:, cb(c)[1]]) for c in range(NCORES)]
            w_o = tile_w(np.asarray(gdn_w_out[j]))
            DM = 4096
            onorm = _c(np.asarray(gdn_out_norm[j]).reshape(128, 1))
            extra = [{"zT": zT[c], "onorm": onorm} for c in range(NCORES)]
            gdn = True
        del ofull
        com = {"w_o": w_o, "w_gu": tile_w(np.asarray(ffn_w_gate_up[i])), "w_dn": tile_w(np.asarray(ffn_w_down[i])),
               "w_pp": tile_w(np.asarray(ple_w_proj[i])), "w_pg": tile_w(np.asarray(ple_w_gate[i])),
               "vecs": _c(np.concatenate([col_vec(np.asarray(ffn_norm[i])), col_vec(np.asarray(ple_norm[i])),
                                          col_vec(np.asarray(ple_gate_norm[i]))], axis=1))}
        ims = []
        for c in range(NCORES):
            b, tok = cb(c)
            ims.append(dict(com, hT=hT[c], oT=oT[c], pT=_c(p[i][b, tok].T), **extra[c]))
        res = _run(_prog("post", build_post, NT, DM, gdn), ims)
        del ims, oT, com
        hT = [res[c]["hT_out"] for c in range(NCORES)]
        del res
    out = np.empty((B, S, D), np.float32)
    for c in range(NCORES):
        b, tok = cb(c)
        out[b, tok] = hT[c].T
    return out
```

```python
import math
from contextlib import ExitStack

import numpy as np
import concourse.bass as bass
import concourse.mybir as mybir
from concourse.bass_utils import run_bass_kernel_spmd

F32 = mybir.dt.float32
I32 = mybir.dt.int32
AF = mybir.ActivationFunctionType
ALU = mybir.AluOpType

D_MODEL = 2048
DEPTH = 4
NORM_EPS = 1e-6
PLE_DIM = 256
MLA_HEADS = 16
MLA_Q_LORA = 512
MLA_KV_LORA = 512
MLA_NOPE = 128
MLA_ROPE = 64
MLA_QK = 192
MLA_V = 128
ROPE_THETA = 10000.0
GDN_QK_HEADS = 16
GDN_V_HEADS = 32
GDN_DK = 128
GDN_DV = 128
GDN_CONV = 4
GDN_KEY_DIM = 2048
GDN_VAL_DIM = 4096
GDN_CONV_DIM = 8192
D_FF = 5632
NCORES = 8
TWO_PI_HI = 6.28125
TWO_PI_LO = 2.0 * math.pi - 6.28125
PI_SAFE = 3.1415925

SAME_ENGINE_SYNC = True
N_DMA_SEMS = 24


class Prog:
    ENGS = ("pe", "act", "dve", "pool", "sp")

    def __init__(self):
        self.nc = bass.Bass("TRN2", target_bir_lowering=False)
        self.stack = ExitStack()
        self.ops = {e: [] for e in self.ENGS}
        self.last_w = {}
        self.readers = {}
        self.dma_uses = [0] * N_DMA_SEMS
        self.dma_rr = 0
        self.out_dmas = []
        self.n_ops = 0

    def dram_in(self, name, shape, dtype=F32):
        return self.nc.dram_tensor(name, list(shape), dtype, kind="ExternalInput").ap()

    def dram_out(self, name, shape, dtype=F32):
        return self.nc.dram_tensor(name, list(shape), dtype, kind="ExternalOutput").ap()

    def sbuf(self, name, shape, dtype=F32):
        return self.stack.enter_context(self.nc.sbuf_tensor(name, list(shape), dtype))

    def psum(self, name, shape, dtype=F32):
        return self.stack.enter_context(self.nc.psum_tensor(name, list(shape), dtype))

    def _deps(self, eng, r, w):
        deps = set()
        for k in r:
            if k in self.last_w:
                deps.add(self.last_w[k])
        for k in w:
            if k in self.last_w:
                deps.add(self.last_w[k])
            for d in self.readers.get(k, ()):
                deps.add(d)
        out = []
        for d in deps:
            if d[0] == "eng" and d[1] == eng:
                if eng in ("pe", "sp") or not SAME_ENGINE_SYNC:
                    continue
            out.append(d)
        return out

    def _mark(self, tok, r, w):
        for k in w:
            self.last_w[k] = tok
            self.readers[k] = []
        for k in r:
            self.readers.setdefault(k, []).append(tok)

    def op(self, eng, fn, r=(), w=()):
        deps = self._deps(eng, r, w)
        idx = len(self.ops[eng])
        for d in deps:
            if d[0] == "eng":
                self.ops[d[1]][d[2]]["inc"] = True
        self.ops[eng].append({"kind": "c", "fn": fn, "deps": deps, "inc": False})
        self._mark(("eng", eng, idx), r, w)
        self.n_ops += 1

    def dma(self, out, in_, r=(), w=(), eng="sp", is_output=False):
        deps = self._deps(eng, r, w)
        for d in deps:
            if d[0] == "eng":
                self.ops[d[1]][d[2]]["inc"] = True
        k = self.dma_rr
        self.dma_rr = (self.dma_rr + 1) % N_DMA_SEMS
        use = self.dma_uses[k]
        self.dma_uses[k] += 1
        if use > 0:
            deps.append(("dma", k, 16 * use))
        tok = ("dma", k, 16 * (use + 1))
        self.ops[eng].append({"kind": "d", "out": out, "in": in_, "deps": deps, "sem": k, "inc": False})
        self._mark(tok, r, w)
        if is_output:
            self.out_dmas.append(tok)
        self.n_ops += 1

    def mm(self, out, lhsT, rhs, start, stop, r=(), w=()):
        self.op("pe", lambda e: e.matmul(out, lhsT, rhs, start=start, stop=stop), r=r, w=w)

    def emit(self):
        nc = self.nc
        st = self.stack
        esem = {e: st.enter_context(nc.semaphore("s_" + e)) for e in self.ENGS}
        dsem = [st.enter_context(nc.semaphore("d%d" % i)) for i in range(N_DMA_SEMS)]
        vals = {}
        for e in self.ENGS:
            c = 0
            for i, o in enumerate(self.ops[e]):
                if o["kind"] == "c" and o["inc"]:
                    c += 1
                    vals[(e, i)] = c
        final = list(self.out_dmas)
        ops = self.ops

        def run(ename, eng):
            waited = {}

            def do_wait(d):
                if d[0] == "eng":
                    key, v, sem = ("e", d[1]), vals[(d[1], d[2])], esem[d[1]]
                else:
                    key, v, sem = ("d", d[1]), d[2], dsem[d[1]]
                if waited.get(key, 0) >= v:
                    return
                waited[key] = v
                eng.wait_ge(sem, v)

            for i, o in enumerate(ops[ename]):
                for d in o["deps"]:
                    do_wait(d)
                if o["kind"] == "c":
                    ins = o["fn"](eng)
                    if o["inc"]:
                        ins.then_inc(esem[ename], 1)
                else:
                    eng.dma_start(out=o["out"], in_=o["in"]).then_inc(dsem[o["sem"]], 16)
            if ename == "sp":
                for d in final:
                    do_wait(d)

        with nc.Block() as block:
            @block.sync
            def _(sync):
                run("sp", sync)

            @block.tensor
            def _(pe):
                run("pe", pe)

            @block.scalar
            def _(act):
                run("act", act)

            @block.vector
            def _(dve):
                run("dve", dve)

            @block.gpsimd
            def _(pool):
                run("pool", pool)
        st.close()
        return nc


class RR:
    def __init__(self, items):
        self.items = list(items)
        self.i = 0

    def __call__(self):
        v = self.items[self.i % len(self.items)]
        self.i += 1
        return v


def tile_w(W, mc=128):
    K, M = W.shape
    return np.ascontiguousarray(W.reshape(K // 128, 128, M // mc, mc).transpose(2, 1, 0, 3))


def col_vec(v):
    return np.ascontiguousarray(v.reshape(-1, 128).T)


def rsqrt_from(P, out, out_key, src, src_key, D, eps, post_scale=1.0):
    s2 = post_scale * post_scale
    P.op("act", lambda e: e.activation(out, src, AF.Sqrt, bias=eps / s2, scale=1.0 / (D * s2)), r=[src_key], w=[out_key])
    P.op("dve", lambda e: e.reciprocal(out, out), r=[out_key], w=[out_key])


class NormHelper:
    def __init__(self, P, T):
        self.P = P
        self.T = T
        self.ones = P.sbuf("ones", [128, 128])
        P.op("pool", lambda e: e.memset(self.ones[:], 1.0), w=["ones"])
        self.sq = [P.sbuf("nsq%d" % i, [128, T]) for i in range(2)]
        self.ps = P.psum("nps", [128, T])
        self.cnt = 0

    def rstd(self, chunks, D, out, out_key, eps=NORM_EPS, post_scale=1.0):
        P = self.P
        n = len(chunks)
        for i, (ap, key, kp) in enumerate(chunks):
            s = self.cnt % 2
            self.cnt += 1
            sq = self.sq[s]
            sk = "nsq%d" % s
            eng = "act" if i % 2 == 0 else "pool"
            if eng == "act":
                P.op("act", lambda e, sq=sq, ap=ap, kp=kp: e.activation(sq[0:kp, :], ap, AF.Square), r=[key], w=[sk])
            else:
                P.op("pool", lambda e, sq=sq, ap=ap, kp=kp: e.tensor_tensor(sq[0:kp, :], ap, ap, ALU.mult), r=[key], w=[sk])
            P.mm(self.ps[:], self.ones[0:kp, :], sq[0:kp, :], start=(i == 0), stop=(i == n - 1), r=[sk, "ones"], w=["nps"])
        rsqrt_from(P, out, out_key, self.ps[:], "nps", D, eps, post_scale)


def build_post(NT, DM, gdn):
    T = 512 if NT >= 512 else NT
    NTILES = NT // T
    KC = D_MODEL // 128
    MC = DM // 128
    FC = D_FF // 128
    FH = 11
    NR = FC // FH
    P = Prog()
    hT = P.dram_in("hT", [D_MODEL, NT])
    oT = P.dram_in("oT", [DM, NT])
    w_o = P.dram_in("w_o", [KC, 128, MC, 128])
    w_gu = P.dram_in("w_gu", [2 * FC, 128, KC, 128])
    w_dn = P.dram_in("w_dn", [KC, 128, FC, 128])
    w_pp = P.dram_in("w_pp", [KC, 128, 2, 128])
    w_pg = P.dram_in("w_pg", [KC, 128, KC, 128])
    vecs = P.dram_in("vecs", [128, 3 * KC])
    pT = P.dram_in("pT", [PLE_DIM, NT])
    if gdn:
        zT = P.dram_in("zT", [DM, NT])
        onorm = P.dram_in("onorm", [128, 1])
    hout = P.dram_out("hT_out", [D_MODEL, NT])

    h = P.sbuf("h", [128, KC, T])
    hn = P.sbuf("hn", [128, KC, T])
    act = P.sbuf("actb", [128, FH, T])
    wg_b = [P.sbuf("wg%d" % i, [128, KC, 128]) for i in range(2)]
    wu_b = [P.sbuf("wu%d" % i, [128, KC, 128]) for i in range(2)]
    wd_b = [P.sbuf("wd%d" % i, [128, FH, 128]) for i in range(2)]
    wpp_b = [P.sbuf("wpp%d" % i, [128, 2, 128]) for i in range(2)]
    vec = P.sbuf("vec", [128, 3 * KC])
    rs = P.sbuf("rs", [128, T])
    rs2 = P.sbuf("rs2", [128, T])
    sg = [P.sbuf("sg%d" % i, [128, T]) for i in range(2)]
    ptile = P.sbuf("ptile", [128, 2, T])
    tmp = [P.sbuf("tmp%d" % i, [128, T]) for i in range(2)]
    if gdn:
        zb = [P.sbuf("zb%d" % i, [128, T]) for i in range(2)]
        on_sb = P.sbuf("on_sb", [128, 1])
    pA = [P.psum("pA%d" % i, [128, T]) for i in range(2)]
    pB = [P.psum("pB%d" % i, [128, T]) for i in range(2)]
    NH = NormHelper(P, T)

    P.dma(vec[:], vecs[:, :], w=["vec"])
    if gdn:
        P.dma(on_sb[:], onorm[:, :], w=["on_sb"])
    cnt = {"wo": 0, "wg": 0, "wd": 0, "ob": 0, "pa": 0, "pb": 0, "sg": 0, "wpp": 0, "tmp": 0, "zb": 0}

    def nxt(k, n):
        v = cnt[k] % n
        cnt[k] += 1
        return v

    for t in range(NTILES):
        ts = slice(t * T, (t + 1) * T)
        P.dma(h[:], hT[:, ts].rearrange("(c p) n -> p c n", p=128), w=["h"])
        for half in range(MC // KC):
            P.dma(hn[:], oT[half * D_MODEL:(half + 1) * D_MODEL, ts].rearrange("(c p) n -> p c n", p=128), w=["hn"])
            if gdn:
                for c in range(KC):
                    zs = nxt("zb", 2)
                    gc_ = half * KC + c
                    P.dma(zb[zs][:], zT[gc_ * 128:(gc_ + 1) * 128, ts], w=["zb%d" % zs])
                    NH.rstd([(hn[:, c, :], "hn", 128)], 128, rs[:], "rs")
                    P.op("act", lambda e, zs=zs: e.activation(zb[zs][:], zb[zs][:], AF.Silu), r=["zb%d" % zs], w=["zb%d" % zs])
                    P.op("dve", lambda e, c=c: e.scalar_tensor_tensor(hn[:, c, :], hn[:, c, :], on_sb[:, 0:1], rs[:], ALU.mult, ALU.mult),
                         r=["hn", "on_sb", "rs"], w=["hn"])
                    P.op("pool", lambda e, c=c, zs=zs: e.tensor_tensor(hn[:, c, :], hn[:, c, :], zb[zs][:], ALU.mult),
                         r=["hn", "zb%d" % zs], w=["hn"])
            for d in range(KC):
                s = nxt("wg", 2)
                P.dma(wg_b[s][:], w_o[d][:, half * KC:(half + 1) * KC, :], w=["wg%d" % s])
                pa = nxt("pa", 2)
                for c in range(KC):
                    P.mm(pA[pa][:], wg_b[s][:, c, :], hn[:, c, :], start=(c == 0), stop=(c == KC - 1),
                         r=["wg%d" % s, "hn"], w=["pA%d" % pa])
                P.op("dve", lambda e, d=d, pa=pa: e.tensor_tensor(h[:, d, :], h[:, d, :], pA[pa][:], ALU.add),
                     r=["h", "pA%d" % pa], w=["h"])
        NH.rstd([(h[:, c, :], "h", 128) for c in range(KC)], D_MODEL, rs[:], "rs")
        for c in range(KC):
            P.op("dve", lambda e, c=c: e.scalar_tensor_tensor(hn[:, c, :], h[:, c, :], vec[:, c:c + 1], rs[:], ALU.mult, ALU.mult),
                 r=["h", "vec", "rs"], w=["hn"])
        for rd in range(NR):
            for fi in range(FH):
                f = rd * FH + fi
                s = nxt("wg", 2)
                P.dma(wg_b[s][:], w_gu[f], w=["wg%d" % s])
                P.dma(wu_b[s][:], w_gu[FC + f], w=["wu%d" % s])
                pa = nxt("pa", 2)
                pb = nxt("pb", 2)
                for c in range(KC):
                    P.mm(pA[pa][:], wg_b[s][:, c, :], hn[:, c, :], start=(c == 0), stop=(c == KC - 1),
                         r=["wg%d" % s, "hn"], w=["pA%d" % pa])
                for c in range(KC):
                    P.mm(pB[pb][:], wu_b[s][:, c, :], hn[:, c, :], start=(c == 0), stop=(c == KC - 1),
                         r=["wu%d" % s, "hn"], w=["pB%d" % pb])
                g = nxt("sg", 2)
                P.op("act", lambda e, g=g, pa=pa: e.activation(sg[g][:], pA[pa][:], AF.Silu), r=["pA%d" % pa], w=["sg%d" % g])
                P.op("dve", lambda e, g=g, pb=pb, fi=fi: e.tensor_tensor(act[:, fi, :], sg[g][:], pB[pb][:], ALU.mult),
                     r=["sg%d" % g, "pB%d" % pb], w=["actb"])
            for d in range(KC):
                s = nxt("wd", 2)
                P.dma(wd_b[s][:], w_dn[d][:, rd * FH:(rd + 1) * FH, :], w=["wd%d" % s])
                pa = nxt("pa", 2)
                for fi in range(FH):
                    P.mm(pA[pa][:], wd_b[s][:, fi, :], act[:, fi, :], start=(fi == 0), stop=(fi == FH - 1),
                         r=["wd%d" % s, "actb"], w=["pA%d" % pa])
                P.op("dve", lambda e, d=d, pa=pa: e.tensor_tensor(h[:, d, :], h[:, d, :], pA[pa][:], ALU.add),
                     r=["h", "pA%d" % pa], w=["h"])
        P.dma(ptile[:], pT[:, ts].rearrange("(c p) n -> p c n", p=128), w=["ptile"])
        for d in range(KC):
            s = nxt("wpp", 2)
            P.dma(wpp_b[s][:], w_pp[d], w=["wpp%d" % s])
            pa = nxt("pa", 2)
            for c in range(2):
                P.mm(pA[pa][:], wpp_b[s][:, c, :], ptile[:, c, :], start=(c == 0), stop=(c == 1),
                     r=["wpp%d" % s, "ptile"], w=["pA%d" % pa])
            g = nxt("sg", 2)
            P.op("act", lambda e, g=g, pa=pa: e.activation(sg[g][:], pA[pa][:], AF.Square), r=["pA%d" % pa], w=["sg%d" % g])
            P.mm(NH.ps[:], NH.ones[:], sg[g][:], start=(d == 0), stop=(d == KC - 1), r=["sg%d" % g, "ones"], w=["nps"])
        rsqrt_from(P, rs2[:], "rs2", NH.ps[:], "nps", D_MODEL, NORM_EPS, 1.0)
        NH.rstd([(h[:, c, :], "h", 128) for c in range(KC)], D_MODEL, rs[:], "rs")
        for c in range(KC):
            P.op("dve", lambda e, c=c: e.scalar_tensor_tensor(hn[:, c, :], h[:, c, :], vec[:, 2 * KC + c:2 * KC + c + 1], rs[:], ALU.mult, ALU.mult),
                 r=["h", "vec", "rs"], w=["hn"])
        for d in range(KC):
            s = nxt("wg", 2)
            P.dma(wg_b[s][:], w_pg[d], w=["wg%d" % s])
            s2 = nxt("wpp", 2)
            P.dma(wpp_b[s2][:], w_pp[d], w=["wpp%d" % s2])
            pa = nxt("pa", 2)
            pb = nxt("pb", 2)
            for c in range(KC):
                P.mm(pA[pa][:], wg_b[s][:, c, :], hn[:, c, :], start=(c == 0), stop=(c == KC - 1),
                     r=["wg%d" % s, "hn"], w=["pA%d" % pa])
            for c in range(2):
                P.mm(pB[pb][:], wpp_b[s2][:, c, :], ptile[:, c, :], start=(c == 0), stop=(c == 1),
                     r=["wpp%d" % s2, "ptile"], w=["pB%d" % pb])
            g = nxt("sg", 2)
            P.op("act", lambda e, g=g, pa=pa: e.activation(sg[g][:], pA[pa][:], AF.Sigmoid), r=["pA%d" % pa], w=["sg%d" % g])
            tq = nxt("tmp", 2)
            P.op("dve", lambda e, tq=tq, pb=pb, d=d: e.scalar_tensor_tensor(tmp[tq][:], pB[pb][:], vec[:, KC + d:KC + d + 1], rs2[:], ALU.mult, ALU.mult),
                 r=["pB%d" % pb, "vec", "rs2"], w=["tmp%d" % tq])
            P.op("pool", lambda e, tq=tq, g=g: e.tensor_tensor(tmp[tq][:], tmp[tq][:], sg[g][:], ALU.mult),
                 r=["tmp%d" % tq, "sg%d" % g], w=["tmp%d" % tq])
            P.op("dve", lambda e, tq=tq, d=d: e.tensor_tensor(h[:, d, :], h[:, d, :], tmp[tq][:], ALU.add),
                 r=["h", "tmp%d" % tq], w=["h"])
        P.dma(hout[:, ts].rearrange("(c p) n -> p c n", p=128), h[:], r=["h"], is_output=True)
    return P.emit()


def rope_consts():
    inv = ROPE_THETA ** (-np.arange(0, MLA_ROPE, 2, dtype=np.float32) / np.float32(MLA_ROPE))
    inv = inv.astype(np.float32)
    invf = np.concatenate([inv, inv]).reshape(64, 1).astype(np.float32)
    rot = np.zeros((64, 64), np.float32)
    for m in range(32):
        rot[m + 32, m] = -1.0
        rot[m, m + 32] = 1.0
    return invf, rot


def build_mla_pre(NT):
    T = 512 if NT >= 512 else NT
    NTILES = NT // T
    KC = 16
    H = MLA_HEADS
    P = Prog()
    hT = P.dram_in("hT", [D_MODEL, NT])
    posb = P.dram_in("posb", [64, NT], I32)
    w_in = P.dram_in("w_in", [8, 128, KC, 128])
    w_inpe = P.dram_in("w_inpe", [1, 128, KC, 64])
    w_qn = P.dram_in("w_qn", [H, 128, 4, 128])
    w_qr = P.dram_in("w_qr", [H, 128, 4, 64])
    w_kn = P.dram_in("w_kn", [H, 128, 4, 128])
    w_v = P.dram_in("w_v", [H, 128, 4, 128])
    vecs = P.dram_in("vecs", [128, KC + 4 + 4 + 4])
    invf_d = P.dram_in("invf", [64, 1])
    rot_d = P.dram_in("rot", [64, 64])
    q_nope_o = P.dram_out("q_nope", [H, 128, NT])
    q_rope_o = P.dram_out("q_rope", [H, 64, NT])
    k_nope_o = P.dram_out("k_nope", [H, 128, NT])
    k_rope_o = P.dram_out("k_rope", [H, 64, NT])
    v_o = P.dram_out("vT", [H, 128, NT])

    h = P.sbuf("h", [128, KC, T])
    clat = P.sbuf("clat", [128, 8, T])
    kpe = P.sbuf("kpe", [64, T])
    vec = P.sbuf("vec", [128, KC + 12])
    invf = P.sbuf("invf_s", [64, 1])
    rot = P.sbuf("rot_s", [64, 64])
    posi = P.sbuf("posi", [64, T], I32)
    posf = P.sbuf("posf", [64, T])
    u0 = P.sbuf("u0", [64, T])
    u1 = P.sbuf("u1", [64, T])
    sin_t = P.sbuf("sin_t", [64, T])
    cos_t = P.sbuf("cos_t", [64, T])
    rs = P.sbuf("rs", [128, T])
    wb = [P.sbuf("wb%d" % i, [128, KC, 128]) for i in range(2)]
    wpe = P.sbuf("wpe", [128, KC, 64])
    wh = {k: [P.sbuf("w%s%d" % (k, i), [128, 4, 128 if k != "qr" else 64]) for i in range(2)] for k in ("qn", "qr", "kn", "v")}
    xn = [P.sbuf("xn%d" % i, [128, T]) for i in range(2)]
    xr = [P.sbuf("xr%d" % i, [64, T]) for i in range(2)]
    on = [P.sbuf("on%d" % i, [128, T]) for i in range(3)]
    orr = [P.sbuf("or%d" % i, [64, T]) for i in range(2)]
    t64 = [P.sbuf("t64_%d" % i, [64, T]) for i in range(2)]
    pA = [P.psum("pA%d" % i, [128, T]) for i in range(2)]
    pB = [P.psum("pB%d" % i, [128, T]) for i in range(2)]
    pR = P.psum("pR", [64, T])
    NH = NormHelper(P, T)
    cnt = {}

    def nxt(k, n):
        v = cnt.get(k, 0) % n
        cnt[k] = cnt.get(k, 0) + 1
        return v

    P.dma(vec[:], vecs[:, :], w=["vec"])
    P.dma(invf[:], invf_d[:, :], w=["invf"])
    P.dma(rot[:], rot_d[:, :], w=["rot"])
    C_QN, C_KN, C_QR, C_KR = KC + 8, KC + 9, KC + 10, KC + 11

    def rope(src, src_key, dst, dst_key):
        P.mm(pR[:], rot[:], src, True, True, r=[src_key, "rot"], w=["pR"])
        tq = nxt("t64", 2)
        P.op("dve", lambda e: e.tensor_tensor(t64[tq][:], pR[:], sin_t[:], ALU.mult), r=["pR", "sin_t"], w=["t64_%d" % tq])
        P.op("pool", lambda e: e.tensor_tensor(dst, src, cos_t[:], ALU.mult), r=[src_key, "cos_t"], w=[dst_key])
        P.op("pool", lambda e: e.tensor_tensor(dst, dst, t64[tq][:], ALU.add), r=[dst_key, "t64_%d" % tq], w=[dst_key])

    for t in range(NTILES):
        ts = slice(t * T, (t + 1) * T)
        P.dma(h[:], hT[:, ts].rearrange("(c p) n -> p c n", p=128), w=["h"])
        P.dma(posi[:], posb[:, ts], w=["posi"])
        P.op("dve", lambda e: e.tensor_copy(posf[:], posi[:]), r=["posi"], w=["posf"])
        P.op("dve", lambda e: e.tensor_scalar(posf[:], posf[:], invf[:, 0:1], None, ALU.mult), r=["posf", "invf"], w=["posf"])
        P.op("dve", lambda e: e.tensor_scalar(u1[:], posf[:], 1.0 / (2.0 * math.pi), None, ALU.mult), r=["posf"], w=["u1"])
        P.op("dve", lambda e: e.tensor_copy(posi[:], u1[:]), r=["u1"], w=["posi"])
        P.op("dve", lambda e: e.tensor_copy(u1[:], posi[:]), r=["posi"], w=["u1"])
        P.op("dve", lambda e: e.scalar_tensor_tensor(u0[:], u1[:], -TWO_PI_HI, posf[:], ALU.mult, ALU.add), r=["u1", "posf"], w=["u0"])
        P.op("dve", lambda e: e.scalar_tensor_tensor(u0[:], u1[:], -TWO_PI_LO, u0[:], ALU.mult, ALU.add), r=["u1", "u0"], w=["u0"])

        def wrap(buf, key):
            P.op("dve", lambda e: e.tensor_scalar(u1[:], buf, math.pi, -2.0 * math.pi, ALU.is_ge, ALU.mult), r=[key], w=["u1"])
            P.op("dve", lambda e: e.tensor_tensor(buf, buf, u1[:], ALU.add), r=[key, "u1"], w=[key])
            P.op("dve", lambda e: e.tensor_scalar(u1[:], buf, -math.pi, 2.0 * math.pi, ALU.is_lt, ALU.mult), r=[key], w=["u1"])
            P.op("dve", lambda e: e.tensor_tensor(buf, buf, u1[:], ALU.add), r=[key, "u1"], w=[key])
            P.op("dve", lambda e: e.tensor_scalar(buf, buf, PI_SAFE, -PI_SAFE, ALU.min, ALU.max), r=[key], w=[key])

        wrap(u0[:], "u0")
        P.op("act", lambda e: e.activation(sin_t[:], u0[:], AF.Sin), r=["u0"], w=["sin_t"])
        P.op("dve", lambda e: e.tensor_scalar(u0[:], u0[:], 0.5 * math.pi, None, ALU.add), r=["u0"], w=["u0"])
        wrap(u0[:], "u0")
        P.op("act", lambda e: e.activation(cos_t[:], u0[:], AF.Sin), r=["u0"], w=["cos_t"])
        NH.rstd([(h[:, c, :], "h", 128) for c in range(KC)], D_MODEL, rs[:], "rs")
        for c in range(KC):
            P.op("dve", lambda e, c=c: e.scalar_tensor_tensor(h[:, c, :], h[:, c, :], vec[:, c:c + 1], rs[:], ALU.mult, ALU.mult),
                 r=["h", "vec", "rs"], w=["h"])
        for m in range(8):
            s = nxt("wb", 2)
            P.dma(wb[s][:], w_in[m], w=["wb%d" % s])
            pa = nxt("pa", 2)
            for c in range(KC):
                P.mm(pA[pa][:], wb[s][:, c, :], h[:, c, :], c == 0, c == KC - 1, r=["wb%d" % s, "h"], w=["pA%d" % pa])
            eng = "act" if m % 2 == 0 else "dve"
            if eng == "act":
                P.op("act", lambda e, m=m, pa=pa: e.copy(clat[:, m, :], pA[pa][:]), r=["pA%d" % pa], w=["clat"])
            else:
                P.op("dve", lambda e, m=m, pa=pa: e.tensor_copy(clat[:, m, :], pA[pa][:]), r=["pA%d" % pa], w=["clat"])
        P.dma(wpe[:], w_inpe[0], w=["wpe"])
        pb = nxt("pb", 2)
        for c in range(KC):
            P.mm(pB[pb][0:64, :], wpe[:, c, :], h[:, c, :], c == 0, c == KC - 1, r=["wpe", "h"], w=["pB%d" % pb])
        P.op("act", lambda e, pb=pb: e.copy(kpe[:], pB[pb][0:64, :]), r=["pB%d" % pb], w=["kpe"])
        for base, col in ((0, KC), (4, KC + 4)):
            NH.rstd([(clat[:, base + c, :], "clat", 128) for c in range(4)], 512, rs[:], "rs")
            for c in range(4):
                P.op("dve", lambda e, c=c, base=base, col=col: e.scalar_tensor_tensor(
                    clat[:, base + c, :], clat[:, base + c, :], vec[:, col + c:col + c + 1], rs[:], ALU.mult, ALU.mult),
                    r=["clat", "vec", "rs"], w=["clat"])
        for hd in range(H):
            s = nxt("wh", 2)
            for k, wd in (("qn", w_qn), ("qr", w_qr), ("kn", w_kn), ("v", w_v)):
                P.dma(wh[k][s][:], wd[hd], w=["w%s%d" % (k, s)])
            pa = nxt("pa", 2)
            pb = nxt("pb", 2)
            for c in range(4):
                P.mm(pA[pa][:], wh["qn"][s][:, c, :], clat[:, c, :], c == 0, c == 3, r=["wqn%d" % s, "clat"], w=["pA%d" % pa])
            for c in range(4):
                P.mm(pB[pb][0:64, :], wh["qr"][s][:, c, :], clat[:, c, :], c == 0, c == 3, r=["wqr%d" % s, "clat"], w=["pB%d" % pb])
            a = nxt("xn", 2)
            b = nxt("xr", 2)
            P.op("act", lambda e, a=a, pa=pa: e.copy(xn[a][:], pA[pa][:]), r=["pA%d" % pa], w=["xn%d" % a])
            P.op("dve", lambda e, b=b, pb=pb: e.tensor_copy(xr[b][:], pB[pb][0:64, :]), r=["pB%d" % pb], w=["xr%d" % b])
            NH.rstd([(xn[a][:], "xn%d" % a, 128), (xr[b][:], "xr%d" % b, 64)], MLA_QK, rs[:], "rs")
            o = nxt("on", 3)
            P.op("dve", lambda e, a=a, o=o: e.scalar_tensor_tensor(on[o][:], xn[a][:], vec[:, C_QN:C_QN + 1], rs[:], ALU.mult, ALU.mult),
                 r=["xn%d" % a, "vec", "rs"], w=["on%d" % o])
            P.dma(q_nope_o[hd][:, ts], on[o][:], r=["on%d" % o], is_output=True, eng="pool")
            P.op("dve", lambda e, b=b: e.scalar_tensor_tensor(xr[b][:], xr[b][:], vec[0:64, C_QR:C_QR + 1], rs[0:64, :], ALU.mult, ALU.mult),
                 r=["xr%d" % b, "vec", "rs"], w=["xr%d" % b])
            ro = nxt("or", 2)
            rope(xr[b][:], "xr%d" % b, orr[ro][:], "or%d" % ro)
            P.dma(q_rope_o[hd][:, ts], orr[ro][:], r=["or%d" % ro], is_output=True, eng="pool")
            pa = nxt("pa", 2)
            for c in range(4):
                P.mm(pA[pa][:], wh["kn"][s][:, c, :], clat[:, 4 + c, :], c == 0, c == 3, r=["wkn%d" % s, "clat"], w=["pA%d" % pa])
            a = nxt("xn", 2)
            P.op("act", lambda e, a=a, pa=pa: e.copy(xn[a][:], pA[pa][:]), r=["pA%d" % pa], w=["xn%d" % a])
            NH.rstd([(xn[a][:], "xn%d" % a, 128), (kpe[:], "kpe", 64)], MLA_QK, rs[:], "rs")
            o = nxt("on", 3)
            P.op("dve", lambda e, a=a, o=o: e.scalar_tensor_tensor(on[o][:], xn[a][:], vec[:, C_KN:C_KN + 1], rs[:], ALU.mult, ALU.mult),
                 r=["xn%d" % a, "vec", "rs"], w=["on%d" % o])
            P.dma(k_nope_o[hd][:, ts], on[o][:], r=["on%d" % o], is_output=True, eng="pool")
            b = nxt("xr", 2)
            P.op("dve", lambda e, b=b: e.scalar_tensor_tensor(xr[b][:], kpe[:], vec[0:64, C_KR:C_KR + 1], rs[0:64, :], ALU.mult, ALU.mult),
                 r=["kpe", "vec", "rs"], w=["xr%d" % b])
            ro = nxt("or", 2)
            rope(xr[b][:], "xr%d" % b, orr[ro][:], "or%d" % ro)
            P.dma(k_rope_o[hd][:, ts], orr[ro][:], r=["or%d" % ro], is_output=True, eng="pool")
            pa = nxt("pa", 2)
            for c in range(4):
                P.mm(pA[pa][:], wh["v"][s][:, c, :], clat[:, 4 + c, :], c == 0, c == 3, r=["wv%d" % s, "clat"], w=["pA%d" % pa])
            o = nxt("on", 3)
            P.op("act", lambda e, o=o, pa=pa: e.copy(on[o][:], pA[pa][:]), r=["pA%d" % pa], w=["on%d" % o])
            P.dma(v_o[hd][:, ts], on[o][:], r=["on%d" % o], is_output=True, eng="pool")
    return P.emit()


def mla_pre_inputs(hT, posb, lw):
    H = MLA_HEADS
    w_in = lw["w_in"]
    w_uq = lw["w_uq"].reshape(512, H, MLA_QK)
    w_ukv = lw["w_ukv"].reshape(512, H, 256)
    invf, rot = rope_consts()
    vecs = np.zeros((128, 28), np.float32)
    vecs[:, 0:16] = col_vec(lw["mixer_norm"])
    vecs[:, 16:20] = col_vec(lw["q_lat_norm"])
    vecs[:, 20:24] = col_vec(lw["kv_lat_norm"])
    vecs[:, 24] = lw["q_norm"][:128]
    vecs[:, 25] = lw["k_norm"][:128]
    vecs[0:64, 26] = lw["q_norm"][128:]
    vecs[0:64, 27] = lw["k_norm"][128:]
    return {
        "w_in": tile_w(w_in[:, :1024]), "w_inpe": tile_w(np.ascontiguousarray(w_in[:, 1024:1088]), 64),
        "w_qn": tile_w(np.ascontiguousarray(w_uq[:, :, :128]).reshape(512, H * 128)),
        "w_qr": tile_w(np.ascontiguousarray(w_uq[:, :, 128:]).reshape(512, H * 64), 64),
        "w_kn": tile_w(np.ascontiguousarray(w_ukv[:, :, :128]).reshape(512, H * 128)),
        "w_v": tile_w(np.ascontiguousarray(w_ukv[:, :, 128:]).reshape(512, H * 128)),
        "vecs": vecs, "invf": invf, "rot": rot,
    }


def attn_masks(T=512):
    m = np.zeros((T // 128, 128, T), np.float32)
    q = np.arange(T)[None, :]
    for kb in range(T // 128):
        k = (kb * 128 + np.arange(128))[:, None]
        m[kb] = (q >= k).astype(np.float32)
    return m


def build_attn(S, NP):
    T = 512
    NQ = S // T
    NB = T // 128
    scale = MLA_QK ** -0.5
    P = Prog()
    qn_d = P.dram_in("qn", [NP, 128, S])
    qr_d = P.dram_in("qr", [NP, 64, S])
    kn_d = P.dram_in("kn", [NP, 128, S])
    kr_d = P.dram_in("kr", [NP, 64, S])
    v_d = P.dram_in("v", [NP, 128, S // 128, 128])
    mk_d = P.dram_in("masks", [NB, 128, T])
    o_d = P.dram_out("oT", [NP, 128, S])
    qn = [P.sbuf("qn%d" % i, [128, T]) for i in range(2)]
    qr = [P.sbuf("qr%d" % i, [64, T]) for i in range(2)]
    kn = [P.sbuf("kn%d" % i, [128, T]) for i in range(3)]
    kr = [P.sbuf("kr%d" % i, [64, T]) for i in range(3)]
    vb = [P.sbuf("vb%d" % i, [128, NB, 128]) for i in range(3)]
    mk = P.sbuf("mk", [128, NB, T])
    ones = P.sbuf("ones", [128, 128])
    pt = [P.sbuf("pt%d" % i, [128, T]) for i in range(3)]
    rl = P.sbuf("rl", [128, T])
    pacc = [P.sbuf("pacc%d" % i, [128, T]) for i in range(2)]
    ot = [P.sbuf("ot%d" % i, [128, T]) for i in range(2)]
    sp = [P.psum("sp%d" % i, [128, T]) for i in range(2)]
    op_ = [P.psum("op%d" % i, [128, T]) for i in range(2)]
    lp = [P.psum("lp%d" % i, [128, T]) for i in range(2)]
    P.op("pool", lambda e: e.memset(ones[:], 1.0), w=["ones"])
    for kb in range(NB):
        P.dma(mk[:, kb, :], mk_d[kb], w=["mk"])
    cnt = {}

    def nxt(k, n):
        v = cnt.get(k, 0) % n
        cnt[k] = cnt.get(k, 0) + 1
        return v

    for pr in range(NP):
        for j in range(NQ):
            qs = nxt("q", 2)
            tsq = slice(j * T, (j + 1) * T)
            P.dma(qn[qs][:], qn_d[pr][:, tsq], w=["qn%d" % qs])
            P.dma(qr[qs][:], qr_d[pr][:, tsq], w=["qr%d" % qs])
            acc = nxt("acc", 2)
            blocks = [(c, kb) for c in range(j + 1) for kb in range(NB)]
            kslot = {}
            spslot = {}

            def emit_qk(idx, pr=pr, j=j, qs=qs, kslot=kslot, spslot=spslot, blocks=blocks):
                c, kb = blocks[idx]
                if kb == 0:
                    ks = nxt("k", 3)
                    kslot[c] = ks
                    tsk = slice(c * T, (c + 1) * T)
                    P.dma(kn[ks][:], kn_d[pr][:, tsk], w=["kn%d" % ks])
                    P.dma(kr[ks][:], kr_d[pr][:, tsk], w=["kr%d" % ks])
                    P.dma(vb[ks][:], v_d[pr][:, c * NB:(c + 1) * NB, :], w=["vb%d" % ks])
                ks = kslot[c]
                s = nxt("sp", 2)
                spslot[idx] = s
                ksl = slice(kb * 128, (kb + 1) * 128)
                P.mm(sp[s][:], kn[ks][:, ksl], qn[qs][:], True, False, r=["kn%d" % ks, "qn%d" % qs], w=["sp%d" % s])
                P.mm(sp[s][:], kr[ks][:, ksl], qr[qs][:], False, True, r=["kr%d" % ks, "qr%d" % qs], w=["sp%d" % s])

            def emit_rest(idx, j=j, acc=acc, kslot=kslot, spslot=spslot, blocks=blocks):
                c, kb = blocks[idx]
                ks = kslot[c]
                s = spslot[idx]
                p = nxt("pt", 3)
                P.op("act", lambda e, p=p, s=s: e.activation(pt[p][:], sp[s][:], AF.Exp, scale=scale), r=["sp%d" % s], w=["pt%d" % p])
                if c == j:
                    P.op("pool", lambda e, p=p, kb=kb: e.tensor_tensor(pt[p][:], pt[p][:], mk[:, kb, :], ALU.mult),
                         r=["pt%d" % p, "mk"], w=["pt%d" % p])
                first = (idx == 0)
                last = (idx == len(blocks) - 1)
                P.mm(op_[acc][:], vb[ks][:, kb, :], pt[p][:], first, last, r=["vb%d" % ks, "pt%d" % p], w=["op%d" % acc])
                if first:
                    P.op("pool", lambda e, p=p, acc=acc: e.tensor_copy(pacc[acc][:], pt[p][:]), r=["pt%d" % p], w=["pacc%d" % acc])
                else:
                    P.op("pool", lambda e, p=p, acc=acc: e.tensor_tensor(pacc[acc][:], pacc[acc][:], pt[p][:], ALU.add),
                         r=["pacc%d" % acc, "pt%d" % p], w=["pacc%d" % acc])

            emit_qk(0)
            for idx in range(len(blocks)):
                if idx + 1 < len(blocks):
                    emit_qk(idx + 1)
                emit_rest(idx)
            P.mm(lp[acc][:], ones[:], pacc[acc][:], True, True, r=["ones", "pacc%d" % acc], w=["lp%d" % acc])
            P.op("dve", lambda e, acc=acc: e.reciprocal(rl[:], lp[acc][:]), r=["lp%d" % acc], w=["rl"])
            o = nxt("ot", 2)
            P.op("dve", lambda e, acc=acc, o=o: e.tensor_tensor(ot[o][:], op_[acc][:], rl[:], ALU.mult), r=["op%d" % acc, "rl"], w=["ot%d" % o])
            P.dma(o_d[pr][:, tsq], ot[o][:], r=["ot%d" % o], is_output=True)
    return P.emit()


class Rot:
    def __init__(self, P, name, shape, n, psum=False, dtype=F32):
        self.bufs = [(P.psum if psum else P.sbuf)("%s%d" % (name, i), shape, dtype) for i in range(n)]
        self.keys = ["%s%d" % (name, i) for i in range(n)]
        self.i = 0

    def next(self):
        k = self.i % len(self.bufs)
        self.i += 1
        return self.bufs[k], self.keys[k]


def build_gdn_pre(NT):
    T = 512 if NT >= 512 else NT
    NTILES = NT // T
    KC = 16
    NQK = 64
    P = Prog()
    hT = P.dram_in("hT", [D_MODEL, 3 + NT])
    w_in = P.dram_in("w_in", [96, 128, KC, 128])
    w_ba = P.dram_in("w_ba", [1, 128, KC, 64])
    vecs = P.dram_in("vecs", [128, KC])
    cw_d = P.dram_in("cw", [128, NQK * 4])
    hp_d = P.dram_in("hp", [64, 2])
    qT_o = P.dram_out("qT", [2048, NT])
    kT_o = P.dram_out("kT", [2048, NT])
    vT_o = P.dram_out("vT", [4096, NT])
    zT_o = P.dram_out("zT", [4096, NT])
    bg_o = P.dram_out("bg", [64, NT])

    h = P.sbuf("h", [128, KC, T])
    hh = P.sbuf("hh", [128, KC, 3])
    vec = P.sbuf("vec", [128, KC])
    cw = P.sbuf("cw_s", [128, NQK * 4])
    hp = P.sbuf("hp_s", [64, 2])
    nal = P.sbuf("nal", [64, 1])
    carry = P.sbuf("carry", [128, NQK, 3])
    rs = P.sbuf("rs", [128, T])
    rsh = P.sbuf("rsh", [128, 3])
    wb = Rot(P, "wb", [128, KC, 128], 2)
    wba = P.sbuf("wba", [128, KC, 64])
    xc = Rot(P, "xc", [128, T + 3], 2)
    yb = Rot(P, "yb", [128, T], 2)
    ob = Rot(P, "ob", [128, T], 3)
    sb64 = Rot(P, "sb64_", [64, T], 3)
    bgt = P.sbuf("bgt", [64, T])
    pA = Rot(P, "pA", [128, T], 3, psum=True)
    pH = P.psum("pH", [128, 512])
    NH = NormHelper(P, T)

    P.dma(vec[:], vecs[:, :], w=["vec"])
    P.dma(cw[:], cw_d[:, :], w=["cw_s"])
    P.dma(hp[:], hp_d[:, :], w=["hp_s"])
    P.op("act", lambda e: e.activation(nal[32:64, :], hp[32:64, 1:2], AF.Exp), r=["hp_s"], w=["nal"])
    P.op("dve", lambda e: e.tensor_scalar(nal[32:64, :], nal[32:64, :], -1.0, None, ALU.mult), r=["nal"], w=["nal"])

    P.dma(hh[:], hT[:, 0:3].rearrange("(c p) n -> p c n", p=128), w=["hh"])
    sqh = P.sbuf("sqh", [128, 3])
    for c in range(KC):
        P.op("dve", lambda e, c=c: e.tensor_tensor(sqh[:], hh[:, c, :], hh[:, c, :], ALU.mult), r=["hh"], w=["sqh"])
        P.mm(pH[:, 0:3], NH.ones[:], sqh[:], c == 0, c == KC - 1, r=["sqh", "ones"], w=["pH"])
    rsqrt_from(P, rsh[:], "rsh", pH[:, 0:3], "pH", D_MODEL, NORM_EPS)
    for c in range(KC):
        P.op("dve", lambda e, c=c: e.scalar_tensor_tensor(hh[:, c, :], hh[:, c, :], vec[:, c:c + 1], rsh[:], ALU.mult, ALU.mult),
             r=["hh", "vec", "rsh"], w=["hh"])
    for m in range(NQK):
        w, wk = wb.next()
        P.dma(w[:], w_in[m], w=[wk])
        for c in range(KC):
            P.mm(pH[:, 0:3], w[:, c, :], hh[:, c, :], c == 0, c == KC - 1, r=[wk, "hh"], w=["pH"])
        P.op("act", lambda e, m=m: e.copy(carry[:, m, :], pH[:, 0:3]), r=["pH"], w=["carry"])

    for t in range(NTILES):
        ts = slice(t * T, (t + 1) * T)
        P.dma(h[:], hT[:, 3 + t * T:3 + (t + 1) * T].rearrange("(c p) n -> p c n", p=128), w=["h"])
        NH.rstd([(h[:, c, :], "h", 128) for c in range(KC)], D_MODEL, rs[:], "rs")
        for c in range(KC):
            P.op("dve", lambda e, c=c: e.scalar_tensor_tensor(h[:, c, :], h[:, c, :], vec[:, c:c + 1], rs[:], ALU.mult, ALU.mult),
                 r=["h", "vec", "rs"], w=["h"])
        for m in range(96):
            w, wk = wb.next()
            P.dma(w[:], w_in[m], w=[wk])
            ps, pk = pA.next()
            for c in range(KC):
                P.mm(ps[:], w[:, c, :], h[:, c, :], c == 0, c == KC - 1, r=[wk, "h"], w=[pk])
            if m >= NQK:
                o, ok = ob.next()
                P.op("act", lambda e, o=o, ps=ps: e.copy(o[:], ps[:]), r=[pk], w=[ok])
                P.dma(zT_o[(m - NQK) * 128:(m - NQK + 1) * 128, ts], o[:], r=[ok], is_output=True, eng="pool")
                continue
            x, xk = xc.next()
            P.op("act", lambda e, x=x, ps=ps: e.copy(x[:, 3:3 + T], ps[:]), r=[pk], w=[xk])
            P.op("pool", lambda e, x=x, m=m: e.tensor_copy(x[:, 0:3], carry[:, m, :]), r=["carry"], w=[xk])
            P.op("pool", lambda e, x=x, m=m: e.tensor_copy(carry[:, m, :], x[:, T:T + 3]), r=[xk], w=["carry"])
            y, yk = yb.next()
            P.op("dve", lambda e, x=x, y=y, m=m: e.tensor_scalar(y[:], x[:, 0:T], cw[:, m * 4:m * 4 + 1], None, ALU.mult), r=[xk, "cw_s"], w=[yk])
            for j in range(1, 4):
                P.op("dve", lambda e, x=x, y=y, m=m, j=j: e.scalar_tensor_tensor(y[:], x[:, j:j + T], cw[:, m * 4 + j:m * 4 + j + 1], y[:], ALU.mult, ALU.add),
                     r=[xk, "cw_s", yk], w=[yk])
            o, ok = ob.next()
            P.op("act", lambda e, o=o, y=y: e.activation(o[:], y[:], AF.Silu), r=[yk], w=[ok])
            if m < 32:
                NH.rstd([(o[:], ok, 128)], 1.0, rs[:], "rs", eps=NORM_EPS, post_scale=(GDN_DK ** -0.5 if m < 16 else 1.0))
                P.op("dve", lambda e, o=o: e.tensor_tensor(o[:], o[:], rs[:], ALU.mult), r=[ok, "rs"], w=[ok])
                dst = qT_o if m < 16 else kT_o
                mm_ = m % 16
            else:
                dst = vT_o
                mm_ = m - 32
            P.dma(dst[mm_ * 128:(mm_ + 1) * 128, ts], o[:], r=[ok], is_output=True, eng="pool")
        P.dma(wba[:], w_ba[0], w=["wba"])
        ps, pk = pA.next()
        for c in range(KC):
            P.mm(ps[0:64, :], wba[:, c, :], h[:, c, :], c == 0, c == KC - 1, r=["wba", "h"], w=[pk])
        P.op("act", lambda e, ps=ps: e.activation(bgt[0:32, :], ps[0:32, :], AF.Sigmoid), r=[pk], w=["bgt"])
        x1, k1 = sb64.next()
        x2, k2 = sb64.next()
        x3, k3 = sb64.next()
        P.op("dve", lambda e, ps=ps, x1=x1: e.tensor_scalar(x1[32:64, :], ps[32:64, :], hp[32:64, 0:1], None, ALU.add), r=[pk, "hp_s"], w=[k1])
        P.op("act", lambda e, x1=x1, x2=x2: e.activation(x2[32:64, :], x1[32:64, :], AF.Abs), r=[k1], w=[k2])
        P.op("act", lambda e, x2=x2: e.activation(x2[32:64, :], x2[32:64, :], AF.Exp, scale=-1.0), r=[k2], w=[k2])
        P.op("act", lambda e, x2=x2, x3=x3: e.activation(x3[32:64, :], x2[32:64, :], AF.Ln, bias=1.0), r=[k2], w=[k3])
        P.op("dve", lambda e, x1=x1, x3=x3: e.scalar_tensor_tensor(x3[32:64, :], x1[32:64, :], 0.0, x3[32:64, :], ALU.max, ALU.add), r=[k1, k3], w=[k3])
        P.op("dve", lambda e, x3=x3: e.tensor_scalar(bgt[32:64, :], x3[32:64, :], nal[32:64, 0:1], None, ALU.mult), r=[k3, "nal"], w=["bgt"])
        P.dma(bg_o[:, ts], bgt[:], r=["bgt"], is_output=True, eng="pool")
    return P.emit()


def gdn_pre_inputs(lw):
    w_in = lw["w_in"]
    cw = lw["conv_w"]
    cwt = np.ascontiguousarray(cw.T.reshape(64, 128, 4).transpose(1, 0, 2).reshape(128, 256))
    hp = np.zeros((64, 2), np.float32)
    hp[32:, 0] = lw["dt_bias"]
    hp[32:, 1] = lw["a_log"]
    return {"w_in": tile_w(np.ascontiguousarray(w_in[:, :12288])), "w_ba": tile_w(np.ascontiguousarray(w_in[:, 12288:12352]), 64),
            "vecs": col_vec(lw["mixer_norm"]), "cw": cwt, "hp": hp}


def gdn_consts():
    p = np.arange(128)[:, None]
    f = np.arange(128)[None, :]
    triu = (f >= p).astype(np.float32)
    stril = (p > f).astype(np.float32)
    ident = np.eye(128, dtype=np.float32)
    return np.ascontiguousarray(np.stack([triu, stril, ident]))


def build_gdn_core(S, NHD):
    C = 128
    NCH = S // C
    GRP = 4 if NCH % 4 == 0 else 1
    NQ = (NHD + 1) // 2
    P = Prog()
    QT = P.dram_in("QT", [NQ, 128, S])
    KT = P.dram_in("KT", [NQ, 128, S])
    Ktm = P.dram_in("Ktm", [NQ, 128, NCH, 128])
    Vtm = P.dram_in("Vtm", [NHD, 128, NCH, 128])
    Gd = P.dram_in("G", [NHD, 128, NCH])
    Bd = P.dram_in("Bt", [NHD, 128, NCH])
    cst = P.dram_in("cst", [3, 128, 128])
    o_d = P.dram_out("o", [NHD, 128, NCH, 128])

    triu = P.sbuf("triu", [128, 128])
    stril = P.sbuf("stril", [128, 128])
    ident = P.sbuf("ident", [128, 128])
    ones = P.sbuf("ones", [128, 128])
    P.dma(triu[:], cst[0], w=["triu"])
    P.dma(stril[:], cst[1], w=["stril"])
    P.dma(ident[:], cst[2], w=["ident"])
    P.op("pool", lambda e: e.memset(ones[:], 1.0), w=["ones"])
    Gh = Rot(P, "Gh", [128, NCH], 2)
    Bh = Rot(P, "Bh", [128, NCH], 2)
    kt4 = Rot(P, "kt4_", [128, GRP * 128], 2)
    qt4 = Rot(P, "qt4_", [128, GRP * 128], 2)
    km4 = Rot(P, "km4_", [128, GRP, 128], 2)
    vm4 = Rot(P, "vm4_", [128, GRP, 128], 2)
    ob4 = Rot(P, "ob4_", [128, GRP, 128], 2)
    St = P.sbuf("St", [128, 128])
    sq = lambda name, n: Rot(P, name, [128, 128], n)
    gbc, decL, decT, L0r, N0r, intr, Pr, Lr, Nr = (sq("gbc", 2), sq("decL", 2), sq("decT", 2), sq("L0r", 2), sq("N0r", 2),
                                                   sq("intr", 3), sq("Pr", 3), sq("Lr", 3), sq("Nr", 3))
    t12 = sq("t12_", 3)
    Vb, Kbg, Kdec, ub, wTb, vnew, o1 = sq("Vb", 2), sq("Kbg", 2), sq("Kdec", 3), sq("ub", 3), sq("wTb", 3), sq("vnew", 2), sq("o1_", 2)
    scl = Rot(P, "scl", [128, 8], 3)
    banks = [P.psum("bk%d" % i, [128, 512]) for i in range(8)]

    class PsRot:
        def __init__(self, idxs):
            self.idxs = idxs
            self.i = 0

        def next(self):
            k = self.idxs[self.i % len(self.idxs)]
            self.i += 1
            return banks[k], "bk%d" % k

    psA = PsRot([0, 1, 2, 3, 4])
    psB = PsRot([5, 6, 7])

    def copy_to(eng, dst, dk, src, sk):
        if eng == "act":
            P.op("act", lambda e: e.copy(dst, src), r=[sk], w=[dk])
        else:
            P.op("dve", lambda e: e.tensor_copy(dst, src), r=[sk], w=[dk])

    for hd in range(NHD):
        qk = hd // 2
        G_, Gk = Gh.next()
        B_, Bk = Bh.next()
        P.dma(G_[:], Gd[hd], w=[Gk])
        P.dma(B_[:], Bd[hd], w=[Bk])
        P.op("pool", lambda e: e.memset(St[:], 0.0), w=["St"])
        grp = {}

        def load_group(gi):
            k4, k4k = kt4.next()
            q4, q4k = qt4.next()
            m4, m4k = km4.next()
            v4, v4k = vm4.next()
            sl = slice(gi * GRP * 128, (gi + 1) * GRP * 128)
            P.dma(k4[:], KT[qk][:, sl], w=[k4k])
            P.dma(q4[:], QT[qk][:, sl], w=[q4k])
            P.dma(m4[:], Ktm[qk][:, gi * GRP:(gi + 1) * GRP, :], w=[m4k])
            P.dma(v4[:], Vtm[hd][:, gi * GRP:(gi + 1) * GRP, :], w=[v4k])
            grp[gi] = (k4, k4k, q4, q4k, m4, m4k, v4, v4k)

        def pre(n):
            gi, li = n // GRP, n % GRP
            if li == 0:
                load_group(gi)
            k4, k4k, q4, q4k, m4, m4k, v4, v4k = grp[gi]
            ktc = k4[:, li * 128:(li + 1) * 128]
            qtc = q4[:, li * 128:(li + 1) * 128]
            ktm = m4[:, li, :]
            vtm = v4[:, li, :]
            gcol = G_[:, n:n + 1]
            bcol = B_[:, n:n + 1]
            gb, gbk = gbc.next()
            P.op("dve", lambda e: e.tensor_scalar(gb[:], ones[:], gcol, None, ALU.mult), r=["ones", Gk], w=[gbk])
            pss, pssk = psA.next()
            P.mm(pss[:, 0:128], triu[:], gb[:], True, True, r=["triu", gbk], w=[pssk])
            P.mm(pss[:, 128:256], ones[:], gb[:], True, True, r=["ones", gbk], w=[pssk])
            psb_, psbk = psA.next()
            psb = psb_[:, 0:128]
            P.mm(psb, gb[:], triu[:], True, True, r=[gbk, "triu"], w=[psbk])
            sc, sck = scl.next()
            P.op("act", lambda e: e.copy(sc[:, 0:1], pss[:, 0:1]), r=[pssk], w=[sck])
            P.op("act", lambda e: e.copy(sc[:, 1:2], pss[:, 128:129]), r=[pssk], w=[sck])
            P.op("act", lambda e: e.activation(sc[:, 2:4], sc[:, 0:2], AF.Exp), r=[sck], w=[sck])
            P.op("dve", lambda e: e.tensor_tensor(sc[:, 4:5], sc[:, 1:2], sc[:, 0:1], ALU.subtract), r=[sck], w=[sck])
            P.op("act", lambda e: e.activation(sc[:, 4:5], sc[:, 4:5], AF.Exp), r=[sck], w=[sck])
            P.op("dve", lambda e: e.tensor_tensor(sc[:, 5:6], sc[:, 2:3], bcol, ALU.mult), r=[sck, Bk], w=[sck])
            t1, t1k = t12.next()
            dl, dlk = decL.next()
            P.op("dve", lambda e: e.tensor_scalar(t1[:], psb, sc[:, 0:1], 0.0, ALU.subtract, ALU.max), r=[psbk, sck], w=[t1k])
            P.op("act", lambda e: e.activation(dl[:], t1[:], AF.Exp, scale=-1.0), r=[t1k], w=[dlk])
            P.op("pool", lambda e: e.tensor_tensor(dl[:], dl[:], stril[:], ALU.mult), r=[dlk, "stril"], w=[dlk])
            t2, t2k = t12.next()
            dt_, dtk = decT.next()
            P.op("dve", lambda e: e.tensor_scalar(t2[:], psb, sc[:, 0:1], 0.0, ALU.subtract, ALU.min), r=[psbk, sck], w=[t2k])
            P.op("act", lambda e: e.activation(dt_[:], t2[:], AF.Exp), r=[t2k], w=[dtk])
            P.op("pool", lambda e: e.tensor_tensor(dt_[:], dt_[:], triu[:], ALU.mult), r=[dtk, "triu"], w=[dtk])
            psc, psck = psA.next()
            psc = psc[:, 0:128]
            P.mm(psc, ktc, ktc, True, True, r=[k4k], w=[psck])
            L0, L0k = L0r.next()
            P.op("dve", lambda e: e.scalar_tensor_tensor(L0[:], psc, bcol, dl[:], ALU.mult, ALU.mult), r=[psck, Bk, dlk], w=[L0k])
            psd, psdk = psA.next()
            psd = psd[:, 0:128]
            P.mm(psd, L0[:], ident[:], True, True, r=[L0k, "ident"], w=[psdk])
            N0, N0k = N0r.next()
            copy_to("act", N0[:], N0k, psd, psdk)
            pse, psek = psA.next()
            pse = pse[:, 0:128]
            P.mm(pse, ktc, qtc, True, True, r=[k4k, q4k], w=[psek])
            it, itk = intr.next()
            P.op("dve", lambda e: e.tensor_tensor(it[:], pse, dt_[:], ALU.mult), r=[psek, dtk], w=[itk])
            Pc, Pck = Pr.next()
            P.op("pool", lambda e, Pc=Pc: e.tensor_tensor(Pc[:], ident[:], N0[:], ALU.subtract), r=["ident", N0k], w=[Pck])
            Lp, Lpk, Np, Npk = L0, L0k, N0, N0k
            for lev in range(1, 7):
                psl, pslk = psA.next()
                psl = psl[:, 0:128]
                P.mm(psl, Np[:], Lp[:], True, True, r=[Npk, Lpk], w=[pslk])
                if lev < 6:
                    psn, psnk = psA.next()
                    psn = psn[:, 0:128]
                    P.mm(psn, Lp[:], Np[:], True, True, r=[Npk, Lpk], w=[psnk])
                Ln, Lnk = Lr.next()
                copy_to("dve" if lev % 2 else "act", Ln[:], Lnk, psl, pslk)
                if lev < 6:
                    Nn, Nnk = Nr.next()
                    copy_to("act" if lev % 2 else "dve", Nn[:], Nnk, psn, psnk)
                psp, pspk = psA.next()
                psp = psp[:, 0:128]
                P.mm(psp, Ln[:], Pc[:], True, True, r=[Lnk, Pck], w=[pspk])
                Pn, Pnk = Pr.next()
                P.op("dve", lambda e, Pn=Pn, Pc=Pc, psp=psp: e.tensor_tensor(Pn[:], Pc[:], psp, ALU.add), r=[Pck, pspk], w=[Pnk])
                Pc, Pck = Pn, Pnk
                Lp, Lpk = Ln, Lnk
                if lev < 6:
                    Np, Npk = Nn, Nnk
            vb_, vbk = Vb.next()
            kb_, kbk = Kbg.next()
            kd_, kdk = Kdec.next()
            P.op("act", lambda e: e.mul(vb_[:], vtm, bcol), r=[v4k, Bk], w=[vbk])
            P.op("act", lambda e: e.mul(kb_[:], ktm, sc[:, 5:6]), r=[m4k, sck], w=[kbk])
            P.op("pool", lambda e: e.tensor_scalar(kd_[:], ktm, sc[:, 4:5], None, ALU.mult), r=[m4k, sck], w=[kdk])
            psu, psuk = psA.next()
            psu = psu[:, 0:128]
            P.mm(psu, Pc[:], vb_[:], True, True, r=[Pck, vbk], w=[psuk])
            u_, uk = ub.next()
            copy_to("act", u_[:], uk, psu, psuk)
            psw, pswk = psA.next()
            psw = psw[:, 0:128]
            P.mm(psw, kb_[:], Pc[:], True, True, r=[kbk, Pck], w=[pswk])
            w_, wk = wTb.next()
            copy_to("dve", w_[:], wk, psw, pswk)
            return dict(u=u_, uk=uk, w=w_, wk=wk, it=it, itk=itk, kd=kd_, kdk=kdk, qtc=qtc, q4k=q4k, sc=sc, sck=sck)

        obuf = [None]

        def seq(n, d):
            gi, li = n // GRP, n % GRP
            if li == 0:
                obuf[0] = ob4.next()
            ob_, obk = obuf[0]
            sc = d["sc"]
            ps1, ps1k = psB.next()
            ps1 = ps1[:, 0:128]
            P.mm(ps1, d["w"][:], St[:], True, True, r=[d["wk"], "St"], w=[ps1k])
            vn, vnk = vnew.next()
            P.op("dve", lambda e: e.tensor_tensor(vn[:], d["u"][:], ps1, ALU.subtract), r=[d["uk"], ps1k], w=[vnk])
            ps2, ps2k = psB.next()
            ps2 = ps2[:, 0:128]
            P.mm(ps2, d["qtc"], St[:], True, True, r=[d["q4k"], "St"], w=[ps2k])
            o1_, o1k = o1.next()
            P.op("act", lambda e: e.mul(o1_[:], ps2, sc[:, 2:3]), r=[ps2k, d["sck"]], w=[o1k])
            ps3, ps3k = psB.next()
            ps3 = ps3[:, 0:128]
            P.mm(ps3, d["it"][:], vn[:], True, True, r=[d["itk"], vnk], w=[ps3k])
            P.op("dve", lambda e: e.tensor_tensor(ob_[:, li, :], o1_[:], ps3, ALU.add), r=[o1k, ps3k], w=[obk])
            ps4, ps4k = psB.next()
            ps4 = ps4[:, 0:128]
            P.mm(ps4, d["kd"][:], vn[:], True, True, r=[d["kdk"], vnk], w=[ps4k])
            P.op("dve", lambda e: e.scalar_tensor_tensor(St[:], St[:], sc[:, 3:4], ps4, ALU.mult, ALU.add), r=["St", d["sck"], ps4k], w=["St"])
            if li == GRP - 1:
                P.dma(o_d[hd][:, gi * GRP:(gi + 1) * GRP, :], ob_[:], r=[obk], is_output=True)

        prev = pre(0)
        for n in range(NCH):
            nxt_ = pre(n + 1) if n + 1 < NCH else None
            seq(n, prev)
            prev = nxt_
    return P.emit()


_PROGS = {}


def _prog(key, builder, *args):
    k = (key,) + tuple(args)
    if k not in _PROGS:
        _PROGS[k] = builder(*args)
    return _PROGS[k]


def _run(nc, in_maps):
    import sys
    import time
    t0 = time.time()
    res = run_bass_kernel_spmd(nc, in_maps, core_ids=list(range(NCORES)))
    print("[kernel] launch %.1fs" % (time.time() - t0), file=sys.stderr, flush=True)
    return res.results


def _c(a):
    return np.ascontiguousarray(a, dtype=np.float32)


def kernel(x, p, positions, mixer_norm,
           mla_w_in, mla_q_lat_norm, mla_kv_lat_norm, mla_w_uq, mla_w_ukv, mla_q_norm, mla_k_norm, mla_w_o,
           gdn_w_in, gdn_conv_w, gdn_a_log, gdn_dt_bias, gdn_out_norm, gdn_w_out,
           ffn_norm, ffn_w_gate_up, ffn_w_down,
           ple_w_proj, ple_norm, ple_gate_norm, ple_w_gate):
    x = np.asarray(x)
    p = np.asarray(p)
    positions = np.asarray(positions)
    B, S, D = x.shape
    CB = NCORES // B
    NT = S // CB
    NCH = S // 128
    cb = lambda c: (c // CB, slice((c % CB) * NT, (c % CB + 1) * NT))
    hT = []
    posb = []
    for c in range(NCORES):
        b, tok = cb(c)
        hT.append(_c(x[b, tok].T))
        posb.append(np.ascontiguousarray(np.broadcast_to(positions[b, tok][None, :], (64, NT)).astype(np.int32)))
    masks = attn_masks()
    gconst = gdn_consts()
    depth = mixer_norm.shape[0]
    for i in range(depth):
        j = i // 2
        if i % 2 == 0:
            lw = {"mixer_norm": np.asarray(mixer_norm[i]), "w_in": np.asarray(mla_w_in[j]), "q_lat_norm": np.asarray(mla_q_lat_norm[j]),
                  "kv_lat_norm": np.asarray(mla_kv_lat_norm[j]), "w_uq": np.asarray(mla_w_uq[j]), "w_ukv": np.asarray(mla_w_ukv[j]),
                  "q_norm": np.asarray(mla_q_norm[j]), "k_norm": np.asarray(mla_k_norm[j])}
            com = mla_pre_inputs(None, None, lw)
            res = _run(_prog("mla_pre", build_mla_pre, NT), [dict(com, hT=hT[c], posb=posb[c]) for c in range(NCORES)])
            H = MLA_HEADS
            full = {k: np.empty((B, H, d, S), np.float32) for k, d in (("q_nope", 128), ("q_rope", 64), ("k_nope", 128), ("k_rope", 64), ("vT", 128))}
            for c in range(NCORES):
                b, tok = cb(c)
                for k in full:
                    full[k][b][:, :, tok] = res[c][k]
            del res
            flat = {k: v.reshape(B * H, v.shape[2], S) for k, v in full.items()}
            vtm = _c(flat["vT"].transpose(0, 2, 1).reshape(B * H, NCH, 128, 128).transpose(0, 2, 1, 3))
            NP = B * H // NCORES
            ims = []
            for c in range(NCORES):
                sl = slice(c * NP, (c + 1) * NP)
                ims.append({"qn": _c(flat["q_nope"][sl]), "qr": _c(flat["q_rope"][sl]), "kn": _c(flat["k_nope"][sl]),
                            "kr": _c(flat["k_rope"][sl]), "v": _c(vtm[sl]), "masks": masks})
            res = _run(_prog("attn", build_attn, S, NP), ims)
            del ims, full, flat, vtm
            ofull = np.concatenate([res[c]["oT"] for c in range(NCORES)], axis=0).reshape(B, H * 128, S)
            del res
            oT = [_c(ofull[cb(c)[0]][:, cb(c)[1]]) for c in range(NCORES)]
            w_o = tile_w(np.asarray(mla_w_o[j]))
            DM = H * 128
            extra = [{} for _ in range(NCORES)]
            gdn = False
        else:
            lw = {"mixer_norm": np.asarray(mixer_norm[i]), "w_in": np.asarray(gdn_w_in[j]), "conv_w": np.asarray(gdn_conv_w[j]),
                  "a_log": np.asarray(gdn_a_log[j]), "dt_bias": np.asarray(gdn_dt_bias[j])}
            com = gdn_pre_inputs(lw)
            ims = []
            for c in range(NCORES):
                he = np.zeros((D, 3 + NT), np.float32)
                he[:, 3:] = hT[c]
                if c % CB != 0:
                    he[:, 0:3] = hT[c - 1][:, NT - 3:]
                ims.append(dict(com, hT=he))
            res = _run(_prog("gdn_pre", build_gdn_pre, NT), ims)
            del ims
            full = {k: np.empty((B, d, S), np.float32) for k, d in (("qT", 2048), ("kT", 2048), ("vT", 4096), ("bg", 64))}
            zT = []
            for c in range(NCORES):
                b, tok = cb(c)
                for k in full:
                    full[k][b][:, tok] = res[c][k]
                zT.append(_c(res[c]["zT"]))
            del res
            NHD = B * GDN_V_HEADS // NCORES
            NQ = NHD // 2
            tmaj = lambda a: _c(a.transpose(0, 2, 1).reshape(a.shape[0], NCH, 128, a.shape[1]).transpose(0, 2, 1, 3))
            ims = []
            for c in range(NCORES):
                b = c // CB
                v0 = (c % CB) * NHD
                q0 = v0 // 2
                QT = full["qT"][b].reshape(16, 128, S)[q0:q0 + NQ]
                KT = full["kT"][b].reshape(16, 128, S)[q0:q0 + NQ]
                VT = full["vT"][b].reshape(32, 128, S)[v0:v0 + NHD]
                Bt = full["bg"][b][v0:v0 + NHD]
                G = full["bg"][b][32 + v0:32 + v0 + NHD]
                ims.append({"QT": _c(QT), "KT": _c(KT), "Ktm": tmaj(KT), "Vtm": tmaj(VT),
                            "G": _c(G.reshape(NHD, NCH, 128).transpose(0, 2, 1)), "Bt": _c(Bt.reshape(NHD, NCH, 128).transpose(0, 2, 1)),
                            "cst": gconst})
            res = _run(_prog("gdn_core", build_gdn_core, S, NHD), ims)
            del ims, full
            ofull = np.empty((B, 32, 128, S), np.float32)
            for c in range(NCORES):
                b = c // CB
                v0 = (c % CB) * NHD
                o = res[c]["o"]
                ofull[b, v0:v0 + NHD] = o.transpose(0, 3, 2, 1).reshape(NHD, 128, S)
            del res
            ofull = ofull.reshape(B, 4096, S)
            oT = [_c(ofull[cb(c)[0]][:, cb(c)[1]]) for c in range(NCORES)]
            w_o = tile_w(np.asarray(gdn_w_out[j]))
            DM = 4096
            onorm = _c(np.asarray(gdn_out_norm[j]).reshape(128, 1))
            extra = [{"zT": zT[c], "onorm": onorm} for c in range(NCORES)]
            gdn = True
        del ofull
        com = {"w_o": w_o, "w_gu": tile_w(np.asarray(ffn_w_gate_up[i])), "w_dn": tile_w(np.asarray(ffn_w_down[i])),
               "w_pp": tile_w(np.asarray(ple_w_proj[i])), "w_pg": tile_w(np.asarray(ple_w_gate[i])),
               "vecs": _c(np.concatenate([col_vec(np.asarray(ffn_norm[i])), col_vec(np.asarray(ple_norm[i])),
                                          col_vec(np.asarray(ple_gate_norm[i]))], axis=1))}
        ims = []
        for c in range(NCORES):
            b, tok = cb(c)
            ims.append(dict(com, hT=hT[c], oT=oT[c], pT=_c(p[i][b, tok].T), **extra[c]))
        res = _run(_prog("post", build_post, NT, DM, gdn), ims)
        del ims, oT, com
        hT = [res[c]["hT_out"] for c in range(NCORES)]
        del res
    out = np.empty((B, S, D), np.float32)
    for c in range(NCORES):
        b, tok = cb(c)
        out[b, tok] = hT[c].T
    return out
```
